# Optimizing a Trainium2 kernel written in Bass

```python
import math
import jax, jax.numpy as jnp
from jax import lax
import numpy as np

D_MODEL = 1024
BATCH = 8
SEQ = 4096
DEPTH = 1

CHUNK = 64
Q_BLOCK = 128
MEM_TOKENS = 256
D_MIX = D_MODEL
RWKV_HEAD = 64
RWKV_WIDTH = D_MIX // 2
RWKV_HEADS = RWKV_WIDTH // RWKV_HEAD
DECAY_LORA = 64
AAA_LORA = 64
GATE_LORA = 128
RWKV_COLS = 3 * RWKV_WIDTH + DECAY_LORA + AAA_LORA + GATE_LORA
GN_EPS = 64e-5
DIFF_WIDTH = D_MIX - RWKV_WIDTH
DIFF_HEADS = 4
DIFF_VDIM = DIFF_WIDTH // DIFF_HEADS
DIFF_QKDIM = DIFF_VDIM // 2
DIFF_COLS = 3 * DIFF_WIDTH
D_IN_TOTAL = RWKV_COLS + DIFF_COLS
MEM_HEADS = 4
MEM_HEAD_DIM = D_MODEL // MEM_HEADS
D_FF = 4 * D_MODEL
RMS_EPS = 1e-5
NEG_INF = -1e30

kernel_name = "hymba_rwkv7_diffattn_memory_block"


def rms_norm(x, w, eps=RMS_EPS):
    xf = x.astype(jnp.float32)
    y = xf * lax.rsqrt(jnp.mean(jnp.square(xf), axis=-1, keepdims=True) + eps)
    return (y * w.astype(jnp.float32)).astype(x.dtype)


def rwkv7_scan(r, decay, k, v, kk, a):
    B, S, H, N = r.shape
    xs = tuple(jnp.moveaxis(t, 1, 0) for t in (r, decay, k, v, kk, a))

    def step(state, inp):
        r_t, w_t, k_t, v_t, kk_t, a_t = inp
        sa = jnp.einsum("bhvk,bhk->bhv", state, -kk_t)
        state = (state * w_t[:, :, None, :]
                 + sa[..., None] * (kk_t * a_t)[:, :, None, :]
                 + v_t[..., None] * k_t[:, :, None, :])
        y = jnp.einsum("bhvk,bhk->bhv", state, r_t)
        return state, y

    s0 = jnp.zeros((B, H, N, N), jnp.float32)
    _, ys = lax.scan(step, s0, xs)
    return jnp.moveaxis(ys, 0, 1)


def rwkv7_group(p, mu, w0, w_dec_up, a0, a_up, g_up, k_k, k_a, r_k, lnx_w, lnx_b):
    B, S, _ = p.shape
    p_prev = jnp.pad(p, ((0, 0), (1, 0), (0, 0)))[:, :-1]
    p = p + (p_prev - p) * mu
    r, k, v, wd, ad, gd = jnp.split(
        p, [RWKV_WIDTH, 2 * RWKV_WIDTH, 3 * RWKV_WIDTH,
            3 * RWKV_WIDTH + DECAY_LORA, 3 * RWKV_WIDTH + DECAY_LORA + AAA_LORA], axis=-1)
    f32 = jnp.float32
    w_log = -jax.nn.softplus(-(w0 + jnp.tanh(wd) @ w_dec_up)) - 0.5
    decay = jnp.exp(-jnp.exp(w_log.astype(f32)))
    a = jax.nn.sigmoid((a0 + ad @ a_up).astype(f32))
    g = jax.nn.sigmoid(gd) @ g_up
    heads = lambda t: t.astype(f32).reshape(B, S, RWKV_HEADS, RWKV_HEAD)
    kk = heads(k * k_k)
    kk = kk / jnp.maximum(jnp.sqrt(jnp.sum(jnp.square(kk), -1, keepdims=True)), 1e-12)
    k_mod = k.astype(f32) * (1.0 + (a - 1.0) * k_a)
    rh, kh, vh, ah, wh = heads(r), heads(k_mod), heads(v), heads(a), heads(decay)
    y = rwkv7_scan(rh, wh, kh, vh, kk, ah)
    mean = jnp.mean(y, -1, keepdims=True)
    var = jnp.mean(jnp.square(y - mean), -1, keepdims=True)
    y = ((y - mean) * lax.rsqrt(var + GN_EPS)).reshape(B, S, RWKV_WIDTH)
    y = y * lnx_w + lnx_b
    bonus = jnp.sum(rh * kh * r_k, -1, keepdims=True) * vh
    y = y + bonus.reshape(B, S, RWKV_WIDTH)
    return (y * g).astype(p.dtype)


def diff_attn_group(p, lam_q1, lam_k1, lam_q2, lam_k2, subln_w, lam_init):
    B, S, _ = p.shape
    q, k, v = jnp.split(p, [DIFF_WIDTH, 2 * DIFF_WIDTH], axis=-1)
    q = q.reshape(B, S, DIFF_HEADS, 2, DIFF_QKDIM)
    k = k.reshape(B, S, DIFF_HEADS, 2, DIFF_QKDIM)
    v = v.reshape(B, S, DIFF_HEADS, DIFF_VDIM)
    q1, q2 = q[..., 0, :], q[..., 1, :]
    k1, k2 = k[..., 0, :], k[..., 1, :]
    f32 = jnp.float32
    lam = (jnp.exp(jnp.sum(lam_q1 * lam_k1).astype(f32))
           - jnp.exp(jnp.sum(lam_q2 * lam_k2).astype(f32)) + lam_init)
    scale = DIFF_QKDIM ** -0.5
    outs = []
    for qb in range(S // Q_BLOCK):
        qs, qe = qb * Q_BLOCK, (qb + 1) * Q_BLOCK
        mask = (jnp.arange(qe) // CHUNK)[None, :] <= (jnp.arange(qs, qe) // CHUNK)[:, None]
        s1 = jnp.einsum("bqhd,bkhd->bhqk", q1[:, qs:qe], k1[:, :qe]).astype(f32) * scale
        s2 = jnp.einsum("bqhd,bkhd->bhqk", q2[:, qs:qe], k2[:, :qe]).astype(f32) * scale
        p1 = jax.nn.softmax(jnp.where(mask, s1, NEG_INF), axis=-1)
        p2 = jax.nn.softmax(jnp.where(mask, s2, NEG_INF), axis=-1)
        attn = (p1 - lam * p2).astype(v.dtype)
        outs.append(jnp.einsum("bhqk,bkhd->bqhd", attn, v[:, :qe]))
    o = jnp.concatenate(outs, axis=1)
    o = rms_norm(o, subln_w) * (1.0 - lam_init)
    return o.reshape(B, S, DIFF_WIDTH)


def memory_cross_attn(hn, memn, w_mq, w_mk, w_mv, w_mo):
    B, S, _ = hn.shape
    M = memn.shape[1]
    q = (hn @ w_mq).reshape(B, S, MEM_HEADS, MEM_HEAD_DIM)
    k = (memn @ w_mk).reshape(B, M, MEM_HEADS, MEM_HEAD_DIM)
    v = (memn @ w_mv).reshape(B, M, MEM_HEADS, MEM_HEAD_DIM)
    s = jnp.einsum("bshd,bmhd->bhsm", q, k).astype(jnp.float32) * (MEM_HEAD_DIM ** -0.5)
    pr = jax.nn.softmax(s, axis=-1).astype(v.dtype)
    o = jnp.einsum("bhsm,bmhd->bshd", pr, v).reshape(B, S, D_MODEL)
    return o @ w_mo


def setup_inputs(seed: int = 0) -> dict:
    key = jax.random.key(seed)
    ks = jax.random.split(key, 32)
    f32 = jnp.float32
    nrm = lambda k, shape, s: jax.random.normal(k, shape, f32) * s
    L = DEPTH
    return {
        "x": nrm(ks[0], (BATCH, SEQ, D_MODEL), 1.0),
        "mem": nrm(ks[1], (BATCH, MEM_TOKENS, D_MODEL), 1.0),
        "norm_mix_w": 1.0 + nrm(ks[2], (L, D_MODEL), 0.05),
        "w_in": nrm(ks[3], (L, D_MODEL, D_IN_TOTAL), D_MODEL ** -0.5),
        "mu_shift": jax.random.uniform(ks[4], (L, RWKV_COLS), f32),
        "w_decay0": jnp.linspace(-6.0, -1.0, RWKV_WIDTH, dtype=f32)[None] + nrm(ks[5], (L, RWKV_WIDTH), 0.1),
        "w_decay_up": nrm(ks[6], (L, DECAY_LORA, RWKV_WIDTH), 0.1),
        "a0": nrm(ks[7], (L, RWKV_WIDTH), 0.1),
        "a_up": nrm(ks[8], (L, AAA_LORA, RWKV_WIDTH), AAA_LORA ** -0.5),
        "g_up": nrm(ks[9], (L, GATE_LORA, RWKV_WIDTH), GATE_LORA ** -0.5),
        "k_k": 0.85 + nrm(ks[10], (L, RWKV_WIDTH), 0.05),
        "k_a": 1.0 + nrm(ks[11], (L, RWKV_WIDTH), 0.05),
        "r_k": nrm(ks[12], (L, RWKV_HEADS, RWKV_HEAD), 0.1),
        "lnx_w": 1.0 + nrm(ks[13], (L, RWKV_WIDTH), 0.05),
        "lnx_b": nrm(ks[14], (L, RWKV_WIDTH), 0.02),
        "lam_q1": nrm(ks[15], (L, DIFF_QKDIM), 0.1),
        "lam_k1": nrm(ks[16], (L, DIFF_QKDIM), 0.1),
        "lam_q2": nrm(ks[17], (L, DIFF_QKDIM), 0.1),
        "lam_k2": nrm(ks[18], (L, DIFF_QKDIM), 0.1),
        "subln_w": 1.0 + nrm(ks[19], (L, DIFF_VDIM), 0.05),
        "w_out": nrm(ks[20], (L, D_MIX, D_MODEL), D_MIX ** -0.5),
        "norm_mem_w": 1.0 + nrm(ks[21], (L, D_MODEL), 0.05),
        "norm_src_w": 1.0 + nrm(ks[22], (L, D_MODEL), 0.05),
        "w_mq": nrm(ks[23], (L, D_MODEL, D_MODEL), D_MODEL ** -0.5),
        "w_mk": nrm(ks[24], (L, D_MODEL, D_MODEL), D_MODEL ** -0.5),
        "w_mv": nrm(ks[25], (L, D_MODEL, D_MODEL), D_MODEL ** -0.5),
        "w_mo": nrm(ks[26], (L, D_MODEL, D_MODEL), D_MODEL ** -0.5),
        "norm_mlp_w": 1.0 + nrm(ks[27], (L, D_MODEL), 0.05),
        "w_up": nrm(ks[28], (L, D_MODEL, D_FF), D_MODEL ** -0.5),
        "w_down": nrm(ks[29], (L, D_FF, D_MODEL), D_FF ** -0.5),
        "norm_final_w": 1.0 + nrm(ks[30], (D_MODEL,), 0.05),
    }


def reference(x, mem, norm_mix_w, w_in, mu_shift, w_decay0, w_decay_up, a0, a_up, g_up,
              k_k, k_a, r_k, lnx_w, lnx_b, lam_q1, lam_k1, lam_q2, lam_k2, subln_w, w_out,
              norm_mem_w, norm_src_w, w_mq, w_mk, w_mv, w_mo, norm_mlp_w, w_up, w_down,
              norm_final_w):
    h = x
    for l in range(DEPTH):
        lam_init = 0.8 - 0.6 * math.exp(-0.3 * l)
        proj = rms_norm(h, norm_mix_w[l]) @ w_in[l]
        p_rwkv, p_diff = proj[..., :RWKV_COLS], proj[..., RWKV_COLS:]
        y_rwkv = rwkv7_group(p_rwkv, mu_shift[l], w_decay0[l], w_decay_up[l], a0[l], a_up[l],
                             g_up[l], k_k[l], k_a[l], r_k[l], lnx_w[l], lnx_b[l])
        y_diff = diff_attn_group(p_diff, lam_q1[l], lam_k1[l], lam_q2[l], lam_k2[l],
                                 subln_w[l], lam_init)
        h = h + jnp.concatenate([y_rwkv, y_diff], axis=-1) @ w_out[l]
        h = h + memory_cross_attn(rms_norm(h, norm_mem_w[l]), rms_norm(mem, norm_src_w[l]),
                                  w_mq[l], w_mk[l], w_mv[l], w_mo[l])
        hn = rms_norm(h, norm_mlp_w[l])
        h = h + jnp.square(jax.nn.relu(hn @ w_up[l])) @ w_down[l]
    return rms_norm(h, norm_final_w)
```

```python
import os
import numpy as np
from contextlib import ExitStack
import concourse.bass as bass
import concourse.mybir as mybir
from concourse.bass_utils import run_bass_kernel_spmd

F32 = mybir.dt.float32
BF16 = mybir.dt.bfloat16
AF = mybir.ActivationFunctionType
ALU = mybir.AluOpType
ENGS = ('pe', 'act', 'dve', 'pool', 'sp')

S = 4096
D = 1024
TT_ = 512
NT = S // TT_
NV = 87
C0 = float(np.exp(-0.5))
CI, CM5, CRS, CBO, NCST = 0, 128, 768, 1280, 1408
NPAN = 29
WIN_POS = [(c // 4, c % 4) for c in range(16)] + [(4, 0), (4, 1), (4, 2), (4, 3), (5, 0), (5, 1),
                                                    (6, 0), (6, 1), (6, 2), (6, 3)]


class Res:
    __slots__ = ('name', 'w', 'rs')

    def __init__(self, name=''):
        self.name = name
        self.w = None
        self.rs = {}


class Trk:
    def __init__(self):
        self.streams = {e: [] for e in ENGS}
        self.cnt = {e: 0 for e in ENGS}
        self.known = {e: {} for e in ENGS}
        self.dma_cnt = {}

    def _deps(self, eng, reads, writes):
        deps = {}
        for r in reads:
            if r.w is not None and deps.get(r.w[0], 0) < r.w[1]:
                deps[r.w[0]] = r.w[1]
        for r in writes:
            if r.w is not None and deps.get(r.w[0], 0) < r.w[1]:
                deps[r.w[0]] = r.w[1]
            for k, i in r.rs.items():
                if deps.get(k, 0) < i:
                    deps[k] = i
        kn = self.known[eng]
        for k, i in deps.items():
            if k == eng and eng in ('pe', 'sp'):
                continue
            if kn.get(k, 0) >= i:
                continue
            kn[k] = i
            self.streams[eng].append(('wait', k, i))

    def op(self, eng, fn, reads=(), writes=()):
        self._deps(eng, reads, writes)
        self.cnt[eng] += 1
        i = self.cnt[eng]
        self.streams[eng].append(('op', fn))
        for r in reads:
            if r.rs.get(eng, 0) < i:
                r.rs[eng] = i
        for r in writes:
            r.w = (eng, i)
            r.rs = {}

    def dma(self, eng, semkey, fn, reads=(), writes=()):
        self._deps(eng, reads, writes)
        self.dma_cnt[semkey] = self.dma_cnt.get(semkey, 0) + 1
        k = 'dma:' + semkey
        i = self.dma_cnt[semkey] * 16
        self.streams[eng].append(('dma', fn, semkey))
        for r in reads:
            if r.rs.get(k, 0) < i:
                r.rs[k] = i
        for r in writes:
            r.w = (k, i)
            r.rs = {}

    def emit(self, nc, block, es):
        sems = {}
        for e in ENGS:
            sems[e] = es.enter_context(nc.semaphore('sem_' + e))
        for k in self.dma_cnt:
            sems['dma:' + k] = es.enter_context(nc.semaphore('dsem_' + k))
        streams = self.streams

        def run(engname, eng):
            own = sems[engname]
            for ent in streams[engname]:
                if ent[0] == 'wait':
                    eng.wait_ge(sems[ent[1]], ent[2])
                elif ent[0] == 'op':
                    ent[1](eng).then_inc(own, 1)
                else:
                    ent[1](eng).then_inc(sems['dma:' + ent[2]], 16)

        block.tensor(lambda eng: run('pe', eng))
        block.scalar(lambda eng: run('act', eng))
        block.vector(lambda eng: run('dve', eng))
        block.gpsimd(lambda eng: run('pool', eng))
        block.sync(lambda eng: run('sp', eng))


def build_nc(ntiles=NT, dbg=None, stage_limit=99):
    nc = bass.Bass("TRN2", target_bir_lowering=False)
    xT = nc.dram_tensor("xT", [D, S], F32, kind="ExternalInput").ap()
    memT = nc.dram_tensor("memT", [D, 256], F32, kind="ExternalInput").ap()
    wpan = nc.dram_tensor("wpan", [NPAN, 128, 4096], F32, kind="ExternalInput").ap()
    wmkv = nc.dram_tensor("wmkv", [2, 128, 8192], F32, kind="ExternalInput").ap()
    wlora = nc.dram_tensor("wlora", [128, 1024], F32, kind="ExternalInput").ap()
    vecs = nc.dram_tensor("vecs", [128, NV], F32, kind="ExternalInput").ap()
    cst = nc.dram_tensor("cst", [128, NCST], F32, kind="ExternalInput").ap()
    outT = nc.dram_tensor("outT", [D, S], F32, kind="ExternalOutput").ap()
    wbf = nc.dram_tensor("wbf", [NPAN, 128, 4096], BF16).ap()
    wkvbf = nc.dram_tensor("wkvbf", [2, 128, 8192], BF16).ap()
    dbg_out = None
    if dbg:
        dbg_out = nc.dram_tensor("dbg", [128, 16, 512], F32, kind="ExternalOutput").ap()

    T = Trk()
    es = ExitStack()
    with es:
        def sb(n, s, d):
            return es.enter_context(nc.sbuf_tensor(n, s, d))

        vec = sb('vec', [128, NV], F32)
        dv = sb('dvec', [128, 32], F32)
        c32 = sb('c32', [128, 128], F32)
        m5b = sb('m5b', [128, 640], BF16)
        rsb = sb('rsb', [128, 512], BF16)
        cb = sb('cb', [128, 6, 128], BF16)
        Kc = sb('Kc', [128, 4, S], BF16)
        Vc = sb('Vc', [128, S // 128, 512], BF16)
        KmT = sb('KmT', [128, 8, 256], BF16)
        Vm = sb('Vm', [128, 2, 1024], BF16)
        lorab = sb('lorab', [128, 512], BF16)
        gupb = sb('gupb', [128, 512], BF16)
        ring = sb('ring', [128, 3, 4096], BF16)
        abf = sb('abf', [128, 8, 512], BF16)
        NPIECE = 36
        AR = sb('AR', [128, NPIECE * 512], F32)
        SM = sb('SM', [128, 8, 1408], BF16)
        carry = sb('carry', [128, 16], F32)
        ST = sb('ST', [128, 2, 4, 64], BF16)
        wcb = sb('wcb', [128, 4, 8], F32)
        rtok = sb('rtok', [128, 4], F32)
        ones32 = sb('ones32', [128, 128], F32)
        ps = [es.enter_context(nc.psum_tensor(f'ps{i}', [128, 512], F32)) for i in range(8)]

        R_vec, R_dv, R_c32, R_cb = Res('vec'), Res('dv'), Res('c32'), Res('cb')
        R_Kc = [Res(f'Kc{t}') for t in range(NT)]
        R_Vc = [Res(f'Vc{t}') for t in range(NT)]
        R_KmT, R_Vm, R_lora, R_gup = Res('KmT'), Res('Vm'), Res('lora'), Res('gup')
        R_ring = [Res(f'ring{i}') for i in range(3)]
        R_abf = [Res(f'abf{i}') for i in range(8)]
        R_pc = [Res(f'pc{i}') for i in range(NPIECE)]
        R_carry, R_wcb, R_rtok = Res('carry'), Res('wcb'), Res('rtok')
        R_ST = [[[Res(f'ST{p}_{j}_{h}') for h in range(2)] for j in range(4)] for p in range(2)]
        st_par = [[0, 0] for _ in range(4)]
        R_ps = [Res(f'ps{i}') for i in range(8)]
        R_wbf = [Res(f'wbf{i}') for i in range(NPAN)]
        R_out = Res('out')

        auto_pump = {'w': 0.0, 'fn': None, 'busy': False}

        def _ap():
            if auto_pump['w'] > 0 and auto_pump['fn'] is not None and not auto_pump['busy']:
                auto_pump['busy'] = True
                auto_pump['fn'](auto_pump['w'])
                auto_pump['busy'] = False

        def OP(eng, meth, reads, writes, *a, **k):
            T.op(eng, (lambda e: getattr(e, meth)(*a, **k)), reads, writes)
            _ap()

        def MM(out, lhsT, rhs, start, stop, reads, writes):
            T.op('pe', (lambda e: e.matmul(out, lhsT=lhsT, rhs=rhs, start=start, stop=stop)), reads, writes)
            if stop:
                _ap()

        def DMA(eng, key, out, in_, reads, writes):
            T.dma(eng, key, (lambda e: e.dma_start(out=out, in_=in_)), reads, writes)

        def pc(i, n=1):
            return AR[:, i * 512:(i + n) * 512]

        def pcb(i, n=1):
            return AR[:, i * 512:(i + n) * 512].bitcast(BF16)

        def Rp(i, n=1):
            return R_pc[i:i + n]

        bank_ctr = [0]

        bank_pool = list(range(7))

        def nb():
            b = bank_pool[bank_ctr[0] % len(bank_pool)]
            bank_ctr[0] += 1
            return b

        identb, blk1, blk64, ones1024, ones128, ones1 = [cb[:, i, :] for i in range(6)]
        ident32 = c32[:, CI:CI + 128]
        eps_rms = dv[:, 20:21]
        eps_gn = dv[:, 21:22]

        DMA('sp', 'ld0', vec[:], vecs, [], [R_vec])
        DMA('sp', 'ld1', c32[:], cst[:, 0:128], [], [R_c32])
        DMA('pool', 'ld5a', m5b[:], cst[:, CM5:CM5 + 640], [], [R_c32])
        DMA('pool', 'ld5b', rsb[:], cst[:, CRS:CRS + 512], [], [R_c32])
        DMA('pool', 'ld5c', cb[:, 1, :], cst[:, CBO:CBO + 128], [], [R_cb])
        SKIP = []

        def cast_panel(pi, deps=()):
            key = f'wc{pi}' if pi < 7 else ('wcB' if pi < 13 else 'wcC')
            DMA('pool', key, wbf[pi], wpan[pi], list(deps), [R_wbf[pi]])

        DMA('pool', 'ld2', lorab[:], wlora[:, 0:512], [], [R_lora])
        DMA('pool', 'ld3', gupb[:], wlora[:, 512:1024], [], [R_gup])
        R_wkvbf = [Res('wkvbf0'), Res('wkvbf1')]
        for pi in range(7):
            cast_panel(pi)
        DMA('pool', 'ld4a', wkvbf[0], wmkv[0], [], [R_wkvbf[0]])
        DMA('pool', 'ld4b', wkvbf[1], wmkv[1], [], [R_wkvbf[1]])
        for pi in range(7, 13):
            cast_panel(pi)
        OP('dve', 'tensor_copy', [R_c32], [R_cb], out=identb, in_=ident32)
        OP('dve', 'tensor_scalar', [R_cb], [R_cb], out=blk64, in0=blk1, scalar1=1.0 / 64, scalar2=None,
           op0=ALU.mult)
        OP('pool', 'memset', [], [R_cb], ones1024, 1.0 / 1024)
        OP('pool', 'memset', [], [R_cb], ones128, 1.0 / 128)
        OP('pool', 'memset', [], [R_cb], ones1, 1.0)
        OP('pool', 'memset', [], [R_cb], ones32[:], 1.0)
        OP('dve', 'tensor_scalar', [R_vec], [R_dv], out=dv[:, 0:14], in0=vec[:, 8:22], scalar1=-1.0, scalar2=1.0,
           op0=ALU.mult, op1=ALU.add)
        OP('dve', 'tensor_scalar', [R_vec], [R_dv], out=dv[:, 14:18], in0=vec[:, 34:38], scalar1=-1.0, scalar2=1.0,
           op0=ALU.mult, op1=ALU.add)
        OP('dve', 'tensor_scalar', [R_vec], [R_dv], out=dv[:, 18:19], in0=vec[:, 50:51], scalar1=0.8, scalar2=None,
           op0=ALU.mult)
        OP('pool', 'memset', [], [R_dv], dv[:, 20:21], 1e-5)
        OP('pool', 'memset', [], [R_dv], dv[:, 21:22], 64e-5)
        OP('pool', 'memset', [], [R_carry], carry[:], 0.0)
        OP('pool', 'memset', [], [r for a in R_ST for b in a for r in b], ST[:], 0.0)
        OP('dve', 'tensor_tensor', [R_vec], [R_dv], out=dv[:, 22:23], in0=vec[:, 83:84], in1=vec[:, 84:85], op=ALU.mult)
        OP('dve', 'tensor_tensor', [R_vec], [R_dv], out=dv[:, 23:24], in0=vec[:, 85:86], in1=vec[:, 86:87], op=ALU.mult)
        OP('pool', 'memset', [], [R_pc[0]], pc(0)[:, 0:128], 1.0)
        if 'lam' not in SKIP:
          MM(ps[0][:, 0:2], pc(0)[:, 0:128], dv[:, 22:24], True, True, [R_pc[0], R_dv], [R_ps[0]])
        OP('act', 'activation', [R_ps[0]], [R_dv], out=dv[:, 24:26], in_=ps[0][:, 0:2], func=AF.Exp)
        OP('dve', 'tensor_tensor', [R_dv], [R_dv], out=dv[:, 26:27], in0=dv[:, 25:26], in1=dv[:, 24:25], op=ALU.subtract)
        OP('dve', 'tensor_scalar', [R_dv], [R_dv], out=dv[:, 19:20], in0=dv[:, 26:27], scalar1=-0.2, scalar2=None,
           op0=ALU.add)

        def mem_kv_loads():
            m32 = pc(28, 4).rearrange('p (k m) -> p k m', m=256)
            DMA('sp', 'ld2b', m32, memT.rearrange('(k p) m -> p k m', p=128), [], Rp(28, 4))
            DMA('sp', 'ld4c', pcb(8, 8), wkvbf[0], [R_wkvbf[0]], Rp(8, 8))
            DMA('sp', 'ld4d', pcb(16, 8), wkvbf[1], [R_wkvbf[1]], Rp(16, 8))

        def mem_kv_compute():
            m32 = pc(28, 4).rearrange('p (k m) -> p k m', m=256)
            wkv = [pcb(8, 8), pcb(16, 8)]
            msq = pcb(32, 2).rearrange('p (k m) -> p k m', m=256)
            memn = pcb(34, 2).rearrange('p (k m) -> p k m', m=256)
            OP('act', 'activation', Rp(28, 4), Rp(32, 2), out=msq, in_=m32, func=AF.Square)
            bms = nb()
            for k in range(8):
                MM(ps[bms][:, 0:256], ones1024, msq[:, k, :], k == 0, k == 7, [R_cb] + Rp(32, 2), [R_ps[bms]])
            mr = pc(24)[:, 0:256]
            OP('act', 'activation', [R_ps[bms], R_dv], [R_pc[24]], out=mr, in_=ps[bms][:, 0:256], func=AF.Ln, bias=eps_rms, scale=1.0)
            OP('act', 'activation', [R_pc[24]], [R_pc[24]], out=mr, in_=mr, func=AF.Exp, scale=-0.5)
            for k in range(8):
                OP('dve', 'scalar_tensor_tensor', Rp(28, 4) + [R_pc[24], R_vec], Rp(34, 2), out=memn[:, k, :], in0=m32[:, k, :],
                   scalar=vec[:, 59 + k:60 + k], in1=mr, op0=ALU.mult, op1=ALU.mult)
            for which in range(2):
                wv = wkv[which].rearrange('p (k n) -> p k n', n=1024)
                Rw = Rp(8 + 8 * which, 8)
                if which == 0:
                    for c in range(8):
                        b = nb()
                        for k in range(8):
                            MM(ps[b][:, 0:256], wv[:, k, c * 128:(c + 1) * 128], memn[:, k, :], k == 0, k == 7,
                               Rw + Rp(34, 2), [R_ps[b]])
                        OP('act', 'activation', [R_ps[b]], [R_KmT], out=KmT[:, c, :], in_=ps[b][:, 0:256], func=AF.Copy)
                else:
                    for mb in range(2):
                        for nh in range(2):
                            b = nb()
                            for k in range(8):
                                MM(ps[b][:, :], memn[:, k, mb * 128:(mb + 1) * 128], wv[:, k, nh * 512:(nh + 1) * 512], k == 0, k == 7,
                                   Rw + Rp(34, 2), [R_ps[b]])
                            OP('act', 'activation', [R_ps[b]], [R_Vm], out=Vm[:, mb, nh * 512:(nh + 1) * 512], in_=ps[b][:, :], func=AF.Copy)

        ring_state = {'next': 0}
        panel_slot = {}

        def load_panel(seq):
            if seq >= ntiles * NPAN or seq in panel_slot:
                return
            assert seq == ring_state['next']
            ring_state['next'] += 1
            slot = seq % 3
            panel_slot[seq] = slot
            pi = seq % NPAN
            gl = pi if pi < 7 else (12 if pi < 13 else NPAN - 1)
            DMA('sp', f'ring{slot}', ring[:, slot, :], wbf[pi], [R_wbf[gl]], [R_ring[slot]])

        def use_panel(seq):
            load_panel(seq)
            load_panel(seq + 1)
            load_panel(seq + 2)
            return panel_slot[seq]

        P_PRW = 0
        P_F = 14
        P_B = 21
        P_TOK = 27
        P_YSB = 16
        P_TMP = 29
        P_RSTD = 31

        def rms_stats(src_chunks_fn, nsrc_reads, sq_piece, rstd_piece, bank):
            sqv = pcb(sq_piece).rearrange('p (h t) -> p h t', t=512)
            for k in range(8):
                src, rr = src_chunks_fn(k)
                OP('act', 'activation', rr, [R_pc[sq_piece]], out=sqv[:, k % 2, :], in_=src, func=AF.Square)
                MM(ps[bank][:, :], ones1024, sqv[:, k % 2, :], k == 0, k == 7, [R_cb, R_pc[sq_piece]], [R_ps[bank]])
            rs = pc(rstd_piece)
            OP('act', 'activation', [R_ps[bank], R_dv], [R_pc[rstd_piece]], out=rs, in_=ps[bank][:, :], func=AF.Ln, bias=eps_rms, scale=1.0)
            OP('act', 'activation', [R_pc[rstd_piece]], [R_pc[rstd_piece]], out=rs, in_=rs, func=AF.Exp, scale=-0.5)
            return rs

        R_sm = [[Res(f'sm{ci}_{n}') for n in range(13)] for ci in range(8)]
        R_pt = [Res(f'pt{i}') for i in range(4)]
        R_kk2rk = (Res('kk2'), Res('rk'))
        R_wcbj = [Res(f'wcb{j}') for j in range(4)]

        class _Stop(Exception):
            pass

        def CK(n):
            MARKS.append((n, dict(T.cnt)))
            if stage_limit <= n:
                raise _Stop()

        P_XS, P_XSQ, P_XR = 24, 26, 27

        def load_x_tile(tau):
            tsl = slice(tau * 512, (tau + 1) * 512)
            sqv = pcb(P_XSQ).rearrange('p (h t) -> p h t', t=512)
            bank_ms = nb()
            for k in range(8):
                stg = pc(P_XS + (k % 2))
                DMA('sp', f'xs{k % 2}', stg, xT[k * 128:(k + 1) * 128, tsl], [], [R_pc[P_XS + (k % 2)]])
                OP('act', 'activation', [R_pc[P_XS + (k % 2)]], [R_pc[P_XSQ]], out=sqv[:, k % 2, :], in_=stg, func=AF.Square)
                MM(ps[bank_ms][:, :], ones1024, sqv[:, k % 2, :], k == 0, k == 7, [R_cb, R_pc[P_XSQ]], [R_ps[bank_ms]])
                if k % 2 == 0:
                    OP('dve', 'tensor_scalar', [R_pc[P_XS + (k % 2)], R_vec], [R_abf[k]], out=abf[:, k, :], in0=stg,
                       scalar1=vec[:, k:k + 1], scalar2=None, op0=ALU.mult)
                else:
                    OP('act', 'activation', [R_pc[P_XS + (k % 2)], R_vec], [R_abf[k]], out=abf[:, k, :], in_=stg, func=AF.Copy, scale=vec[:, k:k + 1])
            rstd = pc(P_XR)
            OP('act', 'activation', [R_ps[bank_ms], R_dv], [R_pc[P_XR]], out=rstd, in_=ps[bank_ms][:, :], func=AF.Ln, bias=eps_rms, scale=1.0)
            OP('act', 'activation', [R_pc[P_XR]], [R_pc[P_XR]], out=rstd, in_=rstd, func=AF.Exp, scale=-0.5)
            bt_ = nb()
            for tb in range(4):
                MM(ps[bt_][:, tb:tb + 1], rstd[:, tb * 128:(tb + 1) * 128], ident32[:, 0:1], True, True, [R_pc[P_XR], R_c32], [R_ps[bt_]])
            OP('dve', 'tensor_copy', [R_ps[bt_]], [R_rtok], out=rtok[:, :], in_=ps[bt_][:, 0:4])

        def tile_body(tau):
            tsl = slice(tau * 512, (tau + 1) * 512)
            seq0 = tau * NPAN
            use_panel(seq0)
            rstd = pc(P_XR)
            CK(1)
            qbf = pcb(P_TMP, 2).rearrange('p (c t) -> p c t', t=512)
            R_qbf = Rp(P_TMP, 2)
            for c in range(22):
                pan, pos = WIN_POS[c]
                slot = use_panel(seq0 + pan)
                b = nb()
                for k in range(8):
                    MM(ps[b][:, :], ring[:, slot, k * 512 + pos * 128:k * 512 + pos * 128 + 128], abf[:, k, :], k == 0, k == 7,
                       [R_ring[slot], R_abf[k]], [R_ps[b]])
                if c < 14:
                    p = pc(P_PRW + c)
                    Rc = R_pc[P_PRW + c]
                    OP('dve', 'tensor_tensor', [R_ps[b], R_pc[P_XR]], [Rc], out=p, in0=ps[b][:, :], in1=rstd, op=ALU.mult)
                    tp = P_TMP + (c % 2)
                    tmp = pc(tp)
                    OP('act', 'activation', [Rc, R_vec], [R_pc[tp]], out=tmp[:, 1:512], in_=p[:, 0:511], func=AF.Copy, scale=vec[:, 8 + c:9 + c])
                    OP('act', 'activation', [R_carry, R_vec], [R_pc[tp]], out=tmp[:, 0:1], in_=carry[:, c:c + 1], func=AF.Copy, scale=vec[:, 8 + c:9 + c])
                    OP('act', 'activation', [Rc], [R_carry], out=carry[:, c:c + 1], in_=p[:, 511:512], func=AF.Copy)
                    OP('dve', 'scalar_tensor_tensor', [Rc, R_pc[tp], R_dv], [Rc], out=p, in0=p, scalar=dv[:, c:c + 1], in1=tmp,
                       op0=ALU.mult, op1=ALU.add)
                elif c < 18:
                    OP('dve', 'scalar_tensor_tensor', [R_ps[b], R_pc[P_XR]], R_qbf, out=qbf[:, c - 14, :], in0=ps[b][:, :], scalar=0.125,
                       in1=rstd, op0=ALU.mult, op1=ALU.mult)
                else:
                    OP('dve', 'tensor_tensor', [R_ps[b], R_pc[P_XR]], [R_Kc[tau]], out=Kc[:, c - 18, tsl], in0=ps[b][:, :], in1=rstd, op=ALU.mult)
            slot = use_panel(seq0 + 6)
            for tb in range(4):
                b = nb()
                for k in range(8):
                    MM(ps[b][:, :], abf[:, k, tb * 128:(tb + 1) * 128], ring[:, slot, k * 512:(k + 1) * 512], k == 0, k == 7,
                       [R_ring[slot], R_abf[k]], [R_ps[b]])
                OP('act', 'activation', [R_ps[b], R_rtok], [R_Vc[tau]], out=Vc[:, tau * 4 + tb, :], in_=ps[b][:, :], func=AF.Copy,
                   scale=rtok[:, tb:tb + 1])

            if tau == 0:
                for pi in range(13, NPAN):
                    cast_panel(pi, deps=[R_Kc[0], R_Vc[0]])
            nkb = 4 * tau + 4
            ot1, ot2, rl = pc(34), pc(35), pc(31)
            accL = [pc(32), pc(33)]
            R_ot1, R_ot2, R_rl = R_pc[34], R_pc[35], R_pc[31]
            R_acc = [R_pc[32], R_pc[33]]
            bO = [4, 5]
            bST = [6, 7]
            st_ctr = [0]

            def attn_gen():
                for hd in range(4):
                    units = [(kb, m) for kb in range(nkb) for m in range(2)]
                    LA = 1
                    uinfo = {}

                    def u_qk(i):
                        kb, m = units[i]
                        qoff = max(0, kb - 4 * tau) * 128
                        ktile = kb // 4
                        bs = bST[st_ctr[0] % 2]
                        ip = st_ctr[0] % 4
                        st_ctr[0] += 1
                        mp = slice(64 * m, 64 * m + 64)
                        MM(ps[bs][:, qoff:512], Kc[mp, hd, kb * 128:(kb + 1) * 128], qbf[mp, hd, qoff:512], True, True, [R_Kc[ktile]] + R_qbf, [R_ps[bs]])
                        pt = pcb(P_PRW + 12 + (ip // 2))[:, (ip % 2) * 512:(ip % 2) * 512 + 512]
                        Rpt = R_pt[ip]
                        OP('act', 'activation', [R_ps[bs]], [Rpt], out=pt[:, qoff:512], in_=ps[bs][:, qoff:512], func=AF.Exp)
                        if kb >= 4 * tau:
                            OP('dve', 'memset', [], [Rpt], pt[64:128, qoff:qoff + 64], 0.0)
                        uinfo[i] = (pt, Rpt, qoff, ktile)

                    def u_pv(i):
                        kb, m = units[i]
                        pt, Rpt, qoff, ktile = uinfo.pop(i)
                        MM(ps[bO[m]][:, qoff:512], Vc[:, kb, hd * 128:(hd + 1) * 128], pt[:, qoff:512], kb == 0, kb == nkb - 1, [R_Vc[ktile], Rpt], [R_ps[bO[m]]])
                        if kb == 0:
                            OP(ACC_ENG, 'tensor_copy', [Rpt], [R_acc[m]], out=accL[m], in_=pt)
                        else:
                            OP(ACC_ENG, 'tensor_tensor', [Rpt, R_acc[m]], [R_acc[m]], out=accL[m][:, qoff:512], in0=accL[m][:, qoff:512], in1=pt[:, qoff:512], op=ALU.add)

                    for i in range(len(units) + LA):
                        if i < len(units):
                            u_qk(i)
                        if i >= LA:
                            u_pv(i - LA)
                        yield
                    bl0 = bST[st_ctr[0] % 2]
                    st_ctr[0] += 1
                    MM(ps[bl0][:, :], ones32[:], accL[0], True, True, [R_cb, R_acc[0]], [R_ps[bl0]])
                    OP('act', 'activation', [R_ps[bl0]], [R_rl], out=rl, in_=ps[bl0][:, :], func=AF.Ln)
                    OP('act', 'activation', [R_rl], [R_rl], out=rl, in_=rl, func=AF.Exp, scale=-1.0)
                    OP('dve', 'tensor_tensor', [R_ps[bO[0]], R_rl], [R_ot1], out=ot1, in0=ps[bO[0]][:, :], in1=rl, op=ALU.mult)
                    yield
                    bl1 = bST[st_ctr[0] % 2]
                    st_ctr[0] += 1
                    MM(ps[bl1][:, :], ones32[:], accL[1], True, True, [R_cb, R_acc[1]], [R_ps[bl1]])
                    OP('act', 'activation', [R_ps[bl1]], [R_rl], out=rl, in_=ps[bl1][:, :], func=AF.Ln)
                    OP('act', 'activation', [R_rl], [R_rl], out=rl, in_=rl, func=AF.Exp, scale=-1.0)
                    OP('dve', 'tensor_scalar', [R_rl, R_dv], [R_rl], out=rl, in0=rl, scalar1=dv[:, 19:20], scalar2=None, op0=ALU.mult)
                    OP('dve', 'tensor_tensor', [R_ps[bO[1]], R_rl], [R_ot2], out=ot2, in0=ps[bO[1]][:, :], in1=rl, op=ALU.mult)
                    yield
                    OP('dve', 'tensor_tensor', [R_ot1, R_ot2], [R_ot1], out=ot1, in0=ot1, in1=ot2, op=ALU.add)
                    osq = rl.bitcast(BF16)[:, 0:512]
                    OP('act', 'activation', [R_ot1], [R_rl], out=osq, in_=ot1, func=AF.Square)
                    bq = bST[st_ctr[0] % 2]
                    st_ctr[0] += 1
                    MM(ps[bq][:, :], ones128, osq, True, True, [R_cb, R_rl], [R_ps[bq]])
                    OP('act', 'activation', [R_ps[bq], R_dv], [R_ot2], out=ot2, in_=ps[bq][:, :], func=AF.Ln, bias=eps_rms, scale=1.0)
                    OP('act', 'activation', [R_ot2], [R_ot2], out=ot2, in_=ot2, func=AF.Exp, scale=-0.5)
                    OP('dve', 'scalar_tensor_tensor', [R_ot1, R_ot2, R_dv], [R_abf[4 + hd]], out=abf[:, 4 + hd, :], in0=ot1, scalar=dv[:, 18:19],
                       in1=ot2, op0=ALU.mult, op1=ALU.mult)
                    yield

            ag = attn_gen()
            n_attn_steps = 4 * (2 * nkb + 1 + 3)
            WP = 1.0
            ACC_ENG = 'dve'
            BURST = 1.0
            pump_rate = 1.08 * n_attn_steps / (4 * (45 * WP + 14))
            pump_acc = [0.0]

            def pump(w=1.0):
                pump_acc[0] += w * pump_rate
                was = auto_pump['busy']
                auto_pump['busy'] = True
                if pump_acc[0] >= BURST:
                    while pump_acc[0] >= 1.0:
                        pump_acc[0] -= 1.0
                        next(ag, None)
                auto_pump['busy'] = was

            auto_pump['fn'] = pump

            bank_pool[:] = [0, 1, 2]

            CK(2)
            tw = pcb(P_B)[:, 0:512]
            sgd = pcb(P_B)[:, 512:1024]
            wda = pc(P_PRW + 12)
            OP('act', 'activation', [R_pc[P_PRW + 12]], [R_pc[P_B]], out=tw[0:64, :], in_=wda[0:64, :], func=AF.Tanh)
            OP('act', 'activation', [R_pc[P_PRW + 12]], [R_pc[P_B]], out=tw[64:128, :], in_=wda[64:128, :], func=AF.Copy)
            OP('act', 'activation', [R_pc[P_PRW + 13]], [R_pc[P_B]], out=sgd, in_=pc(P_PRW + 13), func=AF.Sigmoid)
            f = [pc(P_F + i) for i in range(7)]
            Rf = [R_pc[P_F + i] for i in range(7)]
            kk2 = pcb(P_B + 1)[:, 0:512]
            rk = pcb(P_B + 1)[:, 512:1024]
            btT = pcb(P_B + 2)[:, 0:512]
            ktT = pcb(P_B + 2)[:, 512:1024]
            bhT = pcb(P_B + 3)[:, 0:512]
            khT = pcb(P_B + 3)[:, 512:1024]
            vbf = pcb(P_B + 4)[:, 0:512]
            arT = pcb(P_B + 5).rearrange('p (b s t) -> p b s t', s=2, t=128)
            arF = pcb(P_B + 5)
            Atok = pcb(P_TOK)[:, 0:512].rearrange('p (b c) -> p b c', c=128)
            Vtok = pcb(P_TOK)[:, 512:1024].rearrange('p (b c) -> p b c', c=128)
            Bhtok = pcb(P_TOK + 1)[:, 0:512].rearrange('p (b c) -> p b c', c=128)
            Khtok = pcb(P_TOK + 1)[:, 512:1024].rearrange('p (b c) -> p b c', c=128)
            R_B = [R_pc[P_B + i] for i in range(6)]
            v4 = lambda ap: ap.rearrange('p (b t) -> p b t', t=128)
            v8 = lambda ap: ap.rearrange('p (c t) -> p c t', t=64)
            R_kk2, R_rk = R_kk2rk
            R_wj = R_wcbj

            def prepA(j):
                Kr = pc(P_PRW + 4 + j)
                RKr = R_pc[P_PRW + 4 + j]
                bd = nb()
                MM(ps[bd][:, :], lorab[0:64, j * 128:(j + 1) * 128], tw[0:64, :], True, True, [R_lora, R_pc[P_B]], [R_ps[bd]])
                ba = nb()
                MM(ps[ba][:, :], lorab[64:128, j * 128:(j + 1) * 128], tw[64:128, :], True, True, [R_lora, R_pc[P_B]], [R_ps[ba]])
                OP('act', 'activation', [R_ps[bd], R_vec], [Rf[0]], out=f[0], in_=ps[bd][:, :], func=AF.Sigmoid, bias=vec[:, 22 + j:23 + j], scale=1.0)
                OP('act', 'activation', [R_ps[ba], R_vec], [Rf[4]], out=f[4], in_=ps[ba][:, :], func=AF.Sigmoid, bias=vec[:, 26 + j:27 + j], scale=1.0)
                yield
                OP('act', 'activation', [RKr, R_vec], [Rf[5]], out=f[5], in_=Kr, func=AF.Copy, scale=vec[:, 30 + j:31 + j])
                OP('dve', 'tensor_tensor_scan', [Rf[0], R_c32], [Rf[1]], out=f[1], data0=rsb[:], data1=f[0], initial=0.0,
                   op0=ALU.mult, op1=ALU.add)
                yield
                OP('act', 'activation', [Rf[5]], [R_kk2], out=kk2, in_=f[5], func=AF.Square)
                OP('dve', 'tensor_tensor', [Rf[0], Rf[1]], [Rf[2]], out=f[2], in0=f[1], in1=f[0], op=ALU.subtract)
                yield
                OP('act', 'activation', [Rf[1]], [Rf[3]], out=f[3], in_=f[1], func=AF.Exp, scale=-C0)
                bq = nb()
                MM(ps[bq][:, :], blk1, kk2, True, True, [R_cb, R_kk2], [R_ps[bq]])
                OP('dve', 'tensor_scalar', [R_ps[bq]], [Rf[6]], out=f[6], in0=ps[bq][:, :], scalar1=1e-24, scalar2=None, op0=ALU.max)
                yield
                OP('act', 'activation', [Rf[2]], [Rf[2]], out=f[2], in_=f[2], func=AF.Exp, scale=-C0)
                OP('act', 'activation', [Rf[1]], [Rf[1]], out=f[1], in_=f[1], func=AF.Exp, scale=C0)
                yield
                OP('act', 'activation', [Rf[6]], [Rf[6]], out=f[6], in_=f[6], func=AF.Ln)
                OP('dve', 'tensor_copy', [Rf[3]], [R_wj[j]], out=wcb[:, j, :], in_=v8(f[3])[:, :, 63])
                yield
                OP('act', 'activation', [Rf[6]], [Rf[6]], out=f[6], in_=f[6], func=AF.Exp, scale=-0.5)
                OP('dve', 'tensor_tensor', [Rf[1], Rf[3]], [Rf[0]], out=v8(f[0]), in0=v8(f[1]), in1=v8(f[3])[:, :, 63:64].broadcast_to([128, 8, 64]),
                   op=ALU.mult)
                yield
                OP('dve', 'tensor_tensor', [Rf[5], Rf[6]], [Rf[5]], out=f[5], in0=f[5], in1=f[6], op=ALU.mult)
                yield
                OP('dve', 'tensor_scalar', [Rf[4], R_vec, R_dv], [Rf[6]], out=f[6], in0=f[4], scalar1=vec[:, 34 + j:35 + j], scalar2=dv[:, 14 + j:15 + j],
                   op0=ALU.mult, op1=ALU.add)
                yield
                OP('dve', 'tensor_tensor', [Rf[6], RKr], [Rf[6]], out=f[6], in0=f[6], in1=Kr, op=ALU.mult)
                yield
                OP('dve', 'tensor_tensor', [Rf[5], Rf[4]], [Rf[4]], out=f[4], in0=f[5], in1=f[4], op=ALU.mult)
                yield

            def prepB(j):
                Rr, Vr = pc(P_PRW + j), pc(P_PRW + 8 + j)
                RRr, RVr = R_pc[P_PRW + j], R_pc[P_PRW + 8 + j]
                OP('dve', 'scalar_tensor_tensor', [Rf[5], Rf[2]], [R_B[5]], out=arT[:, :, 0, :], in0=v4(f[5]), scalar=-1.0, in1=v4(f[2]),
                   op0=ALU.mult, op1=ALU.mult)
                OP('act', 'activation', [RVr], [R_B[4]], out=vbf, in_=Vr, func=AF.Copy)
                OP('dve', 'tensor_tensor', [Rf[4], Rf[0]], [R_B[3]], out=bhT, in0=f[4], in1=f[0], op=ALU.mult)
                OP('dve', 'tensor_tensor', [Rf[6], Rf[0]], [R_B[3]], out=khT, in0=f[6], in1=f[0], op=ALU.mult)
                OP('dve', 'tensor_tensor', [RRr, Rf[3]], [R_B[5]], out=arT[:, :, 1, :], in0=v4(Rr), in1=v4(f[3]), op=ALU.mult)
                OP('dve', 'tensor_tensor', [Rf[4], Rf[1]], [R_B[2]], out=btT, in0=f[4], in1=f[1], op=ALU.mult)
                OP('dve', 'tensor_tensor', [Rf[6], Rf[1]], [R_B[2]], out=ktT, in0=f[6], in1=f[1], op=ALU.mult)
                OP('dve', 'scalar_tensor_tensor', [RRr, R_vec, Rf[6]], [R_rk], out=rk, in0=Rr, scalar=vec[:, 38 + j:39 + j], in1=f[6],
                   op0=ALU.mult, op1=ALU.mult)
                for (src, rsrc, dst) in ((None, R_B[5], Atok), (vbf, R_B[4], Vtok), (bhT, R_B[3], Bhtok), (khT, R_B[3], Khtok)):
                    b = nb()
                    pvb = ps[b][:, :].bitcast(BF16)
                    for blk in range(4):
                        s_ap = arF[:, blk * 256:blk * 256 + 128] if src is None else src[:, blk * 128:(blk + 1) * 128]
                        T.op('pe', (lambda e, o=pvb[:, blk * 128:(blk + 1) * 128], i=s_ap: e.transpose(o, i, identb)), [rsrc, R_cb], [R_ps[b]])
                    rdst = R_pc[P_TOK] if (dst is Atok or dst is Vtok) else R_pc[P_TOK + 1]
                    OP('act', 'activation', [R_ps[b]], [rdst], out=dst, in_=pvb[:, 0:512].rearrange('p (b c) -> p b c', c=128), func=AF.Copy)

            auto_pump['w'] = WP
            for _ in prepA(0):
                pass
            for j in range(4):
                Vr = pc(P_PRW + 8 + j)
                RVr = R_pc[P_PRW + 8 + j]
                auto_pump['w'] = WP
                prepB(j)
                gA = prepA(j + 1) if j < 3 else iter(())

                def rpump(w=1.0, nA=2):
                    pump(w)
                    for _ in range(nA):
                        if next(gA, 'done') != 'done':
                            pump(WP)
                auto_pump['w'] = 0.0
                pump(2)
                CK(3)
                bY = 3
                if True:
                    chains = [(hh, blk) for hh in range(2) for blk in range(4)]
                    smv = [SM[:, ci, :] for ci in range(8)]

                    def RS(k, cis=range(8)):
                        return [R_sm[ci][k] for ci in cis]

                    def m_b(lo, hi, n):
                        return m5b[:, lo:hi].rearrange('p (o c) -> p o c', o=1).broadcast_to([128, n, hi - lo])

                    def psv(b, n, w):
                        return ps[b][:, 0:n * w].rearrange('p (c n) -> p c n', n=w)
                    for hh in range(2):
                        pb = 64 * hh
                        c0 = 4 * hh
                        b1 = nb()
                        for blk in range(4):
                            aT_ = arF[pb:pb + 64, blk * 256:blk * 256 + 128]
                            bT_ = btT[pb:pb + 64, blk * 128:(blk + 1) * 128]
                            MM(ps[b1][:, blk * 128:(blk + 1) * 128], aT_, bT_, True, True, [R_B[5], R_B[2]], [R_ps[b1]])
                        OP('dve', 'tensor_tensor', [R_ps[b1], R_c32], RS(10, range(c0, c0 + 4)), out=SM[:, c0:c0 + 4, 0:128], in0=psv(b1, 4, 128), in1=m_b(0, 128, 4), op=ALU.mult)
                        for (srcT, lo, rk_) in ((btT, 128, (11, 0)), (ktT, 384, (1, 12))):
                            for pr in range(2):
                                b2 = nb()
                                for q in range(2):
                                    blk = 2 * pr + q
                                    ar_ = arF[pb:pb + 64, blk * 256:blk * 256 + 256]
                                    xT_ = srcT[pb:pb + 64, blk * 128:(blk + 1) * 128]
                                    MM(ps[b2][:, q * 256:(q + 1) * 256], xT_, ar_, True, True, [R_B[5], R_B[2]], [R_ps[b2]])
                                cis = range(c0 + 2 * pr, c0 + 2 * pr + 2)
                                OP('dve', 'tensor_tensor', [R_ps[b2], R_c32], RS(rk_[0], cis) + RS(rk_[1], cis), out=SM[:, cis[0]:cis[0] + 2, lo:lo + 256], in0=psv(b2, 2, 256),
                                   in1=m_b(lo, lo + 256, 2), op=ALU.mult)
                    OP('dve', 'tensor_tensor', RS(11) + [R_cb], RS(4), out=SM[:, 0:8, 896:1024], in0=SM[:, 0:8, 128:256],
                       in1=identb.rearrange('p (o c) -> p o c', o=1).broadcast_to([128, 8, 128]), op=ALU.add)
                    bZ = nb()
                    for ci, (hh, blk) in enumerate(chains):
                        MM(ps[bZ][:, ci * 64:(ci + 1) * 64], smv[ci][:, 384:512], Vtok[:, blk, hh * 64:(hh + 1) * 64], True, True, [R_sm[ci][1], R_pc[P_TOK]], [R_ps[bZ]])
                    OP('act', 'activation', [R_ps[bZ]], RS(7), out=SM[:, 0:8, 1152:1216], in_=psv(bZ, 8, 64), func=AF.Copy)
                    rpump()
                    CK(3.1)
                    for lvl in range(5):
                        co, rcur_k = (640, (2,)) if lvl % 2 == 1 else (0, (10, 11))
                        no, rnxt_k = (640, (2,)) if (lvl + 1) % 2 == 1 else (0, (10, 11))
                        for p4 in range(4):
                            cis = (2 * p4, 2 * p4 + 1)
                            bC = nb()
                            for q, ci in enumerate(cis):
                                cur = smv[ci][:, co:co + 256]
                                rcur = [R_sm[ci][k_] for k_ in rcur_k]
                                MM(ps[bC][:, q * 256:q * 256 + 128], cur[:, 128:256], cur[:, 0:128], True, True, rcur, [R_ps[bC]])
                                MM(ps[bC][:, q * 256 + 128:q * 256 + 256], cur[:, 0:128], cur[:, 128:256], True, True, rcur, [R_ps[bC]])
                            wr = [R_sm[ci][k_] for ci in cis for k_ in rnxt_k]
                            if p4 % 2 == 0:
                                OP('act', 'activation', [R_ps[bC]], wr, out=SM[:, cis[0]:cis[0] + 2, no:no + 256], in_=psv(bC, 2, 256), func=AF.Copy)
                            else:
                                OP('dve', 'tensor_copy', [R_ps[bC]], wr, out=SM[:, cis[0]:cis[0] + 2, no:no + 256], in_=psv(bC, 2, 256))
                        tc_o, tck = (896, 4) if lvl % 2 == 0 else (1024, 5)
                        tn_o, tnk = (1024, 5) if lvl % 2 == 0 else (896, 4)
                        for p2 in range(2):
                            cis = range(4 * p2, 4 * p2 + 4)
                            bD = nb()
                            for q, ci in enumerate(cis):
                                MM(ps[bD][:, q * 128:(q + 1) * 128], smv[ci][:, no:no + 128], smv[ci][:, tc_o:tc_o + 128], True, True,
                                   [R_sm[ci][k_] for k_ in rnxt_k] + [R_sm[ci][tck]], [R_ps[bD]])
                            OP('dve', 'tensor_tensor', [R_ps[bD]] + RS(tck, cis), RS(tnk, cis), out=SM[:, cis[0]:cis[0] + 4, tn_o:tn_o + 128], in0=psv(bD, 4, 128),
                               in1=SM[:, cis[0]:cis[0] + 4, tc_o:tc_o + 128], op=ALU.add)
                        rpump()
                    CK(3.2)
                    for p2 in range(2):
                        cis = range(4 * p2, 4 * p2 + 4)
                        bF = nb()
                        for q, ci in enumerate(cis):
                            hh, blk = chains[ci]
                            MM(ps[bF][:, q * 128:q * 128 + 64], smv[ci][:, 1024:1152], Atok[:, blk, hh * 64:(hh + 1) * 64], True, True, [R_sm[ci][5], R_pc[P_TOK]], [R_ps[bF]])
                            MM(ps[bF][:, q * 128 + 64:q * 128 + 128], smv[ci][:, 1024:1152], smv[ci][:, 1152:1216], True, True, [R_sm[ci][5], R_sm[ci][7]], [R_ps[bF]])
                        OP('act', 'activation', [R_ps[bF]], RS(7, cis), out=SM[:, cis[0]:cis[0] + 4, 1152:1280], in_=psv(bF, 4, 128), func=AF.Copy)
                    rpump()
                    CK(3.3)
                    bG = nb()
                    for ci, (hh, blk) in enumerate(chains):
                        pb = 64 * hh
                        MM(ps[bG][pb:pb + 64, blk * 128:(blk + 1) * 128], smv[ci][:, 1152:1216], smv[ci][:, 256:384], True, True, [R_sm[ci][7], R_sm[ci][0]], [R_ps[bG]])
                    for hh in range(2):
                        pb = 64 * hh
                        OP('dve', 'tensor_tensor', [R_ps[bG], R_B[5]], RS(1, range(4 * hh, 4 * hh + 4)), out=SM[pb:pb + 64, 4 * hh:4 * hh + 4, 384:512],
                           in0=ps[bG][pb:pb + 64, 0:512].rearrange('p (c n) -> p c n', n=128), in1=arT[pb:pb + 64, 0:4, 1, :], op=ALU.add)
                    for c in range(2):
                        cs = slice(c * 64, (c + 1) * 64)
                        bH = nb()
                        for ci, (hh, blk) in enumerate(chains):
                            pb = 64 * hh
                            MM(ps[bH][pb:pb + 64, blk * 64:(blk + 1) * 64], smv[ci][cs, 1152:1216], Bhtok[cs, blk, hh * 64:(hh + 1) * 64],
                               True, True, [R_sm[ci][7], R_pc[P_TOK + 1]], [R_ps[bH]])
                        for ci, (hh, blk) in enumerate(chains):
                            pb = 64 * hh
                            OP('dve', 'scalar_tensor_tensor', [R_ps[bH], R_c32, R_wcbj[j]], [R_sm[ci][9]], out=smv[ci][pb:pb + 64, 1280 + c * 64:1280 + (c + 1) * 64],
                               in0=c32[pb:pb + 64, CI + pb:CI + pb + 64], scalar=wcb[pb:pb + 64, j, blk * 2 + c:blk * 2 + c + 1],
                               in1=ps[bH][pb:pb + 64, blk * 64:(blk + 1) * 64], op0=ALU.mult, op1=ALU.add)
                        bN = nb()
                        for ci, (hh, blk) in enumerate(chains):
                            pb = 64 * hh
                            MM(ps[bN][pb:pb + 64, blk * 64:(blk + 1) * 64], Bhtok[cs, blk, hh * 64:(hh + 1) * 64], smv[ci][cs, 1216:1280], True, False,
                               [R_pc[P_TOK + 1], R_sm[ci][7]], [R_ps[bN]])
                            MM(ps[bN][pb:pb + 64, blk * 64:(blk + 1) * 64], Khtok[cs, blk, hh * 64:(hh + 1) * 64], Vtok[cs, blk, hh * 64:(hh + 1) * 64], False, True,
                               [R_pc[P_TOK + 1], R_pc[P_TOK]], [R_ps[bN]])
                        for hh in range(2):
                            pb = 64 * hh
                            OP('act', 'activation', [R_ps[bN]], RS(2, range(4 * hh, 4 * hh + 4)), out=SM[pb:pb + 64, 4 * hh:4 * hh + 4, 640 + c * 64:640 + (c + 1) * 64],
                               in_=ps[bN][pb:pb + 64, 0:256].rearrange('p (c n) -> p c n', n=64), func=AF.Copy)
                    rpump()
                    CK(3.4)
                    for ci, (hh, blk), c in [(hh_ * 4 + blk_, (hh_, blk_), c_) for blk_ in range(4) for c_ in range(2) for hh_ in range(2)]:
                        pb = 64 * hh
                        s = smv[ci]
                        if True:
                            par = st_par[j][hh]
                            st_cur, r_cur = ST[pb:pb + 64, par, j, :], R_ST[par][j][hh]
                            st_new, r_new = ST[pb:pb + 64, 1 - par, j, :], R_ST[1 - par][j][hh]
                            st_par[j][hh] = 1 - par
                            bS = nb()
                            so = ps[bS][pb:pb + 64, 0:64]
                            MM(so, s[pb:pb + 64, 1280 + c * 64:1280 + (c + 1) * 64], st_cur, True, False, [R_sm[ci][9], r_cur], [R_ps[bS]])
                            MM(so, identb[pb:pb + 64, pb:pb + 64], s[pb:pb + 64, 640 + c * 64:640 + (c + 1) * 64], False, True, [R_cb, R_sm[ci][2]], [R_ps[bS]])
                            OP('act', 'activation', [R_ps[bS]], [r_new], out=st_new, in_=so, func=AF.Copy)
                            yo = ps[bY][pb:pb + 64, blk * 128 + c * 64:blk * 128 + (c + 1) * 64]
                            MM(yo, st_cur, s[pb:pb + 64, 384 + c * 64:384 + (c + 1) * 64], True, False, [r_cur, R_sm[ci][1]], [R_ps[bY]])
                            MM(yo, s[:, 1216:1280], s[:, 256 + c * 64:256 + (c + 1) * 64], False, False, [R_sm[ci][7], R_sm[ci][0]], [R_ps[bY]])
                            MM(yo, Vtok[:, blk, hh * 64:(hh + 1) * 64], s[:, 512 + c * 64:512 + (c + 1) * 64], False, True, [R_pc[P_TOK], R_sm[ci][12]], [R_ps[bY]])
                        if (blk * 2 + c) % 2 == 1 and hh == 1:
                            rpump(0.5, 1)

                    rpump()
                while next(gA, 'done') != 'done':
                    pump(WP)
                CK(4)
                ysb, Ry = pc(P_PRW + j), R_pc[P_PRW + j]
                tA, RtA = pc(P_PRW + 4 + j), R_pc[P_PRW + 4 + j]
                OP('act', 'activation', [R_ps[bY]], [Ry], out=ysb, in_=ps[bY][:, :], func=AF.Copy)
                ybf_ = pcb(P_B + 4)[:, 512:1024]
                OP('dve', 'tensor_copy', [Ry], [R_B[4]], out=ybf_, in_=ysb)
                OP('act', 'activation', [Ry], [R_kk2], out=kk2, in_=ysb, func=AF.Square)
                bm, bv_ = nb(), nb()
                MM(ps[bm][:, :], blk64, ybf_, True, True, [R_cb, R_B[4]], [R_ps[bm]])
                MM(ps[bv_][:, :], blk64, kk2, True, True, [R_cb, R_kk2], [R_ps[bv_]])
                bb_ = nb()
                MM(ps[bb_][:, :], blk1, rk, True, True, [R_cb, R_rk], [R_ps[bb_]])
                OP('act', 'activation', [R_ps[bm]], [RtA], out=tA, in_=ps[bm][:, :], func=AF.Square)
                OP('dve', 'tensor_tensor', [R_ps[bv_], RtA], [RtA], out=tA, in0=ps[bv_][:, :], in1=tA, op=ALU.subtract)
                OP('act', 'activation', [RtA, R_dv], [RtA], out=tA, in_=tA, func=AF.Ln, bias=eps_gn, scale=1.0)
                OP('dve', 'tensor_tensor', [Ry, R_ps[bm]], [Ry], out=ysb, in0=ysb, in1=ps[bm][:, :], op=ALU.subtract)
                OP('act', 'activation', [RtA], [RtA], out=tA, in_=tA, func=AF.Exp, scale=-0.5)
                OP('dve', 'tensor_tensor', [R_ps[bb_], RVr], [RVr], out=Vr, in0=ps[bb_][:, :], in1=Vr, op=ALU.mult)
                OP('dve', 'scalar_tensor_tensor', [Ry, RtA, R_vec], [Ry], out=ysb, in0=ysb, scalar=vec[:, 42 + j:43 + j], in1=tA, op0=ALU.mult, op1=ALU.mult)
                OP('dve', 'scalar_tensor_tensor', [Ry, RVr, R_vec], [Ry], out=ysb, in0=ysb, scalar=vec[:, 46 + j:47 + j], in1=Vr, op0=ALU.add, op1=ALU.add)
                bg = nb()
                MM(ps[bg][:, :], gupb[:, j * 128:(j + 1) * 128], sgd, True, True, [R_gup, R_pc[P_B]], [R_ps[bg]])
                OP('dve', 'tensor_tensor', [R_ps[bg], Ry], [R_abf[j]], out=abf[:, j, :], in0=ps[bg][:, :], in1=ysb, op=ALU.mult)
                for c in (j, 4 + j):
                    DMA('sp', f'xh{c}', pc(c), xT[c * 128:(c + 1) * 128, tsl], [], [R_pc[c]])

            CK(5)
            auto_pump['w'] = 0.0
            auto_pump['fn'] = None
            for _ in ag:
                pass
            bank_pool[:] = list(range(7))

            CK(6)
            def proj8(seq_base, src_reads_k, rhs_k, evac):
                for c in range(8):
                    slot = use_panel(seq_base + c // 4)
                    pos = c % 4
                    b = nb()
                    for k in range(8):
                        MM(ps[b][:, :], ring[:, slot, k * 512 + pos * 128:k * 512 + pos * 128 + 128], rhs_k(k), k == 0, k == 7,
                           [R_ring[slot]] + src_reads_k(k), [R_ps[b]])
                    evac(c, b)

            h = [pc(i) for i in range(8)]
            Rh = [R_pc[i] for i in range(8)]

            def ev_res(c, b):
                OP('dve', 'tensor_tensor', [R_ps[b], Rh[c]], [Rh[c]], out=h[c], in0=ps[b][:, :], in1=h[c], op=ALU.add)
            if tau == 0:
                mem_kv_loads()
            proj8(seq0 + 7, lambda k: [R_abf[k]], lambda k: abf[:, k, :], ev_res)
            if tau == 0:
                mem_kv_compute()

            CK(7)
            P_SQ2, P_RS2 = 30, 31

            def norm_cast(gcol):
                rs = rms_stats(lambda k: (h[k], [Rh[k]]), None, P_SQ2, P_RS2, nb())
                for k in range(8):
                    if k % 2 == 0:
                        OP('dve', 'tensor_scalar', [Rh[k], R_vec], [R_abf[k]], out=abf[:, k, :], in0=h[k],
                           scalar1=vec[:, gcol + k:gcol + k + 1], scalar2=None, op0=ALU.mult)
                    else:
                        OP('act', 'activation', [Rh[k], R_vec], [R_abf[k]], out=abf[:, k, :], in_=h[k], func=AF.Copy, scale=vec[:, gcol + k:gcol + k + 1])
                return rs
            rs2 = norm_cast(51)
            qm = pcb(24, 4).rearrange('p (c t) -> p c t', t=512)
            Rqm = Rp(24, 4)

            def ev_qm(c, b):
                OP('dve', 'scalar_tensor_tensor', [R_ps[b], R_pc[P_RS2]], Rqm, out=qm[:, c, :], in0=ps[b][:, :], scalar=1.0 / 16, in1=rs2,
                   op0=ALU.mult, op1=ALU.mult)
            proj8(seq0 + 9, lambda k: [R_abf[k]], lambda k: abf[:, k, :], ev_qm)

            CK(7.3)
            pm = pcb(28, 2).rearrange('p (m t) -> p m t', t=512)
            for hd in range(4):
                for mb in range(2):
                    b = nb()
                    for dc in range(2):
                        MM(ps[b][:, :], KmT[:, hd * 2 + dc, mb * 128:(mb + 1) * 128], qm[:, hd * 2 + dc, :], dc == 0, dc == 1, [R_KmT] + Rqm, [R_ps[b]])
                    OP('act', 'activation', [R_ps[b]], [R_pc[28 + mb // 2]], out=pm[:, mb, :], in_=ps[b][:, :], func=AF.Exp)
                bl = nb()
                for mb in range(2):
                    MM(ps[bl][:, :], ones1, pm[:, mb, :], mb == 0, mb == 1, [R_cb, R_pc[28]], [R_ps[bl]])
                rlm = pc(29)
                OP('act', 'activation', [R_ps[bl]], [R_pc[29]], out=rlm, in_=ps[bl][:, :], func=AF.Ln)
                OP('act', 'activation', [R_pc[29]], [R_pc[29]], out=rlm, in_=rlm, func=AF.Exp, scale=-1.0)
                for dc in range(2):
                    b = nb()
                    for mb in range(2):
                        MM(ps[b][:, :], Vm[:, mb, (hd * 2 + dc) * 128:(hd * 2 + dc + 1) * 128], pm[:, mb, :], mb == 0, mb == 1, [R_Vm, R_pc[28]], [R_ps[b]])
                    OP('dve', 'tensor_tensor', [R_ps[b], R_pc[29]], [R_abf[hd * 2 + dc]], out=abf[:, hd * 2 + dc, :], in0=ps[b][:, :], in1=rlm, op=ALU.mult)

            CK(7.6)
            proj8(seq0 + 11, lambda k: [R_abf[k]], lambda k: abf[:, k, :], ev_res)

            CK(8)
            rs3 = norm_cast(67)
            ubf = pcb(8, 16).rearrange('p (f t) -> p f t', t=512)
            for fo in range(32):
                slot = use_panel(seq0 + 13 + fo // 4)
                pos = fo % 4
                b = nb()
                for k in range(8):
                    MM(ps[b][:, :], ring[:, slot, k * 512 + pos * 128:k * 512 + pos * 128 + 128], abf[:, k, :], k == 0, k == 7,
                       [R_ring[slot], R_abf[k]], [R_ps[b]])
                tr = pc(28 + fo % 2)
                OP('dve', 'scalar_tensor_tensor', [R_ps[b], R_pc[P_RS2]], [R_pc[28 + fo % 2]], out=tr, in0=ps[b][:, :], scalar=0.0, in1=rs3,
                   op0=ALU.max, op1=ALU.mult)
                OP('act', 'activation', [R_pc[28 + fo % 2]], [R_pc[8 + fo // 2]], out=ubf[:, fo, :], in_=tr, func=AF.Square)
            if tau + 1 < ntiles:
                load_x_tile(tau + 1)
            CK(8.5)
            for c in range(8):
                slot = use_panel(seq0 + 21 + c)
                b = nb()
                for fo in range(32):
                    MM(ps[b][:, :], ring[:, slot, fo * 128:(fo + 1) * 128], ubf[:, fo, :], fo == 0, fo == 31, [R_ring[slot], R_pc[8 + fo // 2]], [R_ps[b]])
                ev_res(c, b)
            CK(9)
            rs4 = rms_stats(lambda k: (h[k], [Rh[k]]), None, P_SQ2, P_RS2, nb())
            for c in range(8):
                OP('dve', 'scalar_tensor_tensor', [Rh[c], R_vec, R_pc[P_RS2]], [Rh[c]], out=h[c], in0=h[c],
                   scalar=vec[:, 75 + c:76 + c], in1=rs4, op0=ALU.mult, op1=ALU.mult)
                DMA('sp', f'st{c}', outT[c * 128:(c + 1) * 128, tsl], h[c], [Rh[c]], [R_out])

        try:
            CK(0)
            load_x_tile(0)
            for tau in range(ntiles):
                tile_body(tau)
        except _Stop:
            pass

        for key in [f'st{c}' for c in range(8)]:
            if key in T.dma_cnt:
                T.streams['sp'].append(('wait', 'dma:' + key, T.dma_cnt[key] * 16))
        block = es.enter_context(nc.Block())
        T.emit(nc, block, es)
    return nc


def _panels(W, pw):
    K, N = W.shape
    kc = K // 128
    out = []
    for c0 in range(0, N, pw):
        blk = W[:, c0:c0 + pw].reshape(kc, 128, pw).transpose(1, 0, 2).reshape(128, kc * pw)
        out.append(blk)
    return out


def _prep_shared(inp):
    f = lambda k: np.asarray(inp[k], dtype=np.float32)
    w_in = f('w_in')[0]
    zpad = np.zeros((1024, 256), np.float32)
    w_in_perm = np.concatenate([w_in[:, :2560], w_in[:, 2560:2816], zpad, w_in[:, 2816:3328]], axis=1)
    pans = _panels(w_in_perm, 512)
    for k in ('w_out', 'w_mq', 'w_mo'):
        pans += _panels(f(k)[0], 512)
    pans += _panels(f('w_up')[0], 512)
    pans += _panels(f('w_down')[0], 128)
    wpan = np.ascontiguousarray(np.stack(pans, 0))
    assert wpan.shape == (NPAN, 128, 4096)
    wmkv = np.stack([_panels(f('w_mk')[0], 1024)[0], _panels(f('w_mv')[0], 1024)[0]], 0)
    wlora = np.zeros((128, 1024), np.float32)
    wlora[0:64, 0:512] = f('w_decay_up')[0]
    wlora[64:128, 0:512] = f('a_up')[0]
    wlora[:, 512:1024] = f('g_up')[0]
    vec = np.zeros((128, NV), np.float32)
    col = lambda v: np.asarray(v, np.float32).reshape(-1, 128).T
    vec[:, 0:8] = col(f('norm_mix_w')[0])
    vec[:, 8:22] = col(f('mu_shift')[0])
    vec[:, 22:26] = col(f('w_decay0')[0])
    vec[:, 26:30] = col(f('a0')[0])
    vec[:, 30:34] = col(f('k_k')[0])
    vec[:, 34:38] = col(f('k_a')[0])
    vec[:, 38:42] = col(f('r_k')[0].reshape(-1))
    vec[:, 42:46] = col(f('lnx_w')[0])
    vec[:, 46:50] = col(f('lnx_b')[0])
    vec[:, 50] = f('subln_w')[0]
    vec[:, 51:59] = col(f('norm_mem_w')[0])
    vec[:, 59:67] = col(f('norm_src_w')[0])
    vec[:, 67:75] = col(f('norm_mlp_w')[0])
    vec[:, 75:83] = col(f('norm_final_w'))
    for i, k in enumerate(('lam_q1', 'lam_k1', 'lam_q2', 'lam_k2')):
        vec[0:64, 83 + i] = f(k)[0]
    cst = np.zeros((128, NCST), np.float32)
    idx = np.arange(128)
    same = (idx[:, None] // 64) == (idx[None, :] // 64)
    mLT = ((idx[:, None] < idx[None, :]) & same).astype(np.float32)
    mInc = ((idx[:, None] <= idx[None, :]) & same).astype(np.float32)
    cst[:, CI:CI + 128] = np.eye(128, dtype=np.float32)
    cst[:, CM5:CM5 + 640] = np.concatenate([mLT.T, mLT, mInc, mLT, mInc], axis=1)
    rs = np.ones(512, np.float32)
    rs[::64] = 0.0
    cst[:, CRS:CRS + 512] = rs[None, :]
    cst[:, CBO:CBO + 128] = same.astype(np.float32)
    return dict(wpan=wpan, wmkv=np.ascontiguousarray(wmkv), wlora=wlora, vecs=vec, cst=cst)


_NC_CACHE = {}
MARKS = []


def kernel(**inputs):
    x = np.asarray(inputs['x'], dtype=np.float32)
    mem = np.asarray(inputs['mem'], dtype=np.float32)
    shared = _prep_shared(inputs)
    B = x.shape[0]
    in_maps = []
    for b in range(B):
        m = dict(shared)
        m['xT'] = np.ascontiguousarray(x[b].T)
        m['memT'] = np.ascontiguousarray(mem[b].T)
        in_maps.append(m)
    if 'nc' not in _NC_CACHE:
        _NC_CACHE['nc'] = build_nc()
    nc = _NC_CACHE['nc']
    res = run_bass_kernel_spmd(nc, in_maps, core_ids=list(range(B)))
    out = np.stack([np.ascontiguousarray(res.results[b]['outT'].T) for b in range(B)], 0)
    return out.astype(np.float32)
```

```python
import os
import numpy as np
from contextlib import ExitStack
import concourse.bass as bass
import concourse.mybir as mybir
from concourse.bass_utils import run_bass_kernel_spmd

F32 = mybir.dt.float32
BF16 = mybir.dt.bfloat16
AF = mybir.ActivationFunctionType
ALU = mybir.AluOpType
ENGS = ('pe', 'act', 'dve', 'pool', 'sp')

S = 4096
D = 1024
TT_ = 512
NT = S // TT_
NV = 87
C0 = float(np.exp(-0.5))
CI, CM5, CRS, CBO, NCST = 0, 128, 768, 1280, 1408
NPAN = 29
WIN_POS = [(c // 4, c % 4) for c in range(16)] + [(4, 0), (4, 1), (4, 2), (4, 3), (5, 0), (5, 1),
                                                    (6, 0), (6, 1), (6, 2), (6, 3)]


class Res:
    __slots__ = ('name', 'w', 'rs')

    def __init__(self, name=''):
        self.name = name
        self.w = None
        self.rs = {}


class Trk:
    def __init__(self):
        self.streams = {e: [] for e in ENGS}
        self.cnt = {e: 0 for e in ENGS}
        self.known = {e: {} for e in ENGS}
        self.dma_cnt = {}

    def _deps(self, eng, reads, writes):
        deps = {}
        for r in reads:
            if r.w is not None and deps.get(r.w[0], 0) < r.w[1]:
                deps[r.w[0]] = r.w[1]
        for r in writes:
            if r.w is not None and deps.get(r.w[0], 0) < r.w[1]:
                deps[r.w[0]] = r.w[1]
            for k, i in r.rs.items():
                if deps.get(k, 0) < i:
                    deps[k] = i
        kn = self.known[eng]
        for k, i in deps.items():
            if k == eng and eng in ('pe', 'sp'):
                continue
            if kn.get(k, 0) >= i:
                continue
            kn[k] = i
            self.streams[eng].append(('wait', k, i))

    def op(self, eng, fn, reads=(), writes=()):
        self._deps(eng, reads, writes)
        self.cnt[eng] += 1
        i = self.cnt[eng]
        self.streams[eng].append(('op', fn))
        for r in reads:
            if r.rs.get(eng, 0) < i:
                r.rs[eng] = i
        for r in writes:
            r.w = (eng, i)
            r.rs = {}

    def dma(self, eng, semkey, fn, reads=(), writes=()):
        self._deps(eng, reads, writes)
        self.dma_cnt[semkey] = self.dma_cnt.get(semkey, 0) + 1
        k = 'dma:' + semkey
        i = self.dma_cnt[semkey] * 16
        self.streams[eng].append(('dma', fn, semkey))
        for r in reads:
            if r.rs.get(k, 0) < i:
                r.rs[k] = i
        for r in writes:
            r.w = (k, i)
            r.rs = {}

    def emit(self, nc, block, es):
        sems = {}
        for e in ENGS:
            sems[e] = es.enter_context(nc.semaphore('sem_' + e))
        for k in self.dma_cnt:
            sems['dma:' + k] = es.enter_context(nc.semaphore('dsem_' + k))
        streams = self.streams

        def run(engname, eng):
            own = sems[engname]
            for ent in streams[engname]:
                if ent[0] == 'wait':
                    eng.wait_ge(sems[ent[1]], ent[2])
                elif ent[0] == 'op':
                    ent[1](eng).then_inc(own, 1)
                else:
                    ent[1](eng).then_inc(sems['dma:' + ent[2]], 16)

        block.tensor(lambda eng: run('pe', eng))
        block.scalar(lambda eng: run('act', eng))
        block.vector(lambda eng: run('dve', eng))
        block.gpsimd(lambda eng: run('pool', eng))
        block.sync(lambda eng: run('sp', eng))


def build_nc(ntiles=NT, dbg=None, stage_limit=99):
    nc = bass.Bass("TRN2", target_bir_lowering=False)
    xT = nc.dram_tensor("xT", [D, S], F32, kind="ExternalInput").ap()
    memT = nc.dram_tensor("memT", [D, 256], F32, kind="ExternalInput").ap()
    wpan = nc.dram_tensor("wpan", [NPAN, 128, 4096], F32, kind="ExternalInput").ap()
    wmkv = nc.dram_tensor("wmkv", [2, 128, 8192], F32, kind="ExternalInput").ap()
    wlora = nc.dram_tensor("wlora", [128, 1024], F32, kind="ExternalInput").ap()
    vecs = nc.dram_tensor("vecs", [128, NV], F32, kind="ExternalInput").ap()
    cst = nc.dram_tensor("cst", [128, NCST], F32, kind="ExternalInput").ap()
    outT = nc.dram_tensor("outT", [D, S], F32, kind="ExternalOutput").ap()
    wbf = nc.dram_tensor("wbf", [NPAN, 128, 4096], BF16).ap()
    dbg_out = None
    if dbg:
        dbg_out = nc.dram_tensor("dbg", [128, 16, 512], F32, kind="ExternalOutput").ap()

    T = Trk()
    es = ExitStack()
    with es:
        def sb(n, s, d):
            return es.enter_context(nc.sbuf_tensor(n, s, d))

        vec = sb('vec', [128, NV], F32)
        dv = sb('dvec', [128, 32], F32)
        c32 = sb('c32', [128, 128], F32)
        m5b = sb('m5b', [128, 640], BF16)
        rsb = sb('rsb', [128, 512], BF16)
        cb = sb('cb', [128, 6, 128], BF16)
        Kc = sb('Kc', [128, 4, S], BF16)
        Vc = sb('Vc', [128, S // 128, 512], BF16)
        KmT = sb('KmT', [128, 8, 256], BF16)
        Vm = sb('Vm', [128, 2, 1024], BF16)
        lorab = sb('lorab', [128, 512], BF16)
        gupb = sb('gupb', [128, 512], BF16)
        ring = sb('ring', [128, 3, 4096], BF16)
        abf = sb('abf', [128, 8, 512], BF16)
        NPIECE = 36
        AR = sb('AR', [128, NPIECE * 512], F32)
        SM = sb('SM', [128, 8, 1408], BF16)
        carry = sb('carry', [128, 16], F32)
        ST = sb('ST', [128, 2, 4, 64], BF16)
        wcb = sb('wcb', [128, 4, 8], F32)
        rtok = sb('rtok', [128, 4], F32)
        ones32 = sb('ones32', [128, 128], F32)
        ps = [es.enter_context(nc.psum_tensor(f'ps{i}', [128, 512], F32)) for i in range(8)]

        R_vec, R_dv, R_c32, R_cb = Res('vec'), Res('dv'), Res('c32'), Res('cb')
        R_Kc = [Res(f'Kc{t}') for t in range(NT)]
        R_Vc = [Res(f'Vc{t}') for t in range(NT)]
        R_KmT, R_Vm, R_lora, R_gup = Res('KmT'), Res('Vm'), Res('lora'), Res('gup')
        R_ring = [Res(f'ring{i}') for i in range(3)]
        R_abf = [Res(f'abf{i}') for i in range(8)]
        R_pc = [Res(f'pc{i}') for i in range(NPIECE)]
        R_carry, R_wcb, R_rtok = Res('carry'), Res('wcb'), Res('rtok')
        R_ST = [[[Res(f'ST{p}_{j}_{h}') for h in range(2)] for j in range(4)] for p in range(2)]
        st_par = [[0, 0] for _ in range(4)]
        R_ps = [Res(f'ps{i}') for i in range(8)]
        R_wbf = [Res(f'wbf{i}') for i in range(NPAN)]
        R_out = Res('out')

        auto_pump = {'w': 0.0, 'fn': None, 'busy': False}

        def _ap():
            if auto_pump['w'] > 0 and auto_pump['fn'] is not None and not auto_pump['busy']:
                auto_pump['busy'] = True
                auto_pump['fn'](auto_pump['w'])
                auto_pump['busy'] = False

        def OP(eng, meth, reads, writes, *a, **k):
            T.op(eng, (lambda e: getattr(e, meth)(*a, **k)), reads, writes)
            _ap()

        def MM(out, lhsT, rhs, start, stop, reads, writes):
            T.op('pe', (lambda e: e.matmul(out, lhsT=lhsT, rhs=rhs, start=start, stop=stop)), reads, writes)
            if stop:
                _ap()

        def DMA(eng, key, out, in_, reads, writes):
            T.dma(eng, key, (lambda e: e.dma_start(out=out, in_=in_)), reads, writes)

        def pc(i, n=1):
            return AR[:, i * 512:(i + n) * 512]

        def pcb(i, n=1):
            return AR[:, i * 512:(i + n) * 512].bitcast(BF16)

        def Rp(i, n=1):
            return R_pc[i:i + n]

        bank_ctr = [0]

        bank_pool = list(range(7))

        def nb():
            b = bank_pool[bank_ctr[0] % len(bank_pool)]
            bank_ctr[0] += 1
            return b

        identb, blk1, blk64, ones1024, ones128, ones1 = [cb[:, i, :] for i in range(6)]
        ident32 = c32[:, CI:CI + 128]
        eps_rms = dv[:, 20:21]
        eps_gn = dv[:, 21:22]

        DMA('sp', 'ld0', vec[:], vecs, [], [R_vec])
        DMA('sp', 'ld1', c32[:], cst[:, 0:128], [], [R_c32])
        DMA('pool', 'ld5a', m5b[:], cst[:, CM5:CM5 + 640], [], [R_c32])
        DMA('pool', 'ld5b', rsb[:], cst[:, CRS:CRS + 512], [], [R_c32])
        DMA('pool', 'ld5c', cb[:, 1, :], cst[:, CBO:CBO + 128], [], [R_cb])
        SKIP = []

        def cast_panel(pi, deps=()):
            key = f'wc{pi}' if pi < 7 else ('wcB' if pi < 13 else 'wcC')
            DMA('pool', key, wbf[pi], wpan[pi], list(deps), [R_wbf[pi]])

        DMA('pool', 'ld2', lorab[:], wlora[:, 0:512], [], [R_lora])
        DMA('pool', 'ld3', gupb[:], wlora[:, 512:1024], [], [R_gup])
        wkv = [pcb(8, 8), pcb(16, 8)]
        DMA('pool', 'ld4a', wkv[0], wmkv[0], [], Rp(8, 8))
        DMA('pool', 'ld4b', wkv[1], wmkv[1], [], Rp(16, 8))
        for pi in range(13):
            cast_panel(pi)
        OP('dve', 'tensor_copy', [R_c32], [R_cb], out=identb, in_=ident32)
        OP('dve', 'tensor_scalar', [R_cb], [R_cb], out=blk64, in0=blk1, scalar1=1.0 / 64, scalar2=None,
           op0=ALU.mult)
        OP('pool', 'memset', [], [R_cb], ones1024, 1.0 / 1024)
        OP('pool', 'memset', [], [R_cb], ones128, 1.0 / 128)
        OP('pool', 'memset', [], [R_cb], ones1, 1.0)
        OP('pool', 'memset', [], [R_cb], ones32[:], 1.0)
        OP('dve', 'tensor_scalar', [R_vec], [R_dv], out=dv[:, 0:14], in0=vec[:, 8:22], scalar1=-1.0, scalar2=1.0,
           op0=ALU.mult, op1=ALU.add)
        OP('dve', 'tensor_scalar', [R_vec], [R_dv], out=dv[:, 14:18], in0=vec[:, 34:38], scalar1=-1.0, scalar2=1.0,
           op0=ALU.mult, op1=ALU.add)
        OP('dve', 'tensor_scalar', [R_vec], [R_dv], out=dv[:, 18:19], in0=vec[:, 50:51], scalar1=0.8, scalar2=None,
           op0=ALU.mult)
        OP('pool', 'memset', [], [R_dv], dv[:, 20:21], 1e-5)
        OP('pool', 'memset', [], [R_dv], dv[:, 21:22], 64e-5)
        OP('pool', 'memset', [], [R_carry], carry[:], 0.0)
        OP('pool', 'memset', [], [r for a in R_ST for b in a for r in b], ST[:], 0.0)
        OP('dve', 'tensor_tensor', [R_vec], [R_dv], out=dv[:, 22:23], in0=vec[:, 83:84], in1=vec[:, 84:85], op=ALU.mult)
        OP('dve', 'tensor_tensor', [R_vec], [R_dv], out=dv[:, 23:24], in0=vec[:, 85:86], in1=vec[:, 86:87], op=ALU.mult)
        OP('pool', 'memset', [], [R_pc[0]], pc(0)[:, 0:128], 1.0)
        if 'lam' not in SKIP:
          MM(ps[0][:, 0:2], pc(0)[:, 0:128], dv[:, 22:24], True, True, [R_pc[0], R_dv], [R_ps[0]])
        OP('act', 'activation', [R_ps[0]], [R_dv], out=dv[:, 24:26], in_=ps[0][:, 0:2], func=AF.Exp)
        OP('dve', 'tensor_tensor', [R_dv], [R_dv], out=dv[:, 26:27], in0=dv[:, 25:26], in1=dv[:, 24:25], op=ALU.subtract)
        OP('dve', 'tensor_scalar', [R_dv], [R_dv], out=dv[:, 19:20], in0=dv[:, 26:27], scalar1=-0.2, scalar2=None,
           op0=ALU.add)

        if 'mem' in SKIP:
            stage_limit = -1
        m32 = pc(0, 4).rearrange('p (k m) -> p k m', m=256)
        DMA('sp', 'ld2b', m32, memT.rearrange('(k p) m -> p k m', p=128), [], Rp(0, 4))
        msq = pcb(6, 1)
        msq = pcb(6, 2).rearrange('p (k m) -> p k m', m=256)
        memn = pcb(4, 2).rearrange('p (k m) -> p k m', m=256)
        OP('act', 'activation', Rp(0, 4), Rp(6, 2), out=msq, in_=m32, func=AF.Square)
        for k in range(8):
            MM(ps[1][:, 0:256], ones1024, msq[:, k, :], k == 0, k == 7, [R_cb] + Rp(6, 2), [R_ps[1]])
        mr = pc(24)[:, 0:256]
        OP('act', 'activation', [R_ps[1], R_dv], [R_pc[24]], out=mr, in_=ps[1][:, 0:256], func=AF.Ln, bias=eps_rms, scale=1.0)
        OP('act', 'activation', [R_pc[24]], [R_pc[24]], out=mr, in_=mr, func=AF.Exp, scale=-0.5)
        for k in range(8):
            OP('dve', 'scalar_tensor_tensor', Rp(0, 4) + [R_pc[24], R_vec], Rp(4, 2), out=memn[:, k, :], in0=m32[:, k, :],
               scalar=vec[:, 59 + k:60 + k], in1=mr, op0=ALU.mult, op1=ALU.mult)
        for which in range(2):
            wv = wkv[which].rearrange('p (k n) -> p k n', n=1024)
            Rw = Rp(8 + 8 * which, 8)
            if which == 0:
                for c in range(8):
                    b = nb()
                    for k in range(8):
                        MM(ps[b][:, 0:256], wv[:, k, c * 128:(c + 1) * 128], memn[:, k, :], k == 0, k == 7,
                           Rw + Rp(4, 2), [R_ps[b]])
                    OP('act', 'activation', [R_ps[b]], [R_KmT], out=KmT[:, c, :], in_=ps[b][:, 0:256], func=AF.Copy)
            else:
                for mb in range(2):
                    for nh in range(2):
                        b = nb()
                        for k in range(8):
                            MM(ps[b][:, :], memn[:, k, mb * 128:(mb + 1) * 128], wv[:, k, nh * 512:(nh + 1) * 512], k == 0, k == 7,
                               Rw + Rp(4, 2), [R_ps[b]])
                        OP('act', 'activation', [R_ps[b]], [R_Vm], out=Vm[:, mb, nh * 512:(nh + 1) * 512], in_=ps[b][:, :], func=AF.Copy)

        ring_state = {'next': 0}
        panel_slot = {}

        def load_panel(seq):
            if seq >= ntiles * NPAN or seq in panel_slot:
                return
            assert seq == ring_state['next']
            ring_state['next'] += 1
            slot = seq % 3
            panel_slot[seq] = slot
            pi = seq % NPAN
            gl = pi if pi < 7 else (12 if pi < 13 else NPAN - 1)
            DMA('sp', f'ring{slot}', ring[:, slot, :], wbf[pi], [R_wbf[gl]], [R_ring[slot]])

        def use_panel(seq):
            load_panel(seq)
            load_panel(seq + 1)
            load_panel(seq + 2)
            return panel_slot[seq]

        P_PRW = 0
        P_F = 14
        P_B = 21
        P_TOK = 27
        P_YSB = 16
        P_TMP = 29
        P_RSTD = 31

        def rms_stats(src_chunks_fn, nsrc_reads, sq_piece, rstd_piece, bank):
            sqv = pcb(sq_piece).rearrange('p (h t) -> p h t', t=512)
            for k in range(8):
                src, rr = src_chunks_fn(k)
                OP('act', 'activation', rr, [R_pc[sq_piece]], out=sqv[:, k % 2, :], in_=src, func=AF.Square)
                MM(ps[bank][:, :], ones1024, sqv[:, k % 2, :], k == 0, k == 7, [R_cb, R_pc[sq_piece]], [R_ps[bank]])
            rs = pc(rstd_piece)
            OP('act', 'activation', [R_ps[bank], R_dv], [R_pc[rstd_piece]], out=rs, in_=ps[bank][:, :], func=AF.Ln, bias=eps_rms, scale=1.0)
            OP('act', 'activation', [R_pc[rstd_piece]], [R_pc[rstd_piece]], out=rs, in_=rs, func=AF.Exp, scale=-0.5)
            return rs

        R_sm = [[Res(f'sm{ci}_{n}') for n in range(13)] for ci in range(8)]
        R_pt = [Res(f'pt{i}') for i in range(4)]
        R_kk2rk = (Res('kk2'), Res('rk'))
        R_wcbj = [Res(f'wcb{j}') for j in range(4)]

        class _Stop(Exception):
            pass

        def CK(n):
            MARKS.append((n, dict(T.cnt)))
            if stage_limit <= n:
                raise _Stop()

        P_XS, P_XSQ, P_XR = 24, 26, 27

        def load_x_tile(tau):
            tsl = slice(tau * 512, (tau + 1) * 512)
            sqv = pcb(P_XSQ).rearrange('p (h t) -> p h t', t=512)
            bank_ms = nb()
            for k in range(8):
                stg = pc(P_XS + (k % 2))
                DMA('sp', f'xs{k % 2}', stg, xT[k * 128:(k + 1) * 128, tsl], [], [R_pc[P_XS + (k % 2)]])
                OP('act', 'activation', [R_pc[P_XS + (k % 2)]], [R_pc[P_XSQ]], out=sqv[:, k % 2, :], in_=stg, func=AF.Square)
                MM(ps[bank_ms][:, :], ones1024, sqv[:, k % 2, :], k == 0, k == 7, [R_cb, R_pc[P_XSQ]], [R_ps[bank_ms]])
                if k % 2 == 0:
                    OP('dve', 'tensor_scalar', [R_pc[P_XS + (k % 2)], R_vec], [R_abf[k]], out=abf[:, k, :], in0=stg,
                       scalar1=vec[:, k:k + 1], scalar2=None, op0=ALU.mult)
                else:
                    OP('act', 'activation', [R_pc[P_XS + (k % 2)], R_vec], [R_abf[k]], out=abf[:, k, :], in_=stg, func=AF.Copy, scale=vec[:, k:k + 1])
            rstd = pc(P_XR)
            OP('act', 'activation', [R_ps[bank_ms], R_dv], [R_pc[P_XR]], out=rstd, in_=ps[bank_ms][:, :], func=AF.Ln, bias=eps_rms, scale=1.0)
            OP('act', 'activation', [R_pc[P_XR]], [R_pc[P_XR]], out=rstd, in_=rstd, func=AF.Exp, scale=-0.5)
            bt_ = nb()
            for tb in range(4):
                MM(ps[bt_][:, tb:tb + 1], rstd[:, tb * 128:(tb + 1) * 128], ident32[:, 0:1], True, True, [R_pc[P_XR], R_c32], [R_ps[bt_]])
            OP('dve', 'tensor_copy', [R_ps[bt_]], [R_rtok], out=rtok[:, :], in_=ps[bt_][:, 0:4])

        def tile_body(tau):
            tsl = slice(tau * 512, (tau + 1) * 512)
            seq0 = tau * NPAN
            use_panel(seq0)
            rstd = pc(P_XR)
            CK(1)
            qbf = pcb(P_TMP, 2).rearrange('p (c t) -> p c t', t=512)
            R_qbf = Rp(P_TMP, 2)
            for c in range(22):
                pan, pos = WIN_POS[c]
                slot = use_panel(seq0 + pan)
                b = nb()
                for k in range(8):
                    MM(ps[b][:, :], ring[:, slot, k * 512 + pos * 128:k * 512 + pos * 128 + 128], abf[:, k, :], k == 0, k == 7,
                       [R_ring[slot], R_abf[k]], [R_ps[b]])
                if c < 14:
                    p = pc(P_PRW + c)
                    Rc = R_pc[P_PRW + c]
                    OP('dve', 'tensor_tensor', [R_ps[b], R_pc[P_XR]], [Rc], out=p, in0=ps[b][:, :], in1=rstd, op=ALU.mult)
                    tp = P_TMP + (c % 2)
                    tmp = pc(tp)
                    OP('act', 'activation', [Rc, R_vec], [R_pc[tp]], out=tmp[:, 1:512], in_=p[:, 0:511], func=AF.Copy, scale=vec[:, 8 + c:9 + c])
                    OP('act', 'activation', [R_carry, R_vec], [R_pc[tp]], out=tmp[:, 0:1], in_=carry[:, c:c + 1], func=AF.Copy, scale=vec[:, 8 + c:9 + c])
                    OP('act', 'activation', [Rc], [R_carry], out=carry[:, c:c + 1], in_=p[:, 511:512], func=AF.Copy)
                    OP('dve', 'scalar_tensor_tensor', [Rc, R_pc[tp], R_dv], [Rc], out=p, in0=p, scalar=dv[:, c:c + 1], in1=tmp,
                       op0=ALU.mult, op1=ALU.add)
                elif c < 18:
                    OP('dve', 'scalar_tensor_tensor', [R_ps[b], R_pc[P_XR]], R_qbf, out=qbf[:, c - 14, :], in0=ps[b][:, :], scalar=0.125,
                       in1=rstd, op0=ALU.mult, op1=ALU.mult)
                else:
                    OP('dve', 'tensor_tensor', [R_ps[b], R_pc[P_XR]], [R_Kc[tau]], out=Kc[:, c - 18, tsl], in0=ps[b][:, :], in1=rstd, op=ALU.mult)
            slot = use_panel(seq0 + 6)
            for tb in range(4):
                b = nb()
                for k in range(8):
                    MM(ps[b][:, :], abf[:, k, tb * 128:(tb + 1) * 128], ring[:, slot, k * 512:(k + 1) * 512], k == 0, k == 7,
                       [R_ring[slot], R_abf[k]], [R_ps[b]])
                OP('act', 'activation', [R_ps[b], R_rtok], [R_Vc[tau]], out=Vc[:, tau * 4 + tb, :], in_=ps[b][:, :], func=AF.Copy,
                   scale=rtok[:, tb:tb + 1])

            if tau == 0:
                for pi in range(13, NPAN):
                    cast_panel(pi, deps=[R_Kc[0], R_Vc[0]])
            nkb = 4 * tau + 4
            ot1, ot2, rl = pc(34), pc(35), pc(31)
            accL = [pc(32), pc(33)]
            R_ot1, R_ot2, R_rl = R_pc[34], R_pc[35], R_pc[31]
            R_acc = [R_pc[32], R_pc[33]]
            bO = [4, 5]
            bST = [6, 7]
            st_ctr = [0]

            def attn_gen():
                for hd in range(4):
                    units = [(kb, m) for kb in range(nkb) for m in range(2)]
                    LA = 1
                    uinfo = {}

                    def u_qk(i):
                        kb, m = units[i]
                        qoff = max(0, kb - 4 * tau) * 128
                        ktile = kb // 4
                        if PAIR:
                            bs = bST[m]
                            ip = (kb % 2) * 2 + m
                        else:
                            bs = bST[st_ctr[0] % 2]
                            ip = st_ctr[0] % 4
                            st_ctr[0] += 1
                        mp = slice(64 * m, 64 * m + 64)
                        MM(ps[bs][:, qoff:512], Kc[mp, hd, kb * 128:(kb + 1) * 128], qbf[mp, hd, qoff:512], True, True, [R_Kc[ktile]] + R_qbf, [R_ps[bs]])
                        pt = pcb(P_PRW + 12 + (ip // 2))[:, (ip % 2) * 512:(ip % 2) * 512 + 512]
                        Rpt = R_pt[ip]
                        uinfo[i] = (pt, Rpt, qoff, ktile, bs)
                        if not PAIR:
                            u_exp(i)

                    def u_exp(i):
                        kb, m = units[i]
                        pt, Rpt, qoff, ktile, bs = uinfo[i]
                        OP('act', 'activation', [R_ps[bs]], [Rpt], out=pt[:, qoff:512], in_=ps[bs][:, qoff:512], func=AF.Exp)
                        if kb >= 4 * tau:
                            OP('dve', 'memset', [], [Rpt], pt[64:128, qoff:qoff + 64], 0.0)

                    def u_pv(i):
                        kb, m = units[i]
                        pt, Rpt, qoff, ktile, _bs = uinfo.pop(i)
                        MM(ps[bO[m]][:, qoff:512], Vc[:, kb, hd * 128:(hd + 1) * 128], pt[:, qoff:512], kb == 0, kb == nkb - 1, [R_Vc[ktile], Rpt], [R_ps[bO[m]]])
                        if kb == 0:
                            OP(ACC_ENG, 'tensor_copy', [Rpt], [R_acc[m]], out=accL[m], in_=pt)
                        else:
                            OP(ACC_ENG, 'tensor_tensor', [Rpt, R_acc[m]], [R_acc[m]], out=accL[m][:, qoff:512], in0=accL[m][:, qoff:512], in1=pt[:, qoff:512], op=ALU.add)

                    if PAIR:
                        for kb_ in range(nkb + 1):
                            if kb_ < nkb:
                                u_qk(2 * kb_)
                                u_qk(2 * kb_ + 1)
                                u_exp(2 * kb_)
                                u_exp(2 * kb_ + 1)
                            if kb_ >= 1:
                                u_pv(2 * kb_ - 2)
                                u_pv(2 * kb_ - 1)
                            yield
                            yield
                    else:
                        for i in range(len(units) + LA):
                            if i < len(units):
                                u_qk(i)
                            if i >= LA:
                                u_pv(i - LA)
                            yield
                    bl0 = bST[st_ctr[0] % 2]
                    st_ctr[0] += 1
                    MM(ps[bl0][:, :], ones32[:], accL[0], True, True, [R_cb, R_acc[0]], [R_ps[bl0]])
                    OP('act', 'activation', [R_ps[bl0]], [R_rl], out=rl, in_=ps[bl0][:, :], func=AF.Ln)
                    OP('act', 'activation', [R_rl], [R_rl], out=rl, in_=rl, func=AF.Exp, scale=-1.0)
                    OP('dve', 'tensor_tensor', [R_ps[bO[0]], R_rl], [R_ot1], out=ot1, in0=ps[bO[0]][:, :], in1=rl, op=ALU.mult)
                    yield
                    bl1 = bST[st_ctr[0] % 2]
                    st_ctr[0] += 1
                    MM(ps[bl1][:, :], ones32[:], accL[1], True, True, [R_cb, R_acc[1]], [R_ps[bl1]])
                    OP('act', 'activation', [R_ps[bl1]], [R_rl], out=rl, in_=ps[bl1][:, :], func=AF.Ln)
                    OP('act', 'activation', [R_rl], [R_rl], out=rl, in_=rl, func=AF.Exp, scale=-1.0)
                    OP('dve', 'tensor_scalar', [R_rl, R_dv], [R_rl], out=rl, in0=rl, scalar1=dv[:, 19:20], scalar2=None, op0=ALU.mult)
                    OP('dve', 'tensor_tensor', [R_ps[bO[1]], R_rl], [R_ot2], out=ot2, in0=ps[bO[1]][:, :], in1=rl, op=ALU.mult)
                    yield
                    OP('dve', 'tensor_tensor', [R_ot1, R_ot2], [R_ot1], out=ot1, in0=ot1, in1=ot2, op=ALU.add)
                    osq = rl.bitcast(BF16)[:, 0:512]
                    OP('act', 'activation', [R_ot1], [R_rl], out=osq, in_=ot1, func=AF.Square)
                    bq = bST[st_ctr[0] % 2]
                    st_ctr[0] += 1
                    MM(ps[bq][:, :], ones128, osq, True, True, [R_cb, R_rl], [R_ps[bq]])
                    OP('act', 'activation', [R_ps[bq], R_dv], [R_ot2], out=ot2, in_=ps[bq][:, :], func=AF.Ln, bias=eps_rms, scale=1.0)
                    OP('act', 'activation', [R_ot2], [R_ot2], out=ot2, in_=ot2, func=AF.Exp, scale=-0.5)
                    OP('dve', 'scalar_tensor_tensor', [R_ot1, R_ot2, R_dv], [R_abf[4 + hd]], out=abf[:, 4 + hd, :], in0=ot1, scalar=dv[:, 18:19],
                       in1=ot2, op0=ALU.mult, op1=ALU.mult)
                    yield

            ag = attn_gen()
            n_attn_steps = 4 * (2 * nkb + 1 + 3)
            WP = 1.0
            ACC_ENG = 'dve'
            PAIR = True
            BURST = 1.0
            pump_rate = 1.08 * n_attn_steps / (4 * (45 * WP + 14))
            pump_acc = [0.0]

            def pump(w=1.0):
                pump_acc[0] += w * pump_rate
                was = auto_pump['busy']
                auto_pump['busy'] = True
                if pump_acc[0] >= BURST:
                    while pump_acc[0] >= 1.0:
                        pump_acc[0] -= 1.0
                        next(ag, None)
                auto_pump['busy'] = was

            auto_pump['fn'] = pump

            bank_pool[:] = [0, 1, 2]

            CK(2)
            tw = pcb(P_B)[:, 0:512]
            sgd = pcb(P_B)[:, 512:1024]
            wda = pc(P_PRW + 12)
            OP('act', 'activation', [R_pc[P_PRW + 12]], [R_pc[P_B]], out=tw[0:64, :], in_=wda[0:64, :], func=AF.Tanh)
            OP('act', 'activation', [R_pc[P_PRW + 12]], [R_pc[P_B]], out=tw[64:128, :], in_=wda[64:128, :], func=AF.Copy)
            OP('act', 'activation', [R_pc[P_PRW + 13]], [R_pc[P_B]], out=sgd, in_=pc(P_PRW + 13), func=AF.Sigmoid)
            f = [pc(P_F + i) for i in range(7)]
            Rf = [R_pc[P_F + i] for i in range(7)]
            kk2 = pcb(P_B + 1)[:, 0:512]
            rk = pcb(P_B + 1)[:, 512:1024]
            btT = pcb(P_B + 2)[:, 0:512]
            ktT = pcb(P_B + 2)[:, 512:1024]
            bhT = pcb(P_B + 3)[:, 0:512]
            khT = pcb(P_B + 3)[:, 512:1024]
            vbf = pcb(P_B + 4)[:, 0:512]
            arT = pcb(P_B + 5).rearrange('p (b s t) -> p b s t', s=2, t=128)
            arF = pcb(P_B + 5)
            Atok = pcb(P_TOK)[:, 0:512].rearrange('p (b c) -> p b c', c=128)
            Vtok = pcb(P_TOK)[:, 512:1024].rearrange('p (b c) -> p b c', c=128)
            Bhtok = pcb(P_TOK + 1)[:, 0:512].rearrange('p (b c) -> p b c', c=128)
            Khtok = pcb(P_TOK + 1)[:, 512:1024].rearrange('p (b c) -> p b c', c=128)
            R_B = [R_pc[P_B + i] for i in range(6)]
            v4 = lambda ap: ap.rearrange('p (b t) -> p b t', t=128)
            v8 = lambda ap: ap.rearrange('p (c t) -> p c t', t=64)
            R_kk2, R_rk = R_kk2rk
            R_wj = R_wcbj

            def prepA(j):
                Kr = pc(P_PRW + 4 + j)
                RKr = R_pc[P_PRW + 4 + j]
                bd = nb()
                MM(ps[bd][:, :], lorab[0:64, j * 128:(j + 1) * 128], tw[0:64, :], True, True, [R_lora, R_pc[P_B]], [R_ps[bd]])
                ba = nb()
                MM(ps[ba][:, :], lorab[64:128, j * 128:(j + 1) * 128], tw[64:128, :], True, True, [R_lora, R_pc[P_B]], [R_ps[ba]])
                OP('act', 'activation', [R_ps[bd], R_vec], [Rf[0]], out=f[0], in_=ps[bd][:, :], func=AF.Sigmoid, bias=vec[:, 22 + j:23 + j], scale=1.0)
                OP('act', 'activation', [R_ps[ba], R_vec], [Rf[4]], out=f[4], in_=ps[ba][:, :], func=AF.Sigmoid, bias=vec[:, 26 + j:27 + j], scale=1.0)
                yield
                OP('act', 'activation', [RKr, R_vec], [Rf[5]], out=f[5], in_=Kr, func=AF.Copy, scale=vec[:, 30 + j:31 + j])
                OP('dve', 'tensor_tensor_scan', [Rf[0], R_c32], [Rf[1]], out=f[1], data0=rsb[:], data1=f[0], initial=0.0,
                   op0=ALU.mult, op1=ALU.add)
                yield
                OP('act', 'activation', [Rf[5]], [R_kk2], out=kk2, in_=f[5], func=AF.Square)
                OP('dve', 'tensor_tensor', [Rf[0], Rf[1]], [Rf[2]], out=f[2], in0=f[1], in1=f[0], op=ALU.subtract)
                yield
                OP('act', 'activation', [Rf[1]], [Rf[3]], out=f[3], in_=f[1], func=AF.Exp, scale=-C0)
                bq = nb()
                MM(ps[bq][:, :], blk1, kk2, True, True, [R_cb, R_kk2], [R_ps[bq]])
                OP('dve', 'tensor_scalar', [R_ps[bq]], [Rf[6]], out=f[6], in0=ps[bq][:, :], scalar1=1e-24, scalar2=None, op0=ALU.max)
                yield
                OP('act', 'activation', [Rf[2]], [Rf[2]], out=f[2], in_=f[2], func=AF.Exp, scale=-C0)
                OP('act', 'activation', [Rf[1]], [Rf[1]], out=f[1], in_=f[1], func=AF.Exp, scale=C0)
                yield
                OP('act', 'activation', [Rf[6]], [Rf[6]], out=f[6], in_=f[6], func=AF.Ln)
                OP('dve', 'tensor_copy', [Rf[3]], [R_wj[j]], out=wcb[:, j, :], in_=v8(f[3])[:, :, 63])
                yield
                OP('act', 'activation', [Rf[6]], [Rf[6]], out=f[6], in_=f[6], func=AF.Exp, scale=-0.5)
                OP('dve', 'tensor_tensor', [Rf[1], Rf[3]], [Rf[0]], out=v8(f[0]), in0=v8(f[1]), in1=v8(f[3])[:, :, 63:64].broadcast_to([128, 8, 64]),
                   op=ALU.mult)
                yield
                OP('dve', 'tensor_tensor', [Rf[5], Rf[6]], [Rf[5]], out=f[5], in0=f[5], in1=f[6], op=ALU.mult)
                yield
                OP('dve', 'tensor_scalar', [Rf[4], R_vec, R_dv], [Rf[6]], out=f[6], in0=f[4], scalar1=vec[:, 34 + j:35 + j], scalar2=dv[:, 14 + j:15 + j],
                   op0=ALU.mult, op1=ALU.add)
                yield
                OP('dve', 'tensor_tensor', [Rf[6], RKr], [Rf[6]], out=f[6], in0=f[6], in1=Kr, op=ALU.mult)
                yield
                OP('dve', 'tensor_tensor', [Rf[5], Rf[4]], [Rf[4]], out=f[4], in0=f[5], in1=f[4], op=ALU.mult)
                yield

            def prepB(j):
                Rr, Vr = pc(P_PRW + j), pc(P_PRW + 8 + j)
                RRr, RVr = R_pc[P_PRW + j], R_pc[P_PRW + 8 + j]
                OP('dve', 'scalar_tensor_tensor', [Rf[5], Rf[2]], [R_B[5]], out=arT[:, :, 0, :], in0=v4(f[5]), scalar=-1.0, in1=v4(f[2]),
                   op0=ALU.mult, op1=ALU.mult)
                OP('act', 'activation', [RVr], [R_B[4]], out=vbf, in_=Vr, func=AF.Copy)
                OP('dve', 'tensor_tensor', [Rf[4], Rf[0]], [R_B[3]], out=bhT, in0=f[4], in1=f[0], op=ALU.mult)
                OP('dve', 'tensor_tensor', [Rf[6], Rf[0]], [R_B[3]], out=khT, in0=f[6], in1=f[0], op=ALU.mult)
                OP('dve', 'tensor_tensor', [RRr, Rf[3]], [R_B[5]], out=arT[:, :, 1, :], in0=v4(Rr), in1=v4(f[3]), op=ALU.mult)
                OP('dve', 'tensor_tensor', [Rf[4], Rf[1]], [R_B[2]], out=btT, in0=f[4], in1=f[1], op=ALU.mult)
                OP('dve', 'tensor_tensor', [Rf[6], Rf[1]], [R_B[2]], out=ktT, in0=f[6], in1=f[1], op=ALU.mult)
                OP('dve', 'scalar_tensor_tensor', [RRr, R_vec, Rf[6]], [R_rk], out=rk, in0=Rr, scalar=vec[:, 38 + j:39 + j], in1=f[6],
                   op0=ALU.mult, op1=ALU.mult)
                for (src, rsrc, dst) in ((None, R_B[5], Atok), (vbf, R_B[4], Vtok), (bhT, R_B[3], Bhtok), (khT, R_B[3], Khtok)):
                    b = nb()
                    pvb = ps[b][:, :].bitcast(BF16)
                    for blk in range(4):
                        s_ap = arF[:, blk * 256:blk * 256 + 128] if src is None else src[:, blk * 128:(blk + 1) * 128]
                        T.op('pe', (lambda e, o=pvb[:, blk * 128:(blk + 1) * 128], i=s_ap: e.transpose(o, i, identb)), [rsrc, R_cb], [R_ps[b]])
                    rdst = R_pc[P_TOK] if (dst is Atok or dst is Vtok) else R_pc[P_TOK + 1]
                    OP('act', 'activation', [R_ps[b]], [rdst], out=dst, in_=pvb[:, 0:512].rearrange('p (b c) -> p b c', c=128), func=AF.Copy)

            auto_pump['w'] = WP
            for _ in prepA(0):
                pass
            for j in range(4):
                Vr = pc(P_PRW + 8 + j)
                RVr = R_pc[P_PRW + 8 + j]
                auto_pump['w'] = WP
                prepB(j)
                gA = prepA(j + 1) if j < 3 else iter(())

                def rpump(w=1.0, nA=2):
                    pump(w)
                    for _ in range(nA):
                        if next(gA, 'done') != 'done':
                            pump(WP)
                auto_pump['w'] = 0.0
                pump(2)
                CK(3)
                bY = 3
                if True:
                    chains = [(hh, blk) for hh in range(2) for blk in range(4)]
                    smv = [SM[:, ci, :] for ci in range(8)]

                    def RS(k, cis=range(8)):
                        return [R_sm[ci][k] for ci in cis]

                    def m_b(lo, hi, n):
                        return m5b[:, lo:hi].rearrange('p (o c) -> p o c', o=1).broadcast_to([128, n, hi - lo])

                    def psv(b, n, w):
                        return ps[b][:, 0:n * w].rearrange('p (c n) -> p c n', n=w)
                    for hh in range(2):
                        pb = 64 * hh
                        c0 = 4 * hh
                        b1 = nb()
                        for blk in range(4):
                            aT_ = arF[pb:pb + 64, blk * 256:blk * 256 + 128]
                            bT_ = btT[pb:pb + 64, blk * 128:(blk + 1) * 128]
                            MM(ps[b1][:, blk * 128:(blk + 1) * 128], aT_, bT_, True, True, [R_B[5], R_B[2]], [R_ps[b1]])
                        OP('dve', 'tensor_tensor', [R_ps[b1], R_c32], RS(10, range(c0, c0 + 4)), out=SM[:, c0:c0 + 4, 0:128], in0=psv(b1, 4, 128), in1=m_b(0, 128, 4), op=ALU.mult)
                        for (srcT, lo, rk_) in ((btT, 128, (11, 0)), (ktT, 384, (1, 12))):
                            for pr in range(2):
                                b2 = nb()
                                for q in range(2):
                                    blk = 2 * pr + q
                                    ar_ = arF[pb:pb + 64, blk * 256:blk * 256 + 256]
                                    xT_ = srcT[pb:pb + 64, blk * 128:(blk + 1) * 128]
                                    MM(ps[b2][:, q * 256:(q + 1) * 256], xT_, ar_, True, True, [R_B[5], R_B[2]], [R_ps[b2]])
                                cis = range(c0 + 2 * pr, c0 + 2 * pr + 2)
                                OP('dve', 'tensor_tensor', [R_ps[b2], R_c32], RS(rk_[0], cis) + RS(rk_[1], cis), out=SM[:, cis[0]:cis[0] + 2, lo:lo + 256], in0=psv(b2, 2, 256),
                                   in1=m_b(lo, lo + 256, 2), op=ALU.mult)
                    OP('dve', 'tensor_tensor', RS(11) + [R_cb], RS(4), out=SM[:, 0:8, 896:1024], in0=SM[:, 0:8, 128:256],
                       in1=identb.rearrange('p (o c) -> p o c', o=1).broadcast_to([128, 8, 128]), op=ALU.add)
                    bZ = nb()
                    for ci, (hh, blk) in enumerate(chains):
                        MM(ps[bZ][:, ci * 64:(ci + 1) * 64], smv[ci][:, 384:512], Vtok[:, blk, hh * 64:(hh + 1) * 64], True, True, [R_sm[ci][1], R_pc[P_TOK]], [R_ps[bZ]])
                    OP('act', 'activation', [R_ps[bZ]], RS(7), out=SM[:, 0:8, 1152:1216], in_=psv(bZ, 8, 64), func=AF.Copy)
                    rpump()
                    CK(3.1)
                    for lvl in range(5):
                        co, rcur_k = (640, (2,)) if lvl % 2 == 1 else (0, (10, 11))
                        no, rnxt_k = (640, (2,)) if (lvl + 1) % 2 == 1 else (0, (10, 11))
                        for p4 in range(4):
                            cis = (2 * p4, 2 * p4 + 1)
                            bC = nb()
                            for q, ci in enumerate(cis):
                                cur = smv[ci][:, co:co + 256]
                                rcur = [R_sm[ci][k_] for k_ in rcur_k]
                                MM(ps[bC][:, q * 256:q * 256 + 128], cur[:, 128:256], cur[:, 0:128], True, True, rcur, [R_ps[bC]])
                                MM(ps[bC][:, q * 256 + 128:q * 256 + 256], cur[:, 0:128], cur[:, 128:256], True, True, rcur, [R_ps[bC]])
                            wr = [R_sm[ci][k_] for ci in cis for k_ in rnxt_k]
                            if p4 % 2 == 0:
                                OP('act', 'activation', [R_ps[bC]], wr, out=SM[:, cis[0]:cis[0] + 2, no:no + 256], in_=psv(bC, 2, 256), func=AF.Copy)
                            else:
                                OP('dve', 'tensor_copy', [R_ps[bC]], wr, out=SM[:, cis[0]:cis[0] + 2, no:no + 256], in_=psv(bC, 2, 256))
                        tc_o, tck = (896, 4) if lvl % 2 == 0 else (1024, 5)
                        tn_o, tnk = (1024, 5) if lvl % 2 == 0 else (896, 4)
                        for p2 in range(2):
                            cis = range(4 * p2, 4 * p2 + 4)
                            bD = nb()
                            for q, ci in enumerate(cis):
                                MM(ps[bD][:, q * 128:(q + 1) * 128], smv[ci][:, no:no + 128], smv[ci][:, tc_o:tc_o + 128], True, True,
                                   [R_sm[ci][k_] for k_ in rnxt_k] + [R_sm[ci][tck]], [R_ps[bD]])
                            OP('dve', 'tensor_tensor', [R_ps[bD]] + RS(tck, cis), RS(tnk, cis), out=SM[:, cis[0]:cis[0] + 4, tn_o:tn_o + 128], in0=psv(bD, 4, 128),
                               in1=SM[:, cis[0]:cis[0] + 4, tc_o:tc_o + 128], op=ALU.add)
                        rpump()
                    CK(3.2)
                    for p2 in range(2):
                        cis = range(4 * p2, 4 * p2 + 4)
                        bF = nb()
                        for q, ci in enumerate(cis):
                            hh, blk = chains[ci]
                            MM(ps[bF][:, q * 128:q * 128 + 64], smv[ci][:, 1024:1152], Atok[:, blk, hh * 64:(hh + 1) * 64], True, True, [R_sm[ci][5], R_pc[P_TOK]], [R_ps[bF]])
                            MM(ps[bF][:, q * 128 + 64:q * 128 + 128], smv[ci][:, 1024:1152], smv[ci][:, 1152:1216], True, True, [R_sm[ci][5], R_sm[ci][7]], [R_ps[bF]])
                        OP('act', 'activation', [R_ps[bF]], RS(7, cis), out=SM[:, cis[0]:cis[0] + 4, 1152:1280], in_=psv(bF, 4, 128), func=AF.Copy)
                    rpump()
                    CK(3.3)
                    bG = nb()
                    for ci, (hh, blk) in enumerate(chains):
                        pb = 64 * hh
                        MM(ps[bG][pb:pb + 64, blk * 128:(blk + 1) * 128], smv[ci][:, 1152:1216], smv[ci][:, 256:384], True, True, [R_sm[ci][7], R_sm[ci][0]], [R_ps[bG]])
                    for hh in range(2):
                        pb = 64 * hh
                        OP('dve', 'tensor_tensor', [R_ps[bG], R_B[5]], RS(1, range(4 * hh, 4 * hh + 4)), out=SM[pb:pb + 64, 4 * hh:4 * hh + 4, 384:512],
                           in0=ps[bG][pb:pb + 64, 0:512].rearrange('p (c n) -> p c n', n=128), in1=arT[pb:pb + 64, 0:4, 1, :], op=ALU.add)
                    for c in range(2):
                        cs = slice(c * 64, (c + 1) * 64)
                        bH = nb()
                        for ci, (hh, blk) in enumerate(chains):
                            pb = 64 * hh
                            MM(ps[bH][pb:pb + 64, blk * 64:(blk + 1) * 64], smv[ci][cs, 1152:1216], Bhtok[cs, blk, hh * 64:(hh + 1) * 64],
                               True, True, [R_sm[ci][7], R_pc[P_TOK + 1]], [R_ps[bH]])
                        for ci, (hh, blk) in enumerate(chains):
                            pb = 64 * hh
                            OP('dve', 'scalar_tensor_tensor', [R_ps[bH], R_c32, R_wcbj[j]], [R_sm[ci][9]], out=smv[ci][pb:pb + 64, 1280 + c * 64:1280 + (c + 1) * 64],
                               in0=c32[pb:pb + 64, CI + pb:CI + pb + 64], scalar=wcb[pb:pb + 64, j, blk * 2 + c:blk * 2 + c + 1],
                               in1=ps[bH][pb:pb + 64, blk * 64:(blk + 1) * 64], op0=ALU.mult, op1=ALU.add)
                        bN = nb()
                        for ci, (hh, blk) in enumerate(chains):
                            pb = 64 * hh
                            MM(ps[bN][pb:pb + 64, blk * 64:(blk + 1) * 64], Bhtok[cs, blk, hh * 64:(hh + 1) * 64], smv[ci][cs, 1216:1280], True, False,
                               [R_pc[P_TOK + 1], R_sm[ci][7]], [R_ps[bN]])
                            MM(ps[bN][pb:pb + 64, blk * 64:(blk + 1) * 64], Khtok[cs, blk, hh * 64:(hh + 1) * 64], Vtok[cs, blk, hh * 64:(hh + 1) * 64], False, True,
                               [R_pc[P_TOK + 1], R_pc[P_TOK]], [R_ps[bN]])
                        for hh in range(2):
                            pb = 64 * hh
                            OP('act', 'activation', [R_ps[bN]], RS(2, range(4 * hh, 4 * hh + 4)), out=SM[pb:pb + 64, 4 * hh:4 * hh + 4, 640 + c * 64:640 + (c + 1) * 64],
                               in_=ps[bN][pb:pb + 64, 0:256].rearrange('p (c n) -> p c n', n=64), func=AF.Copy)
                    rpump()
                    CK(3.4)
                    for ci, (hh, blk), c in [(hh_ * 4 + blk_, (hh_, blk_), c_) for blk_ in range(4) for c_ in range(2) for hh_ in range(2)]:
                        pb = 64 * hh
                        s = smv[ci]
                        if True:
                            par = st_par[j][hh]
                            st_cur, r_cur = ST[pb:pb + 64, par, j, :], R_ST[par][j][hh]
                            st_new, r_new = ST[pb:pb + 64, 1 - par, j, :], R_ST[1 - par][j][hh]
                            st_par[j][hh] = 1 - par
                            bS = nb()
                            so = ps[bS][pb:pb + 64, 0:64]
                            MM(so, s[pb:pb + 64, 1280 + c * 64:1280 + (c + 1) * 64], st_cur, True, False, [R_sm[ci][9], r_cur], [R_ps[bS]])
                            MM(so, identb[pb:pb + 64, pb:pb + 64], s[pb:pb + 64, 640 + c * 64:640 + (c + 1) * 64], False, True, [R_cb, R_sm[ci][2]], [R_ps[bS]])
                            OP('act', 'activation', [R_ps[bS]], [r_new], out=st_new, in_=so, func=AF.Copy)
                            yo = ps[bY][pb:pb + 64, blk * 128 + c * 64:blk * 128 + (c + 1) * 64]
                            MM(yo, st_cur, s[pb:pb + 64, 384 + c * 64:384 + (c + 1) * 64], True, False, [r_cur, R_sm[ci][1]], [R_ps[bY]])
                            MM(yo, s[:, 1216:1280], s[:, 256 + c * 64:256 + (c + 1) * 64], False, False, [R_sm[ci][7], R_sm[ci][0]], [R_ps[bY]])
                            MM(yo, Vtok[:, blk, hh * 64:(hh + 1) * 64], s[:, 512 + c * 64:512 + (c + 1) * 64], False, True, [R_pc[P_TOK], R_sm[ci][12]], [R_ps[bY]])
                        if (blk * 2 + c) % 2 == 1 and hh == 1:
                            rpump(0.5, 1)

                    rpump()
                while next(gA, 'done') != 'done':
                    pump(WP)
                CK(4)
                ysb, Ry = pc(P_PRW + j), R_pc[P_PRW + j]
                tA, RtA = pc(P_PRW + 4 + j), R_pc[P_PRW + 4 + j]
                OP('act', 'activation', [R_ps[bY]], [Ry], out=ysb, in_=ps[bY][:, :], func=AF.Copy)
                ybf_ = pcb(P_B + 4)[:, 512:1024]
                OP('dve', 'tensor_copy', [Ry], [R_B[4]], out=ybf_, in_=ysb)
                OP('act', 'activation', [Ry], [R_kk2], out=kk2, in_=ysb, func=AF.Square)
                bm, bv_ = nb(), nb()
                MM(ps[bm][:, :], blk64, ybf_, True, True, [R_cb, R_B[4]], [R_ps[bm]])
                MM(ps[bv_][:, :], blk64, kk2, True, True, [R_cb, R_kk2], [R_ps[bv_]])
                bb_ = nb()
                MM(ps[bb_][:, :], blk1, rk, True, True, [R_cb, R_rk], [R_ps[bb_]])
                OP('act', 'activation', [R_ps[bm]], [RtA], out=tA, in_=ps[bm][:, :], func=AF.Square)
                OP('dve', 'tensor_tensor', [R_ps[bv_], RtA], [RtA], out=tA, in0=ps[bv_][:, :], in1=tA, op=ALU.subtract)
                OP('act', 'activation', [RtA, R_dv], [RtA], out=tA, in_=tA, func=AF.Ln, bias=eps_gn, scale=1.0)
                OP('dve', 'tensor_tensor', [Ry, R_ps[bm]], [Ry], out=ysb, in0=ysb, in1=ps[bm][:, :], op=ALU.subtract)
                OP('act', 'activation', [RtA], [RtA], out=tA, in_=tA, func=AF.Exp, scale=-0.5)
                OP('dve', 'tensor_tensor', [R_ps[bb_], RVr], [RVr], out=Vr, in0=ps[bb_][:, :], in1=Vr, op=ALU.mult)
                OP('dve', 'scalar_tensor_tensor', [Ry, RtA, R_vec], [Ry], out=ysb, in0=ysb, scalar=vec[:, 42 + j:43 + j], in1=tA, op0=ALU.mult, op1=ALU.mult)
                OP('dve', 'scalar_tensor_tensor', [Ry, RVr, R_vec], [Ry], out=ysb, in0=ysb, scalar=vec[:, 46 + j:47 + j], in1=Vr, op0=ALU.add, op1=ALU.add)
                bg = nb()
                MM(ps[bg][:, :], gupb[:, j * 128:(j + 1) * 128], sgd, True, True, [R_gup, R_pc[P_B]], [R_ps[bg]])
                OP('dve', 'tensor_tensor', [R_ps[bg], Ry], [R_abf[j]], out=abf[:, j, :], in0=ps[bg][:, :], in1=ysb, op=ALU.mult)

            CK(5)
            auto_pump['w'] = 0.0
            auto_pump['fn'] = None
            for _ in ag:
                pass
            bank_pool[:] = list(range(7))

            CK(6)
            def proj8(seq_base, src_reads_k, rhs_k, evac):
                for c in range(8):
                    slot = use_panel(seq_base + c // 4)
                    pos = c % 4
                    b = nb()
                    for k in range(8):
                        MM(ps[b][:, :], ring[:, slot, k * 512 + pos * 128:k * 512 + pos * 128 + 128], rhs_k(k), k == 0, k == 7,
                           [R_ring[slot]] + src_reads_k(k), [R_ps[b]])
                    evac(c, b)

            h = [pc(i) for i in range(8)]
            Rh = [R_pc[i] for i in range(8)]
            for c in range(8):
                DMA('sp', f'xh{c}', h[c], xT[c * 128:(c + 1) * 128, tsl], [], [Rh[c]])

            def ev_res(c, b):
                OP('dve', 'tensor_tensor', [R_ps[b], Rh[c]], [Rh[c]], out=h[c], in0=ps[b][:, :], in1=h[c], op=ALU.add)
            proj8(seq0 + 7, lambda k: [R_abf[k]], lambda k: abf[:, k, :], ev_res)

            CK(7)
            P_SQ2, P_RS2 = 30, 31

            def norm_cast(gcol):
                rs = rms_stats(lambda k: (h[k], [Rh[k]]), None, P_SQ2, P_RS2, nb())
                for k in range(8):
                    if k % 2 == 0:
                        OP('dve', 'tensor_scalar', [Rh[k], R_vec], [R_abf[k]], out=abf[:, k, :], in0=h[k],
                           scalar1=vec[:, gcol + k:gcol + k + 1], scalar2=None, op0=ALU.mult)
                    else:
                        OP('act', 'activation', [Rh[k], R_vec], [R_abf[k]], out=abf[:, k, :], in_=h[k], func=AF.Copy, scale=vec[:, gcol + k:gcol + k + 1])
                return rs
            rs2 = norm_cast(51)
            qm = pcb(24, 4).rearrange('p (c t) -> p c t', t=512)
            Rqm = Rp(24, 4)

            def ev_qm(c, b):
                OP('dve', 'scalar_tensor_tensor', [R_ps[b], R_pc[P_RS2]], Rqm, out=qm[:, c, :], in0=ps[b][:, :], scalar=1.0 / 16, in1=rs2,
                   op0=ALU.mult, op1=ALU.mult)
            proj8(seq0 + 9, lambda k: [R_abf[k]], lambda k: abf[:, k, :], ev_qm)

            CK(7.3)
            pm = pcb(28, 2).rearrange('p (m t) -> p m t', t=512)
            for hd in range(4):
                for mb in range(2):
                    b = nb()
                    for dc in range(2):
                        MM(ps[b][:, :], KmT[:, hd * 2 + dc, mb * 128:(mb + 1) * 128], qm[:, hd * 2 + dc, :], dc == 0, dc == 1, [R_KmT] + Rqm, [R_ps[b]])
                    OP('act', 'activation', [R_ps[b]], [R_pc[28 + mb // 2]], out=pm[:, mb, :], in_=ps[b][:, :], func=AF.Exp)
                bl = nb()
                for mb in range(2):
                    MM(ps[bl][:, :], ones1, pm[:, mb, :], mb == 0, mb == 1, [R_cb, R_pc[28]], [R_ps[bl]])
                rlm = pc(29)
                OP('act', 'activation', [R_ps[bl]], [R_pc[29]], out=rlm, in_=ps[bl][:, :], func=AF.Ln)
                OP('act', 'activation', [R_pc[29]], [R_pc[29]], out=rlm, in_=rlm, func=AF.Exp, scale=-1.0)
                for dc in range(2):
                    b = nb()
                    for mb in range(2):
                        MM(ps[b][:, :], Vm[:, mb, (hd * 2 + dc) * 128:(hd * 2 + dc + 1) * 128], pm[:, mb, :], mb == 0, mb == 1, [R_Vm, R_pc[28]], [R_ps[b]])
                    OP('dve', 'tensor_tensor', [R_ps[b], R_pc[29]], [R_abf[hd * 2 + dc]], out=abf[:, hd * 2 + dc, :], in0=ps[b][:, :], in1=rlm, op=ALU.mult)

            CK(7.6)
            proj8(seq0 + 11, lambda k: [R_abf[k]], lambda k: abf[:, k, :], ev_res)

            CK(8)
            rs3 = norm_cast(67)
            ubf = pcb(8, 16).rearrange('p (f t) -> p f t', t=512)
            for fo in range(32):
                slot = use_panel(seq0 + 13 + fo // 4)
                pos = fo % 4
                b = nb()
                for k in range(8):
                    MM(ps[b][:, :], ring[:, slot, k * 512 + pos * 128:k * 512 + pos * 128 + 128], abf[:, k, :], k == 0, k == 7,
                       [R_ring[slot], R_abf[k]], [R_ps[b]])
                tr = pc(28 + fo % 2)
                OP('dve', 'scalar_tensor_tensor', [R_ps[b], R_pc[P_RS2]], [R_pc[28 + fo % 2]], out=tr, in0=ps[b][:, :], scalar=0.0, in1=rs3,
                   op0=ALU.max, op1=ALU.mult)
                OP('act', 'activation', [R_pc[28 + fo % 2]], [R_pc[8 + fo // 2]], out=ubf[:, fo, :], in_=tr, func=AF.Square)
            if tau + 1 < ntiles:
                load_x_tile(tau + 1)
            CK(8.5)
            for c in range(8):
                slot = use_panel(seq0 + 21 + c)
                b = nb()
                for fo in range(32):
                    MM(ps[b][:, :], ring[:, slot, fo * 128:(fo + 1) * 128], ubf[:, fo, :], fo == 0, fo == 31, [R_ring[slot], R_pc[8 + fo // 2]], [R_ps[b]])
                ev_res(c, b)
            CK(9)
            rs4 = rms_stats(lambda k: (h[k], [Rh[k]]), None, P_SQ2, P_RS2, nb())
            for c in range(8):
                OP('dve', 'scalar_tensor_tensor', [Rh[c], R_vec, R_pc[P_RS2]], [Rh[c]], out=h[c], in0=h[c],
                   scalar=vec[:, 75 + c:76 + c], in1=rs4, op0=ALU.mult, op1=ALU.mult)
                DMA('sp', f'st{c}', outT[c * 128:(c + 1) * 128, tsl], h[c], [Rh[c]], [R_out])

        try:
            CK(0)
            load_x_tile(0)
            for tau in range(ntiles):
                tile_body(tau)
        except _Stop:
            pass

        for key in [f'st{c}' for c in range(8)]:
            if key in T.dma_cnt:
                T.streams['sp'].append(('wait', 'dma:' + key, T.dma_cnt[key] * 16))
        block = es.enter_context(nc.Block())
        T.emit(nc, block, es)
    return nc


def _panels(W, pw):
    K, N = W.shape
    kc = K // 128
    out = []
    for c0 in range(0, N, pw):
        blk = W[:, c0:c0 + pw].reshape(kc, 128, pw).transpose(1, 0, 2).reshape(128, kc * pw)
        out.append(blk)
    return out


def _prep_shared(inp):
    f = lambda k: np.asarray(inp[k], dtype=np.float32)
    w_in = f('w_in')[0]
    zpad = np.zeros((1024, 256), np.float32)
    w_in_perm = np.concatenate([w_in[:, :2560], w_in[:, 2560:2816], zpad, w_in[:, 2816:3328]], axis=1)
    pans = _panels(w_in_perm, 512)
    for k in ('w_out', 'w_mq', 'w_mo'):
        pans += _panels(f(k)[0], 512)
    pans += _panels(f('w_up')[0], 512)
    pans += _panels(f('w_down')[0], 128)
    wpan = np.ascontiguousarray(np.stack(pans, 0))
    assert wpan.shape == (NPAN, 128, 4096)
    wmkv = np.stack([_panels(f('w_mk')[0], 1024)[0], _panels(f('w_mv')[0], 1024)[0]], 0)
    wlora = np.zeros((128, 1024), np.float32)
    wlora[0:64, 0:512] = f('w_decay_up')[0]
    wlora[64:128, 0:512] = f('a_up')[0]
    wlora[:, 512:1024] = f('g_up')[0]
    vec = np.zeros((128, NV), np.float32)
    col = lambda v: np.asarray(v, np.float32).reshape(-1, 128).T
    vec[:, 0:8] = col(f('norm_mix_w')[0])
    vec[:, 8:22] = col(f('mu_shift')[0])
    vec[:, 22:26] = col(f('w_decay0')[0])
    vec[:, 26:30] = col(f('a0')[0])
    vec[:, 30:34] = col(f('k_k')[0])
    vec[:, 34:38] = col(f('k_a')[0])
    vec[:, 38:42] = col(f('r_k')[0].reshape(-1))
    vec[:, 42:46] = col(f('lnx_w')[0])
    vec[:, 46:50] = col(f('lnx_b')[0])
    vec[:, 50] = f('subln_w')[0]
    vec[:, 51:59] = col(f('norm_mem_w')[0])
    vec[:, 59:67] = col(f('norm_src_w')[0])
    vec[:, 67:75] = col(f('norm_mlp_w')[0])
    vec[:, 75:83] = col(f('norm_final_w'))
    for i, k in enumerate(('lam_q1', 'lam_k1', 'lam_q2', 'lam_k2')):
        vec[0:64, 83 + i] = f(k)[0]
    cst = np.zeros((128, NCST), np.float32)
    idx = np.arange(128)
    same = (idx[:, None] // 64) == (idx[None, :] // 64)
    mLT = ((idx[:, None] < idx[None, :]) & same).astype(np.float32)
    mInc = ((idx[:, None] <= idx[None, :]) & same).astype(np.float32)
    cst[:, CI:CI + 128] = np.eye(128, dtype=np.float32)
    cst[:, CM5:CM5 + 640] = np.concatenate([mLT.T, mLT, mInc, mLT, mInc], axis=1)
    rs = np.ones(512, np.float32)
    rs[::64] = 0.0
    cst[:, CRS:CRS + 512] = rs[None, :]
    cst[:, CBO:CBO + 128] = same.astype(np.float32)
    return dict(wpan=wpan, wmkv=np.ascontiguousarray(wmkv), wlora=wlora, vecs=vec, cst=cst)


_NC_CACHE = {}
MARKS = []


def kernel(**inputs):
    x = np.asarray(inputs['x'], dtype=np.float32)
    mem = np.asarray(inputs['mem'], dtype=np.float32)
    shared = _prep_shared(inputs)
    B = x.shape[0]
    in_maps = []
    for b in range(B):
        m = dict(shared)
        m['xT'] = np.ascontiguousarray(x[b].T)
        m['memT'] = np.ascontiguousarray(mem[b].T)
        in_maps.append(m)
    if 'nc' not in _NC_CACHE:
        _NC_CACHE['nc'] = build_nc()
    nc = _NC_CACHE['nc']
    res = run_bass_kernel_spmd(nc, in_maps, core_ids=list(range(B)))
    out = np.stack([np.ascontiguousarray(res.results[b]['outT'].T) for b in range(B)], 0)
    return out.astype(np.float32)
```

```python
import os
import numpy as np
from contextlib import ExitStack
import concourse.bass as bass
import concourse.mybir as mybir
from concourse.bass_utils import run_bass_kernel_spmd

F32 = mybir.dt.float32
BF16 = mybir.dt.bfloat16
AF = mybir.ActivationFunctionType
ALU = mybir.AluOpType
ENGS = ('pe', 'act', 'dve', 'pool', 'sp')

S = 4096
D = 1024
TT_ = 512
NT = S // TT_
NV = 87
C0 = float(np.exp(-0.5))
CI, CM5, CRS, CBO, NCST = 0, 128, 768, 1280, 1408
NPAN = 29
WIN_POS = [(c // 4, c % 4) for c in range(16)] + [(4, 0), (4, 1), (4, 2), (4, 3), (5, 0), (5, 1),
                                                    (6, 0), (6, 1), (6, 2), (6, 3)]


class Res:
    __slots__ = ('name', 'w', 'rs')

    def __init__(self, name=''):
        self.name = name
        self.w = None
        self.rs = {}


class Trk:
    def __init__(self):
        self.streams = {e: [] for e in ENGS}
        self.cnt = {e: 0 for e in ENGS}
        self.known = {e: {} for e in ENGS}
        self.dma_cnt = {}

    def _deps(self, eng, reads, writes):
        deps = {}
        for r in reads:
            if r.w is not None and deps.get(r.w[0], 0) < r.w[1]:
                deps[r.w[0]] = r.w[1]
        for r in writes:
            if r.w is not None and deps.get(r.w[0], 0) < r.w[1]:
                deps[r.w[0]] = r.w[1]
            for k, i in r.rs.items():
                if deps.get(k, 0) < i:
                    deps[k] = i
        kn = self.known[eng]
        for k, i in deps.items():
            if k == eng and eng in ('pe', 'sp'):
                continue
            if kn.get(k, 0) >= i:
                continue
            kn[k] = i
            self.streams[eng].append(('wait', k, i))

    def op(self, eng, fn, reads=(), writes=()):
        self._deps(eng, reads, writes)
        self.cnt[eng] += 1
        i = self.cnt[eng]
        self.streams[eng].append(('op', fn))
        for r in reads:
            if r.rs.get(eng, 0) < i:
                r.rs[eng] = i
        for r in writes:
            r.w = (eng, i)
            r.rs = {}

    def dma(self, eng, semkey, fn, reads=(), writes=()):
        self._deps(eng, reads, writes)
        self.dma_cnt[semkey] = self.dma_cnt.get(semkey, 0) + 1
        k = 'dma:' + semkey
        i = self.dma_cnt[semkey] * 16
        self.streams[eng].append(('dma', fn, semkey))
        for r in reads:
            if r.rs.get(k, 0) < i:
                r.rs[k] = i
        for r in writes:
            r.w = (k, i)
            r.rs = {}

    def emit(self, nc, block, es):
        sems = {}
        for e in ENGS:
            sems[e] = es.enter_context(nc.semaphore('sem_' + e))
        for k in self.dma_cnt:
            sems['dma:' + k] = es.enter_context(nc.semaphore('dsem_' + k))
        streams = self.streams

        def run(engname, eng):
            own = sems[engname]
            for ent in streams[engname]:
                if ent[0] == 'wait':
                    eng.wait_ge(sems[ent[1]], ent[2])
                elif ent[0] == 'op':
                    ent[1](eng).then_inc(own, 1)
                else:
                    ent[1](eng).then_inc(sems['dma:' + ent[2]], 16)

        block.tensor(lambda eng: run('pe', eng))
        block.scalar(lambda eng: run('act', eng))
        block.vector(lambda eng: run('dve', eng))
        block.gpsimd(lambda eng: run('pool', eng))
        block.sync(lambda eng: run('sp', eng))


def build_nc(ntiles=NT, dbg=None, stage_limit=99):
    nc = bass.Bass("TRN2", target_bir_lowering=False)
    xT = nc.dram_tensor("xT", [D, S], F32, kind="ExternalInput").ap()
    memT = nc.dram_tensor("memT", [D, 256], F32, kind="ExternalInput").ap()
    wpan = nc.dram_tensor("wpan", [NPAN, 128, 4096], F32, kind="ExternalInput").ap()
    wmkv = nc.dram_tensor("wmkv", [2, 128, 8192], F32, kind="ExternalInput").ap()
    wlora = nc.dram_tensor("wlora", [128, 1024], F32, kind="ExternalInput").ap()
    vecs = nc.dram_tensor("vecs", [128, NV], F32, kind="ExternalInput").ap()
    cst = nc.dram_tensor("cst", [128, NCST], F32, kind="ExternalInput").ap()
    outT = nc.dram_tensor("outT", [D, S], F32, kind="ExternalOutput").ap()
    wbf = nc.dram_tensor("wbf", [NPAN, 128, 4096], BF16).ap()
    dbg_out = None
    if dbg:
        dbg_out = nc.dram_tensor("dbg", [128, 16, 512], F32, kind="ExternalOutput").ap()

    T = Trk()
    es = ExitStack()
    with es:
        def sb(n, s, d):
            return es.enter_context(nc.sbuf_tensor(n, s, d))

        vec = sb('vec', [128, NV], F32)
        dv = sb('dvec', [128, 32], F32)
        c32 = sb('c32', [128, 128], F32)
        m5b = sb('m5b', [128, 640], BF16)
        rsb = sb('rsb', [128, 512], BF16)
        cb = sb('cb', [128, 6, 128], BF16)
        Kc = sb('Kc', [128, 4, S], BF16)
        Vc = sb('Vc', [128, S // 128, 512], BF16)
        KmT = sb('KmT', [128, 8, 256], BF16)
        Vm = sb('Vm', [128, 2, 1024], BF16)
        lorab = sb('lorab', [128, 512], BF16)
        gupb = sb('gupb', [128, 512], BF16)
        ring = sb('ring', [128, 3, 4096], BF16)
        abf = sb('abf', [128, 8, 512], BF16)
        NPIECE = 36
        AR = sb('AR', [128, NPIECE * 512], F32)
        SM = sb('SM', [128, 8, 1408], BF16)
        carry = sb('carry', [128, 16], F32)
        ST = sb('ST', [128, 2, 4, 64], BF16)
        wcb = sb('wcb', [128, 4, 8], F32)
        rtok = sb('rtok', [128, 4], F32)
        ones32 = sb('ones32', [128, 128], F32)
        ps = [es.enter_context(nc.psum_tensor(f'ps{i}', [128, 512], F32)) for i in range(8)]

        R_vec, R_dv, R_c32, R_cb = Res('vec'), Res('dv'), Res('c32'), Res('cb')
        R_Kc = [Res(f'Kc{t}') for t in range(NT)]
        R_Vc = [Res(f'Vc{t}') for t in range(NT)]
        R_KmT, R_Vm, R_lora, R_gup = Res('KmT'), Res('Vm'), Res('lora'), Res('gup')
        R_ring = [Res(f'ring{i}') for i in range(3)]
        R_abf = [Res(f'abf{i}') for i in range(8)]
        R_pc = [Res(f'pc{i}') for i in range(NPIECE)]
        R_carry, R_wcb, R_rtok = Res('carry'), Res('wcb'), Res('rtok')
        R_ST = [[[Res(f'ST{p}_{j}_{h}') for h in range(2)] for j in range(4)] for p in range(2)]
        st_par = [[0, 0] for _ in range(4)]
        R_ps = [Res(f'ps{i}') for i in range(8)]
        R_wbf = [Res(f'wbf{i}') for i in range(NPAN)]
        R_out = Res('out')

        auto_pump = {'w': 0.0, 'fn': None, 'busy': False}

        def _ap():
            if auto_pump['w'] > 0 and auto_pump['fn'] is not None and not auto_pump['busy']:
                auto_pump['busy'] = True
                auto_pump['fn'](auto_pump['w'])
                auto_pump['busy'] = False

        def OP(eng, meth, reads, writes, *a, **k):
            T.op(eng, (lambda e: getattr(e, meth)(*a, **k)), reads, writes)
            _ap()

        def MM(out, lhsT, rhs, start, stop, reads, writes):
            T.op('pe', (lambda e: e.matmul(out, lhsT=lhsT, rhs=rhs, start=start, stop=stop)), reads, writes)
            if stop:
                _ap()

        def DMA(eng, key, out, in_, reads, writes):
            T.dma(eng, key, (lambda e: e.dma_start(out=out, in_=in_)), reads, writes)

        def pc(i, n=1):
            return AR[:, i * 512:(i + n) * 512]

        def pcb(i, n=1):
            return AR[:, i * 512:(i + n) * 512].bitcast(BF16)

        def Rp(i, n=1):
            return R_pc[i:i + n]

        bank_ctr = [0]

        bank_pool = list(range(7))

        def nb():
            b = bank_pool[bank_ctr[0] % len(bank_pool)]
            bank_ctr[0] += 1
            return b

        identb, blk1, blk64, ones1024, ones128, ones1 = [cb[:, i, :] for i in range(6)]
        ident32 = c32[:, CI:CI + 128]
        eps_rms = dv[:, 20:21]
        eps_gn = dv[:, 21:22]

        DMA('sp', 'ld0', vec[:], vecs, [], [R_vec])
        DMA('sp', 'ld1', c32[:], cst[:, 0:128], [], [R_c32])
        DMA('pool', 'ld5a', m5b[:], cst[:, CM5:CM5 + 640], [], [R_c32])
        DMA('pool', 'ld5b', rsb[:], cst[:, CRS:CRS + 512], [], [R_c32])
        DMA('pool', 'ld5c', cb[:, 1, :], cst[:, CBO:CBO + 128], [], [R_cb])
        SKIP = []

        def cast_panel(pi, deps=()):
            key = f'wc{pi}' if pi < 7 else ('wcB' if pi < 13 else 'wcC')
            DMA('pool', key, wbf[pi], wpan[pi], list(deps), [R_wbf[pi]])

        DMA('pool', 'ld2', lorab[:], wlora[:, 0:512], [], [R_lora])
        DMA('pool', 'ld3', gupb[:], wlora[:, 512:1024], [], [R_gup])
        wkv = [pcb(8, 8), pcb(16, 8)]
        DMA('pool', 'ld4a', wkv[0], wmkv[0], [], Rp(8, 8))
        DMA('pool', 'ld4b', wkv[1], wmkv[1], [], Rp(16, 8))
        for pi in range(13):
            cast_panel(pi)
        OP('dve', 'tensor_copy', [R_c32], [R_cb], out=identb, in_=ident32)
        OP('dve', 'tensor_scalar', [R_cb], [R_cb], out=blk64, in0=blk1, scalar1=1.0 / 64, scalar2=None,
           op0=ALU.mult)
        OP('pool', 'memset', [], [R_cb], ones1024, 1.0 / 1024)
        OP('pool', 'memset', [], [R_cb], ones128, 1.0 / 128)
        OP('pool', 'memset', [], [R_cb], ones1, 1.0)
        OP('pool', 'memset', [], [R_cb], ones32[:], 1.0)
        OP('dve', 'tensor_scalar', [R_vec], [R_dv], out=dv[:, 0:14], in0=vec[:, 8:22], scalar1=-1.0, scalar2=1.0,
           op0=ALU.mult, op1=ALU.add)
        OP('dve', 'tensor_scalar', [R_vec], [R_dv], out=dv[:, 14:18], in0=vec[:, 34:38], scalar1=-1.0, scalar2=1.0,
           op0=ALU.mult, op1=ALU.add)
        OP('dve', 'tensor_scalar', [R_vec], [R_dv], out=dv[:, 18:19], in0=vec[:, 50:51], scalar1=0.8, scalar2=None,
           op0=ALU.mult)
        OP('pool', 'memset', [], [R_dv], dv[:, 20:21], 1e-5)
        OP('pool', 'memset', [], [R_dv], dv[:, 21:22], 64e-5)
        OP('pool', 'memset', [], [R_carry], carry[:], 0.0)
        OP('pool', 'memset', [], [r for a in R_ST for b in a for r in b], ST[:], 0.0)
        OP('dve', 'tensor_tensor', [R_vec], [R_dv], out=dv[:, 22:23], in0=vec[:, 83:84], in1=vec[:, 84:85], op=ALU.mult)
        OP('dve', 'tensor_tensor', [R_vec], [R_dv], out=dv[:, 23:24], in0=vec[:, 85:86], in1=vec[:, 86:87], op=ALU.mult)
        OP('pool', 'memset', [], [R_pc[0]], pc(0)[:, 0:128], 1.0)
        if 'lam' not in SKIP:
          MM(ps[0][:, 0:2], pc(0)[:, 0:128], dv[:, 22:24], True, True, [R_pc[0], R_dv], [R_ps[0]])
        OP('act', 'activation', [R_ps[0]], [R_dv], out=dv[:, 24:26], in_=ps[0][:, 0:2], func=AF.Exp)
        OP('dve', 'tensor_tensor', [R_dv], [R_dv], out=dv[:, 26:27], in0=dv[:, 25:26], in1=dv[:, 24:25], op=ALU.subtract)
        OP('dve', 'tensor_scalar', [R_dv], [R_dv], out=dv[:, 19:20], in0=dv[:, 26:27], scalar1=-0.2, scalar2=None,
           op0=ALU.add)

        if 'mem' in SKIP:
            stage_limit = -1
        m32 = pc(0, 4).rearrange('p (k m) -> p k m', m=256)
        DMA('sp', 'ld2b', m32, memT.rearrange('(k p) m -> p k m', p=128), [], Rp(0, 4))
        msq = pcb(6, 1)
        msq = pcb(6, 2).rearrange('p (k m) -> p k m', m=256)
        memn = pcb(4, 2).rearrange('p (k m) -> p k m', m=256)
        OP('act', 'activation', Rp(0, 4), Rp(6, 2), out=msq, in_=m32, func=AF.Square)
        for k in range(8):
            MM(ps[1][:, 0:256], ones1024, msq[:, k, :], k == 0, k == 7, [R_cb] + Rp(6, 2), [R_ps[1]])
        mr = pc(24)[:, 0:256]
        OP('act', 'activation', [R_ps[1], R_dv], [R_pc[24]], out=mr, in_=ps[1][:, 0:256], func=AF.Ln, bias=eps_rms, scale=1.0)
        OP('act', 'activation', [R_pc[24]], [R_pc[24]], out=mr, in_=mr, func=AF.Exp, scale=-0.5)
        for k in range(8):
            OP('dve', 'scalar_tensor_tensor', Rp(0, 4) + [R_pc[24], R_vec], Rp(4, 2), out=memn[:, k, :], in0=m32[:, k, :],
               scalar=vec[:, 59 + k:60 + k], in1=mr, op0=ALU.mult, op1=ALU.mult)
        for which in range(2):
            wv = wkv[which].rearrange('p (k n) -> p k n', n=1024)
            Rw = Rp(8 + 8 * which, 8)
            if which == 0:
                for c in range(8):
                    b = nb()
                    for k in range(8):
                        MM(ps[b][:, 0:256], wv[:, k, c * 128:(c + 1) * 128], memn[:, k, :], k == 0, k == 7,
                           Rw + Rp(4, 2), [R_ps[b]])
                    OP('act', 'activation', [R_ps[b]], [R_KmT], out=KmT[:, c, :], in_=ps[b][:, 0:256], func=AF.Copy)
            else:
                for mb in range(2):
                    for nh in range(2):
                        b = nb()
                        for k in range(8):
                            MM(ps[b][:, :], memn[:, k, mb * 128:(mb + 1) * 128], wv[:, k, nh * 512:(nh + 1) * 512], k == 0, k == 7,
                               Rw + Rp(4, 2), [R_ps[b]])
                        OP('act', 'activation', [R_ps[b]], [R_Vm], out=Vm[:, mb, nh * 512:(nh + 1) * 512], in_=ps[b][:, :], func=AF.Copy)

        ring_state = {'next': 0}
        panel_slot = {}

        def load_panel(seq):
            if seq >= ntiles * NPAN or seq in panel_slot:
                return
            assert seq == ring_state['next']
            ring_state['next'] += 1
            slot = seq % 3
            panel_slot[seq] = slot
            pi = seq % NPAN
            gl = pi if pi < 7 else (12 if pi < 13 else NPAN - 1)
            DMA('sp', f'ring{slot}', ring[:, slot, :], wbf[pi], [R_wbf[gl]], [R_ring[slot]])

        def use_panel(seq):
            load_panel(seq)
            load_panel(seq + 1)
            load_panel(seq + 2)
            return panel_slot[seq]

        P_PRW = 0
        P_F = 14
        P_B = 21
        P_TOK = 27
        P_YSB = 16
        P_TMP = 29
        P_RSTD = 31

        def rms_stats(src_chunks_fn, nsrc_reads, sq_piece, rstd_piece, bank):
            sqv = pcb(sq_piece).rearrange('p (h t) -> p h t', t=512)
            for k in range(8):
                src, rr = src_chunks_fn(k)
                OP('act', 'activation', rr, [R_pc[sq_piece]], out=sqv[:, k % 2, :], in_=src, func=AF.Square)
                MM(ps[bank][:, :], ones1024, sqv[:, k % 2, :], k == 0, k == 7, [R_cb, R_pc[sq_piece]], [R_ps[bank]])
            rs = pc(rstd_piece)
            OP('act', 'activation', [R_ps[bank], R_dv], [R_pc[rstd_piece]], out=rs, in_=ps[bank][:, :], func=AF.Ln, bias=eps_rms, scale=1.0)
            OP('act', 'activation', [R_pc[rstd_piece]], [R_pc[rstd_piece]], out=rs, in_=rs, func=AF.Exp, scale=-0.5)
            return rs

        R_sm = [[Res(f'sm{ci}_{n}') for n in range(13)] for ci in range(8)]
        R_pt = [Res(f'pt{i}') for i in range(4)]
        R_kk2rk = (Res('kk2'), Res('rk'))
        R_wcbj = [Res(f'wcb{j}') for j in range(4)]

        class _Stop(Exception):
            pass

        def CK(n):
            MARKS.append((n, dict(T.cnt)))
            if stage_limit <= n:
                raise _Stop()

        P_XS, P_XSQ, P_XR = 24, 26, 27

        def load_x_tile(tau):
            tsl = slice(tau * 512, (tau + 1) * 512)
            sqv = pcb(P_XSQ).rearrange('p (h t) -> p h t', t=512)
            bank_ms = nb()
            for k in range(8):
                stg = pc(P_XS + (k % 2))
                DMA('sp', f'xs{k % 2}', stg, xT[k * 128:(k + 1) * 128, tsl], [], [R_pc[P_XS + (k % 2)]])
                OP('act', 'activation', [R_pc[P_XS + (k % 2)]], [R_pc[P_XSQ]], out=sqv[:, k % 2, :], in_=stg, func=AF.Square)
                MM(ps[bank_ms][:, :], ones1024, sqv[:, k % 2, :], k == 0, k == 7, [R_cb, R_pc[P_XSQ]], [R_ps[bank_ms]])
                if k % 2 == 0:
                    OP('dve', 'tensor_scalar', [R_pc[P_XS + (k % 2)], R_vec], [R_abf[k]], out=abf[:, k, :], in0=stg,
                       scalar1=vec[:, k:k + 1], scalar2=None, op0=ALU.mult)
                else:
                    OP('act', 'activation', [R_pc[P_XS + (k % 2)], R_vec], [R_abf[k]], out=abf[:, k, :], in_=stg, func=AF.Copy, scale=vec[:, k:k + 1])
            rstd = pc(P_XR)
            OP('act', 'activation', [R_ps[bank_ms], R_dv], [R_pc[P_XR]], out=rstd, in_=ps[bank_ms][:, :], func=AF.Ln, bias=eps_rms, scale=1.0)
            OP('act', 'activation', [R_pc[P_XR]], [R_pc[P_XR]], out=rstd, in_=rstd, func=AF.Exp, scale=-0.5)
            bt_ = nb()
            for tb in range(4):
                MM(ps[bt_][:, tb:tb + 1], rstd[:, tb * 128:(tb + 1) * 128], ident32[:, 0:1], True, True, [R_pc[P_XR], R_c32], [R_ps[bt_]])
            OP('dve', 'tensor_copy', [R_ps[bt_]], [R_rtok], out=rtok[:, :], in_=ps[bt_][:, 0:4])

        def tile_body(tau):
            tsl = slice(tau * 512, (tau + 1) * 512)
            seq0 = tau * NPAN
            use_panel(seq0)
            rstd = pc(P_XR)
            CK(1)
            qbf = pcb(P_TMP, 2).rearrange('p (c t) -> p c t', t=512)
            R_qbf = Rp(P_TMP, 2)
            for c in range(22):
                pan, pos = WIN_POS[c]
                slot = use_panel(seq0 + pan)
                b = nb()
                for k in range(8):
                    MM(ps[b][:, :], ring[:, slot, k * 512 + pos * 128:k * 512 + pos * 128 + 128], abf[:, k, :], k == 0, k == 7,
                       [R_ring[slot], R_abf[k]], [R_ps[b]])
                if c < 14:
                    p = pc(P_PRW + c)
                    Rc = R_pc[P_PRW + c]
                    OP('dve', 'tensor_tensor', [R_ps[b], R_pc[P_XR]], [Rc], out=p, in0=ps[b][:, :], in1=rstd, op=ALU.mult)
                    tp = P_TMP + (c % 2)
                    tmp = pc(tp)
                    OP('act', 'activation', [Rc, R_vec], [R_pc[tp]], out=tmp[:, 1:512], in_=p[:, 0:511], func=AF.Copy, scale=vec[:, 8 + c:9 + c])
                    OP('act', 'activation', [R_carry, R_vec], [R_pc[tp]], out=tmp[:, 0:1], in_=carry[:, c:c + 1], func=AF.Copy, scale=vec[:, 8 + c:9 + c])
                    OP('act', 'activation', [Rc], [R_carry], out=carry[:, c:c + 1], in_=p[:, 511:512], func=AF.Copy)
                    OP('dve', 'scalar_tensor_tensor', [Rc, R_pc[tp], R_dv], [Rc], out=p, in0=p, scalar=dv[:, c:c + 1], in1=tmp,
                       op0=ALU.mult, op1=ALU.add)
                elif c < 18:
                    OP('dve', 'scalar_tensor_tensor', [R_ps[b], R_pc[P_XR]], R_qbf, out=qbf[:, c - 14, :], in0=ps[b][:, :], scalar=0.125,
                       in1=rstd, op0=ALU.mult, op1=ALU.mult)
                else:
                    OP('dve', 'tensor_tensor', [R_ps[b], R_pc[P_XR]], [R_Kc[tau]], out=Kc[:, c - 18, tsl], in0=ps[b][:, :], in1=rstd, op=ALU.mult)
            slot = use_panel(seq0 + 6)
            for tb in range(4):
                b = nb()
                for k in range(8):
                    MM(ps[b][:, :], abf[:, k, tb * 128:(tb + 1) * 128], ring[:, slot, k * 512:(k + 1) * 512], k == 0, k == 7,
                       [R_ring[slot], R_abf[k]], [R_ps[b]])
                OP('act', 'activation', [R_ps[b], R_rtok], [R_Vc[tau]], out=Vc[:, tau * 4 + tb, :], in_=ps[b][:, :], func=AF.Copy,
                   scale=rtok[:, tb:tb + 1])

            if tau == 0:
                for pi in range(13, NPAN):
                    cast_panel(pi, deps=[R_Kc[0], R_Vc[0]])
            nkb = 4 * tau + 4
            ot1, ot2, rl = pc(34), pc(35), pc(31)
            accL = [pc(32), pc(33)]
            R_ot1, R_ot2, R_rl = R_pc[34], R_pc[35], R_pc[31]
            R_acc = [R_pc[32], R_pc[33]]
            bO = [4, 5]
            bST = [6, 7]
            st_ctr = [0]

            def attn_gen():
                for hd in range(4):
                    units = [(kb, m) for kb in range(nkb) for m in range(2)]
                    LA = 1
                    uinfo = {}

                    def u_qk(i):
                        kb, m = units[i]
                        qoff = max(0, kb - 4 * tau) * 128
                        ktile = kb // 4
                        if PAIR:
                            bs = bST[m]
                            ip = (kb % 2) * 2 + m
                        else:
                            bs = bST[st_ctr[0] % 2]
                            ip = st_ctr[0] % 4
                            st_ctr[0] += 1
                        mp = slice(64 * m, 64 * m + 64)
                        MM(ps[bs][:, qoff:512], Kc[mp, hd, kb * 128:(kb + 1) * 128], qbf[mp, hd, qoff:512], True, True, [R_Kc[ktile]] + R_qbf, [R_ps[bs]])
                        pt = pcb(P_PRW + 12 + (ip // 2))[:, (ip % 2) * 512:(ip % 2) * 512 + 512]
                        Rpt = R_pt[ip]
                        uinfo[i] = (pt, Rpt, qoff, ktile, bs)
                        if not PAIR:
                            u_exp(i)

                    def u_exp(i):
                        kb, m = units[i]
                        pt, Rpt, qoff, ktile, bs = uinfo[i]
                        OP('act', 'activation', [R_ps[bs]], [Rpt], out=pt[:, qoff:512], in_=ps[bs][:, qoff:512], func=AF.Exp)
                        if kb >= 4 * tau:
                            OP('dve', 'memset', [], [Rpt], pt[64:128, qoff:qoff + 64], 0.0)

                    def u_pv(i):
                        kb, m = units[i]
                        pt, Rpt, qoff, ktile, _bs = uinfo.pop(i)
                        MM(ps[bO[m]][:, qoff:512], Vc[:, kb, hd * 128:(hd + 1) * 128], pt[:, qoff:512], kb == 0, kb == nkb - 1, [R_Vc[ktile], Rpt], [R_ps[bO[m]]])
                        if kb == 0:
                            OP(ACC_ENG, 'tensor_copy', [Rpt], [R_acc[m]], out=accL[m], in_=pt)
                        else:
                            OP(ACC_ENG, 'tensor_tensor', [Rpt, R_acc[m]], [R_acc[m]], out=accL[m][:, qoff:512], in0=accL[m][:, qoff:512], in1=pt[:, qoff:512], op=ALU.add)

                    if PAIR:
                        for kb_ in range(nkb + 1):
                            if kb_ < nkb:
                                u_qk(2 * kb_)
                                u_qk(2 * kb_ + 1)
                                u_exp(2 * kb_)
                                u_exp(2 * kb_ + 1)
                            if kb_ >= 1:
                                u_pv(2 * kb_ - 2)
                                u_pv(2 * kb_ - 1)
                            yield
                            yield
                    else:
                        for i in range(len(units) + LA):
                            if i < len(units):
                                u_qk(i)
                            if i >= LA:
                                u_pv(i - LA)
                            yield
                    bl0 = bST[st_ctr[0] % 2]
                    st_ctr[0] += 1
                    MM(ps[bl0][:, :], ones32[:], accL[0], True, True, [R_cb, R_acc[0]], [R_ps[bl0]])
                    OP('act', 'activation', [R_ps[bl0]], [R_rl], out=rl, in_=ps[bl0][:, :], func=AF.Ln)
                    OP('act', 'activation', [R_rl], [R_rl], out=rl, in_=rl, func=AF.Exp, scale=-1.0)
                    OP('dve', 'tensor_tensor', [R_ps[bO[0]], R_rl], [R_ot1], out=ot1, in0=ps[bO[0]][:, :], in1=rl, op=ALU.mult)
                    yield
                    bl1 = bST[st_ctr[0] % 2]
                    st_ctr[0] += 1
                    MM(ps[bl1][:, :], ones32[:], accL[1], True, True, [R_cb, R_acc[1]], [R_ps[bl1]])
                    OP('act', 'activation', [R_ps[bl1]], [R_rl], out=rl, in_=ps[bl1][:, :], func=AF.Ln)
                    OP('act', 'activation', [R_rl], [R_rl], out=rl, in_=rl, func=AF.Exp, scale=-1.0)
                    OP('dve', 'tensor_scalar', [R_rl, R_dv], [R_rl], out=rl, in0=rl, scalar1=dv[:, 19:20], scalar2=None, op0=ALU.mult)
                    OP('dve', 'tensor_tensor', [R_ps[bO[1]], R_rl], [R_ot2], out=ot2, in0=ps[bO[1]][:, :], in1=rl, op=ALU.mult)
                    yield
                    OP('dve', 'tensor_tensor', [R_ot1, R_ot2], [R_ot1], out=ot1, in0=ot1, in1=ot2, op=ALU.add)
                    osq = rl.bitcast(BF16)[:, 0:512]
                    OP('act', 'activation', [R_ot1], [R_rl], out=osq, in_=ot1, func=AF.Square)
                    bq = bST[st_ctr[0] % 2]
                    st_ctr[0] += 1
                    MM(ps[bq][:, :], ones128, osq, True, True, [R_cb, R_rl], [R_ps[bq]])
                    OP('act', 'activation', [R_ps[bq], R_dv], [R_ot2], out=ot2, in_=ps[bq][:, :], func=AF.Ln, bias=eps_rms, scale=1.0)
                    OP('act', 'activation', [R_ot2], [R_ot2], out=ot2, in_=ot2, func=AF.Exp, scale=-0.5)
                    OP('dve', 'scalar_tensor_tensor', [R_ot1, R_ot2, R_dv], [R_abf[4 + hd]], out=abf[:, 4 + hd, :], in0=ot1, scalar=dv[:, 18:19],
                       in1=ot2, op0=ALU.mult, op1=ALU.mult)
                    yield

            ag = attn_gen()
            n_attn_steps = 4 * (2 * nkb + 1 + 3)
            WP = 1.0
            ACC_ENG = 'dve'
            PAIR = True
            BURST = 1.0
            pump_rate = 1.08 * n_attn_steps / (4 * (45 * WP + 14))
            pump_acc = [0.0]

            def pump(w=1.0):
                pump_acc[0] += w * pump_rate
                was = auto_pump['busy']
                auto_pump['busy'] = True
                if pump_acc[0] >= BURST:
                    while pump_acc[0] >= 1.0:
                        pump_acc[0] -= 1.0
                        next(ag, None)
                auto_pump['busy'] = was

            auto_pump['fn'] = pump

            bank_pool[:] = [0, 1, 2]

            CK(2)
            tw = pcb(P_B)[:, 0:512]
            sgd = pcb(P_B)[:, 512:1024]
            wda = pc(P_PRW + 12)
            OP('act', 'activation', [R_pc[P_PRW + 12]], [R_pc[P_B]], out=tw[0:64, :], in_=wda[0:64, :], func=AF.Tanh)
            OP('act', 'activation', [R_pc[P_PRW + 12]], [R_pc[P_B]], out=tw[64:128, :], in_=wda[64:128, :], func=AF.Copy)
            OP('act', 'activation', [R_pc[P_PRW + 13]], [R_pc[P_B]], out=sgd, in_=pc(P_PRW + 13), func=AF.Sigmoid)
            f = [pc(P_F + i) for i in range(7)]
            Rf = [R_pc[P_F + i] for i in range(7)]
            kk2 = pcb(P_B + 1)[:, 0:512]
            rk = pcb(P_B + 1)[:, 512:1024]
            btT = pcb(P_B + 2)[:, 0:512]
            ktT = pcb(P_B + 2)[:, 512:1024]
            bhT = pcb(P_B + 3)[:, 0:512]
            khT = pcb(P_B + 3)[:, 512:1024]
            vbf = pcb(P_B + 4)[:, 0:512]
            arT = pcb(P_B + 5).rearrange('p (b s t) -> p b s t', s=2, t=128)
            arF = pcb(P_B + 5)
            Atok = pcb(P_TOK)[:, 0:512].rearrange('p (b c) -> p b c', c=128)
            Vtok = pcb(P_TOK)[:, 512:1024].rearrange('p (b c) -> p b c', c=128)
            Bhtok = pcb(P_TOK + 1)[:, 0:512].rearrange('p (b c) -> p b c', c=128)
            Khtok = pcb(P_TOK + 1)[:, 512:1024].rearrange('p (b c) -> p b c', c=128)
            R_B = [R_pc[P_B + i] for i in range(6)]
            v4 = lambda ap: ap.rearrange('p (b t) -> p b t', t=128)
            v8 = lambda ap: ap.rearrange('p (c t) -> p c t', t=64)
            R_kk2, R_rk = R_kk2rk
            R_wj = R_wcbj

            def prepA(j):
                Kr = pc(P_PRW + 4 + j)
                RKr = R_pc[P_PRW + 4 + j]
                bd = nb()
                MM(ps[bd][:, :], lorab[0:64, j * 128:(j + 1) * 128], tw[0:64, :], True, True, [R_lora, R_pc[P_B]], [R_ps[bd]])
                ba = nb()
                MM(ps[ba][:, :], lorab[64:128, j * 128:(j + 1) * 128], tw[64:128, :], True, True, [R_lora, R_pc[P_B]], [R_ps[ba]])
                OP('act', 'activation', [R_ps[bd], R_vec], [Rf[0]], out=f[0], in_=ps[bd][:, :], func=AF.Sigmoid, bias=vec[:, 22 + j:23 + j], scale=1.0)
                OP('act', 'activation', [R_ps[ba], R_vec], [Rf[4]], out=f[4], in_=ps[ba][:, :], func=AF.Sigmoid, bias=vec[:, 26 + j:27 + j], scale=1.0)
                yield
                OP('act', 'activation', [RKr, R_vec], [Rf[5]], out=f[5], in_=Kr, func=AF.Copy, scale=vec[:, 30 + j:31 + j])
                OP('dve', 'tensor_tensor_scan', [Rf[0], R_c32], [Rf[1]], out=f[1], data0=rsb[:], data1=f[0], initial=0.0,
                   op0=ALU.mult, op1=ALU.add)
                yield
                OP('act', 'activation', [Rf[5]], [R_kk2], out=kk2, in_=f[5], func=AF.Square)
                OP('dve', 'tensor_tensor', [Rf[0], Rf[1]], [Rf[2]], out=f[2], in0=f[1], in1=f[0], op=ALU.subtract)
                yield
                OP('act', 'activation', [Rf[1]], [Rf[3]], out=f[3], in_=f[1], func=AF.Exp, scale=-C0)
                bq = nb()
                MM(ps[bq][:, :], blk1, kk2, True, True, [R_cb, R_kk2], [R_ps[bq]])
                OP('dve', 'tensor_scalar', [R_ps[bq]], [Rf[6]], out=f[6], in0=ps[bq][:, :], scalar1=1e-24, scalar2=None, op0=ALU.max)
                yield
                OP('act', 'activation', [Rf[2]], [Rf[2]], out=f[2], in_=f[2], func=AF.Exp, scale=-C0)
                OP('act', 'activation', [Rf[1]], [Rf[1]], out=f[1], in_=f[1], func=AF.Exp, scale=C0)
                yield
                OP('act', 'activation', [Rf[6]], [Rf[6]], out=f[6], in_=f[6], func=AF.Ln)
                OP('dve', 'tensor_copy', [Rf[3]], [R_wj[j]], out=wcb[:, j, :], in_=v8(f[3])[:, :, 63])
                yield
                OP('act', 'activation', [Rf[6]], [Rf[6]], out=f[6], in_=f[6], func=AF.Exp, scale=-0.5)
                OP('dve', 'tensor_tensor', [Rf[1], Rf[3]], [Rf[0]], out=v8(f[0]), in0=v8(f[1]), in1=v8(f[3])[:, :, 63:64].broadcast_to([128, 8, 64]),
                   op=ALU.mult)
                yield
                OP('dve', 'tensor_tensor', [Rf[5], Rf[6]], [Rf[5]], out=f[5], in0=f[5], in1=f[6], op=ALU.mult)
                yield
                OP('dve', 'tensor_scalar', [Rf[4], R_vec, R_dv], [Rf[6]], out=f[6], in0=f[4], scalar1=vec[:, 34 + j:35 + j], scalar2=dv[:, 14 + j:15 + j],
                   op0=ALU.mult, op1=ALU.add)
                yield
                OP('dve', 'tensor_tensor', [Rf[6], RKr], [Rf[6]], out=f[6], in0=f[6], in1=Kr, op=ALU.mult)
                yield
                OP('dve', 'tensor_tensor', [Rf[5], Rf[4]], [Rf[4]], out=f[4], in0=f[5], in1=f[4], op=ALU.mult)
                yield

            def prepB(j):
                Rr, Vr = pc(P_PRW + j), pc(P_PRW + 8 + j)
                RRr, RVr = R_pc[P_PRW + j], R_pc[P_PRW + 8 + j]
                OP('dve', 'scalar_tensor_tensor', [Rf[5], Rf[2]], [R_B[5]], out=arT[:, :, 0, :], in0=v4(f[5]), scalar=-1.0, in1=v4(f[2]),
                   op0=ALU.mult, op1=ALU.mult)
                OP('act', 'activation', [RVr], [R_B[4]], out=vbf, in_=Vr, func=AF.Copy)
                OP('dve', 'tensor_tensor', [Rf[4], Rf[0]], [R_B[3]], out=bhT, in0=f[4], in1=f[0], op=ALU.mult)
                OP('dve', 'tensor_tensor', [Rf[6], Rf[0]], [R_B[3]], out=khT, in0=f[6], in1=f[0], op=ALU.mult)
                OP('dve', 'tensor_tensor', [RRr, Rf[3]], [R_B[5]], out=arT[:, :, 1, :], in0=v4(Rr), in1=v4(f[3]), op=ALU.mult)
                OP('dve', 'tensor_tensor', [Rf[4], Rf[1]], [R_B[2]], out=btT, in0=f[4], in1=f[1], op=ALU.mult)
                OP('dve', 'tensor_tensor', [Rf[6], Rf[1]], [R_B[2]], out=ktT, in0=f[6], in1=f[1], op=ALU.mult)
                OP('dve', 'scalar_tensor_tensor', [RRr, R_vec, Rf[6]], [R_rk], out=rk, in0=Rr, scalar=vec[:, 38 + j:39 + j], in1=f[6],
                   op0=ALU.mult, op1=ALU.mult)
                for (src, rsrc, dst) in ((None, R_B[5], Atok), (vbf, R_B[4], Vtok), (bhT, R_B[3], Bhtok), (khT, R_B[3], Khtok)):
                    b = nb()
                    pvb = ps[b][:, :].bitcast(BF16)
                    for blk in range(4):
                        s_ap = arF[:, blk * 256:blk * 256 + 128] if src is None else src[:, blk * 128:(blk + 1) * 128]
                        T.op('pe', (lambda e, o=pvb[:, blk * 128:(blk + 1) * 128], i=s_ap: e.transpose(o, i, identb)), [rsrc, R_cb], [R_ps[b]])
                    rdst = R_pc[P_TOK] if (dst is Atok or dst is Vtok) else R_pc[P_TOK + 1]
                    OP('act', 'activation', [R_ps[b]], [rdst], out=dst, in_=pvb[:, 0:512].rearrange('p (b c) -> p b c', c=128), func=AF.Copy)

            auto_pump['w'] = WP
            for _ in prepA(0):
                pass
            for j in range(4):
                Vr = pc(P_PRW + 8 + j)
                RVr = R_pc[P_PRW + 8 + j]
                auto_pump['w'] = WP
                prepB(j)
                gA = prepA(j + 1) if j < 3 else iter(())

                def rpump(w=1.0, nA=2):
                    pump(w)
                    for _ in range(nA):
                        if next(gA, 'done') != 'done':
                            pump(WP)
                auto_pump['w'] = 0.0
                pump(2)
                CK(3)
                bY = 3
                if True:
                    chains = [(hh, blk) for hh in range(2) for blk in range(4)]
                    smv = [SM[:, ci, :] for ci in range(8)]

                    def RS(k, cis=range(8)):
                        return [R_sm[ci][k] for ci in cis]

                    def m_b(lo, hi, n):
                        return m5b[:, lo:hi].rearrange('p (o c) -> p o c', o=1).broadcast_to([128, n, hi - lo])

                    def psv(b, n, w):
                        return ps[b][:, 0:n * w].rearrange('p (c n) -> p c n', n=w)
                    for hh in range(2):
                        pb = 64 * hh
                        c0 = 4 * hh
                        b1 = nb()
                        for blk in range(4):
                            aT_ = arF[pb:pb + 64, blk * 256:blk * 256 + 128]
                            bT_ = btT[pb:pb + 64, blk * 128:(blk + 1) * 128]
                            MM(ps[b1][:, blk * 128:(blk + 1) * 128], aT_, bT_, True, True, [R_B[5], R_B[2]], [R_ps[b1]])
                        OP('dve', 'tensor_tensor', [R_ps[b1], R_c32], RS(10, range(c0, c0 + 4)), out=SM[:, c0:c0 + 4, 0:128], in0=psv(b1, 4, 128), in1=m_b(0, 128, 4), op=ALU.mult)
                        for (srcT, lo, rk_) in ((btT, 128, (11, 0)), (ktT, 384, (1, 12))):
                            for pr in range(2):
                                b2 = nb()
                                for q in range(2):
                                    blk = 2 * pr + q
                                    ar_ = arF[pb:pb + 64, blk * 256:blk * 256 + 256]
                                    xT_ = srcT[pb:pb + 64, blk * 128:(blk + 1) * 128]
                                    MM(ps[b2][:, q * 256:(q + 1) * 256], xT_, ar_, True, True, [R_B[5], R_B[2]], [R_ps[b2]])
                                cis = range(c0 + 2 * pr, c0 + 2 * pr + 2)
                                OP('dve', 'tensor_tensor', [R_ps[b2], R_c32], RS(rk_[0], cis) + RS(rk_[1], cis), out=SM[:, cis[0]:cis[0] + 2, lo:lo + 256], in0=psv(b2, 2, 256),
                                   in1=m_b(lo, lo + 256, 2), op=ALU.mult)
                    OP('dve', 'tensor_tensor', RS(11) + [R_cb], RS(4), out=SM[:, 0:8, 896:1024], in0=SM[:, 0:8, 128:256],
                       in1=identb.rearrange('p (o c) -> p o c', o=1).broadcast_to([128, 8, 128]), op=ALU.add)
                    bZ = nb()
                    for ci, (hh, blk) in enumerate(chains):
                        MM(ps[bZ][:, ci * 64:(ci + 1) * 64], smv[ci][:, 384:512], Vtok[:, blk, hh * 64:(hh + 1) * 64], True, True, [R_sm[ci][1], R_pc[P_TOK]], [R_ps[bZ]])
                    OP('act', 'activation', [R_ps[bZ]], RS(7), out=SM[:, 0:8, 1152:1216], in_=psv(bZ, 8, 64), func=AF.Copy)
                    rpump()
                    CK(3.1)
                    for lvl in range(5):
                        co, rcur_k = (640, (2,)) if lvl % 2 == 1 else (0, (10, 11))
                        no, rnxt_k = (640, (2,)) if (lvl + 1) % 2 == 1 else (0, (10, 11))
                        for p4 in range(4):
                            cis = (2 * p4, 2 * p4 + 1)
                            bC = nb()
                            for q, ci in enumerate(cis):
                                cur = smv[ci][:, co:co + 256]
                                rcur = [R_sm[ci][k_] for k_ in rcur_k]
                                MM(ps[bC][:, q * 256:q * 256 + 128], cur[:, 128:256], cur[:, 0:128], True, True, rcur, [R_ps[bC]])
                                MM(ps[bC][:, q * 256 + 128:q * 256 + 256], cur[:, 0:128], cur[:, 128:256], True, True, rcur, [R_ps[bC]])
                            wr = [R_sm[ci][k_] for ci in cis for k_ in rnxt_k]
                            if p4 % 2 == 0:
                                OP('act', 'activation', [R_ps[bC]], wr, out=SM[:, cis[0]:cis[0] + 2, no:no + 256], in_=psv(bC, 2, 256), func=AF.Copy)
                            else:
                                OP('dve', 'tensor_copy', [R_ps[bC]], wr, out=SM[:, cis[0]:cis[0] + 2, no:no + 256], in_=psv(bC, 2, 256))
                        tc_o, tck = (896, 4) if lvl % 2 == 0 else (1024, 5)
                        tn_o, tnk = (1024, 5) if lvl % 2 == 0 else (896, 4)
                        for p2 in range(2):
                            cis = range(4 * p2, 4 * p2 + 4)
                            bD = nb()
                            for q, ci in enumerate(cis):
                                MM(ps[bD][:, q * 128:(q + 1) * 128], smv[ci][:, no:no + 128], smv[ci][:, tc_o:tc_o + 128], True, True,
                                   [R_sm[ci][k_] for k_ in rnxt_k] + [R_sm[ci][tck]], [R_ps[bD]])
                            OP('dve', 'tensor_tensor', [R_ps[bD]] + RS(tck, cis), RS(tnk, cis), out=SM[:, cis[0]:cis[0] + 4, tn_o:tn_o + 128], in0=psv(bD, 4, 128),
                               in1=SM[:, cis[0]:cis[0] + 4, tc_o:tc_o + 128], op=ALU.add)
                        rpump()
                    CK(3.2)
                    for p2 in range(2):
                        cis = range(4 * p2, 4 * p2 + 4)
                        bF = nb()
                        for q, ci in enumerate(cis):
                            hh, blk = chains[ci]
                            MM(ps[bF][:, q * 128:q * 128 + 64], smv[ci][:, 1024:1152], Atok[:, blk, hh * 64:(hh + 1) * 64], True, True, [R_sm[ci][5], R_pc[P_TOK]], [R_ps[bF]])
                            MM(ps[bF][:, q * 128 + 64:q * 128 + 128], smv[ci][:, 1024:1152], smv[ci][:, 1152:1216], True, True, [R_sm[ci][5], R_sm[ci][7]], [R_ps[bF]])
                        OP('act', 'activation', [R_ps[bF]], RS(7, cis), out=SM[:, cis[0]:cis[0] + 4, 1152:1280], in_=psv(bF, 4, 128), func=AF.Copy)
                    rpump()
                    CK(3.3)
                    bG = nb()
                    for ci, (hh, blk) in enumerate(chains):
                        pb = 64 * hh
                        MM(ps[bG][pb:pb + 64, blk * 128:(blk + 1) * 128], smv[ci][:, 1152:1216], smv[ci][:, 256:384], True, True, [R_sm[ci][7], R_sm[ci][0]], [R_ps[bG]])
                    for hh in range(2):
                        pb = 64 * hh
                        OP('dve', 'tensor_tensor', [R_ps[bG], R_B[5]], RS(1, range(4 * hh, 4 * hh + 4)), out=SM[pb:pb + 64, 4 * hh:4 * hh + 4, 384:512],
                           in0=ps[bG][pb:pb + 64, 0:512].rearrange('p (c n) -> p c n', n=128), in1=arT[pb:pb + 64, 0:4, 1, :], op=ALU.add)
                    for c in range(2):
                        cs = slice(c * 64, (c + 1) * 64)
                        bH = nb()
                        for ci, (hh, blk) in enumerate(chains):
                            pb = 64 * hh
                            MM(ps[bH][pb:pb + 64, blk * 64:(blk + 1) * 64], smv[ci][cs, 1152:1216], Bhtok[cs, blk, hh * 64:(hh + 1) * 64],
                               True, True, [R_sm[ci][7], R_pc[P_TOK + 1]], [R_ps[bH]])
                        for ci, (hh, blk) in enumerate(chains):
                            pb = 64 * hh
                            OP('dve', 'scalar_tensor_tensor', [R_ps[bH], R_c32, R_wcbj[j]], [R_sm[ci][9]], out=smv[ci][pb:pb + 64, 1280 + c * 64:1280 + (c + 1) * 64],
                               in0=c32[pb:pb + 64, CI + pb:CI + pb + 64], scalar=wcb[pb:pb + 64, j, blk * 2 + c:blk * 2 + c + 1],
                               in1=ps[bH][pb:pb + 64, blk * 64:(blk + 1) * 64], op0=ALU.mult, op1=ALU.add)
                        bN = nb()
                        for ci, (hh, blk) in enumerate(chains):
                            pb = 64 * hh
                            MM(ps[bN][pb:pb + 64, blk * 64:(blk + 1) * 64], Bhtok[cs, blk, hh * 64:(hh + 1) * 64], smv[ci][cs, 1216:1280], True, False,
                               [R_pc[P_TOK + 1], R_sm[ci][7]], [R_ps[bN]])
                            MM(ps[bN][pb:pb + 64, blk * 64:(blk + 1) * 64], Khtok[cs, blk, hh * 64:(hh + 1) * 64], Vtok[cs, blk, hh * 64:(hh + 1) * 64], False, True,
                               [R_pc[P_TOK + 1], R_pc[P_TOK]], [R_ps[bN]])
                        for hh in range(2):
                            pb = 64 * hh
                            OP('act', 'activation', [R_ps[bN]], RS(2, range(4 * hh, 4 * hh + 4)), out=SM[pb:pb + 64, 4 * hh:4 * hh + 4, 640 + c * 64:640 + (c + 1) * 64],
                               in_=ps[bN][pb:pb + 64, 0:256].rearrange('p (c n) -> p c n', n=64), func=AF.Copy)
                    rpump()
                    CK(3.4)
                    for ci, (hh, blk), c in [(hh_ * 4 + blk_, (hh_, blk_), c_) for blk_ in range(4) for c_ in range(2) for hh_ in range(2)]:
                        pb = 64 * hh
                        s = smv[ci]
                        if True:
                            par = st_par[j][hh]
                            st_cur, r_cur = ST[pb:pb + 64, par, j, :], R_ST[par][j][hh]
                            st_new, r_new = ST[pb:pb + 64, 1 - par, j, :], R_ST[1 - par][j][hh]
                            st_par[j][hh] = 1 - par
                            bS = nb()
                            so = ps[bS][pb:pb + 64, 0:64]
                            MM(so, s[pb:pb + 64, 1280 + c * 64:1280 + (c + 1) * 64], st_cur, True, False, [R_sm[ci][9], r_cur], [R_ps[bS]])
                            MM(so, identb[pb:pb + 64, pb:pb + 64], s[pb:pb + 64, 640 + c * 64:640 + (c + 1) * 64], False, True, [R_cb, R_sm[ci][2]], [R_ps[bS]])
                            OP('act', 'activation', [R_ps[bS]], [r_new], out=st_new, in_=so, func=AF.Copy)
                            yo = ps[bY][pb:pb + 64, blk * 128 + c * 64:blk * 128 + (c + 1) * 64]
                            MM(yo, st_cur, s[pb:pb + 64, 384 + c * 64:384 + (c + 1) * 64], True, False, [r_cur, R_sm[ci][1]], [R_ps[bY]])
                            MM(yo, s[:, 1216:1280], s[:, 256 + c * 64:256 + (c + 1) * 64], False, False, [R_sm[ci][7], R_sm[ci][0]], [R_ps[bY]])
                            MM(yo, Vtok[:, blk, hh * 64:(hh + 1) * 64], s[:, 512 + c * 64:512 + (c + 1) * 64], False, True, [R_pc[P_TOK], R_sm[ci][12]], [R_ps[bY]])
                        if (blk * 2 + c) % 2 == 1 and hh == 1:
                            rpump(0.5, 1)

                    rpump()
                while next(gA, 'done') != 'done':
                    pump(WP)
                CK(4)
                auto_pump['w'] = 1.0 * WP
                ysb, Ry = pc(P_PRW + j), R_pc[P_PRW + j]
                tA, RtA = pc(P_PRW + 4 + j), R_pc[P_PRW + 4 + j]
                OP('act', 'activation', [R_ps[bY]], [Ry], out=ysb, in_=ps[bY][:, :], func=AF.Copy)
                ybf_ = pcb(P_B + 4)[:, 512:1024]
                OP('dve', 'tensor_copy', [Ry], [R_B[4]], out=ybf_, in_=ysb)
                OP('act', 'activation', [Ry], [R_kk2], out=kk2, in_=ysb, func=AF.Square)
                bm, bv_ = nb(), nb()
                MM(ps[bm][:, :], blk64, ybf_, True, True, [R_cb, R_B[4]], [R_ps[bm]])
                MM(ps[bv_][:, :], blk64, kk2, True, True, [R_cb, R_kk2], [R_ps[bv_]])
                bb_ = nb()
                MM(ps[bb_][:, :], blk1, rk, True, True, [R_cb, R_rk], [R_ps[bb_]])
                OP('act', 'activation', [R_ps[bm]], [RtA], out=tA, in_=ps[bm][:, :], func=AF.Square)
                OP('dve', 'tensor_tensor', [R_ps[bv_], RtA], [RtA], out=tA, in0=ps[bv_][:, :], in1=tA, op=ALU.subtract)
                OP('act', 'activation', [RtA, R_dv], [RtA], out=tA, in_=tA, func=AF.Ln, bias=eps_gn, scale=1.0)
                OP('dve', 'tensor_tensor', [Ry, R_ps[bm]], [Ry], out=ysb, in0=ysb, in1=ps[bm][:, :], op=ALU.subtract)
                OP('act', 'activation', [RtA], [RtA], out=tA, in_=tA, func=AF.Exp, scale=-0.5)
                OP('dve', 'tensor_tensor', [R_ps[bb_], RVr], [RVr], out=Vr, in0=ps[bb_][:, :], in1=Vr, op=ALU.mult)
                OP('dve', 'scalar_tensor_tensor', [Ry, RtA, R_vec], [Ry], out=ysb, in0=ysb, scalar=vec[:, 42 + j:43 + j], in1=tA, op0=ALU.mult, op1=ALU.mult)
                OP('dve', 'scalar_tensor_tensor', [Ry, RVr, R_vec], [Ry], out=ysb, in0=ysb, scalar=vec[:, 46 + j:47 + j], in1=Vr, op0=ALU.add, op1=ALU.add)
                bg = nb()
                MM(ps[bg][:, :], gupb[:, j * 128:(j + 1) * 128], sgd, True, True, [R_gup, R_pc[P_B]], [R_ps[bg]])
                OP('dve', 'tensor_tensor', [R_ps[bg], Ry], [R_abf[j]], out=abf[:, j, :], in0=ps[bg][:, :], in1=ysb, op=ALU.mult)

            CK(5)
            auto_pump['w'] = 0.0
            auto_pump['fn'] = None
            for _ in ag:
                pass
            bank_pool[:] = list(range(7))

            CK(6)
            def proj8(seq_base, src_reads_k, rhs_k, evac):
                for c in range(8):
                    slot = use_panel(seq_base + c // 4)
                    pos = c % 4
                    b = nb()
                    for k in range(8):
                        MM(ps[b][:, :], ring[:, slot, k * 512 + pos * 128:k * 512 + pos * 128 + 128], rhs_k(k), k == 0, k == 7,
                           [R_ring[slot]] + src_reads_k(k), [R_ps[b]])
                    evac(c, b)

            h = [pc(i) for i in range(8)]
            Rh = [R_pc[i] for i in range(8)]
            for c in range(8):
                DMA('sp', f'xh{c}', h[c], xT[c * 128:(c + 1) * 128, tsl], [], [Rh[c]])

            def ev_res(c, b):
                OP('dve', 'tensor_tensor', [R_ps[b], Rh[c]], [Rh[c]], out=h[c], in0=ps[b][:, :], in1=h[c], op=ALU.add)
            proj8(seq0 + 7, lambda k: [R_abf[k]], lambda k: abf[:, k, :], ev_res)

            CK(7)
            P_SQ2, P_RS2 = 30, 31

            def norm_cast(gcol):
                rs = rms_stats(lambda k: (h[k], [Rh[k]]), None, P_SQ2, P_RS2, nb())
                for k in range(8):
                    if k % 2 == 0:
                        OP('dve', 'tensor_scalar', [Rh[k], R_vec], [R_abf[k]], out=abf[:, k, :], in0=h[k],
                           scalar1=vec[:, gcol + k:gcol + k + 1], scalar2=None, op0=ALU.mult)
                    else:
                        OP('act', 'activation', [Rh[k], R_vec], [R_abf[k]], out=abf[:, k, :], in_=h[k], func=AF.Copy, scale=vec[:, gcol + k:gcol + k + 1])
                return rs
            rs2 = norm_cast(51)
            qm = pcb(24, 4).rearrange('p (c t) -> p c t', t=512)
            Rqm = Rp(24, 4)

            def ev_qm(c, b):
                OP('dve', 'scalar_tensor_tensor', [R_ps[b], R_pc[P_RS2]], Rqm, out=qm[:, c, :], in0=ps[b][:, :], scalar=1.0 / 16, in1=rs2,
                   op0=ALU.mult, op1=ALU.mult)
            proj8(seq0 + 9, lambda k: [R_abf[k]], lambda k: abf[:, k, :], ev_qm)

            CK(7.3)
            pm = pcb(28, 2).rearrange('p (m t) -> p m t', t=512)
            for hd in range(4):
                for mb in range(2):
                    b = nb()
                    for dc in range(2):
                        MM(ps[b][:, :], KmT[:, hd * 2 + dc, mb * 128:(mb + 1) * 128], qm[:, hd * 2 + dc, :], dc == 0, dc == 1, [R_KmT] + Rqm, [R_ps[b]])
                    OP('act', 'activation', [R_ps[b]], [R_pc[28 + mb // 2]], out=pm[:, mb, :], in_=ps[b][:, :], func=AF.Exp)
                bl = nb()
                for mb in range(2):
                    MM(ps[bl][:, :], ones1, pm[:, mb, :], mb == 0, mb == 1, [R_cb, R_pc[28]], [R_ps[bl]])
                rlm = pc(29)
                OP('act', 'activation', [R_ps[bl]], [R_pc[29]], out=rlm, in_=ps[bl][:, :], func=AF.Ln)
                OP('act', 'activation', [R_pc[29]], [R_pc[29]], out=rlm, in_=rlm, func=AF.Exp, scale=-1.0)
                for dc in range(2):
                    b = nb()
                    for mb in range(2):
                        MM(ps[b][:, :], Vm[:, mb, (hd * 2 + dc) * 128:(hd * 2 + dc + 1) * 128], pm[:, mb, :], mb == 0, mb == 1, [R_Vm, R_pc[28]], [R_ps[b]])
                    OP('dve', 'tensor_tensor', [R_ps[b], R_pc[29]], [R_abf[hd * 2 + dc]], out=abf[:, hd * 2 + dc, :], in0=ps[b][:, :], in1=rlm, op=ALU.mult)

            CK(7.6)
            proj8(seq0 + 11, lambda k: [R_abf[k]], lambda k: abf[:, k, :], ev_res)

            CK(8)
            rs3 = norm_cast(67)
            ubf = pcb(8, 16).rearrange('p (f t) -> p f t', t=512)
            for fo in range(32):
                slot = use_panel(seq0 + 13 + fo // 4)
                pos = fo % 4
                b = nb()
                for k in range(8):
                    MM(ps[b][:, :], ring[:, slot, k * 512 + pos * 128:k * 512 + pos * 128 + 128], abf[:, k, :], k == 0, k == 7,
                       [R_ring[slot], R_abf[k]], [R_ps[b]])
                tr = pc(28 + fo % 2)
                OP('dve', 'scalar_tensor_tensor', [R_ps[b], R_pc[P_RS2]], [R_pc[28 + fo % 2]], out=tr, in0=ps[b][:, :], scalar=0.0, in1=rs3,
                   op0=ALU.max, op1=ALU.mult)
                OP('act', 'activation', [R_pc[28 + fo % 2]], [R_pc[8 + fo // 2]], out=ubf[:, fo, :], in_=tr, func=AF.Square)
            if tau + 1 < ntiles:
                load_x_tile(tau + 1)
            CK(8.5)
            for c in range(8):
                slot = use_panel(seq0 + 21 + c)
                b = nb()
                for fo in range(32):
                    MM(ps[b][:, :], ring[:, slot, fo * 128:(fo + 1) * 128], ubf[:, fo, :], fo == 0, fo == 31, [R_ring[slot], R_pc[8 + fo // 2]], [R_ps[b]])
                ev_res(c, b)
            CK(9)
            rs4 = rms_stats(lambda k: (h[k], [Rh[k]]), None, P_SQ2, P_RS2, nb())
            for c in range(8):
                OP('dve', 'scalar_tensor_tensor', [Rh[c], R_vec, R_pc[P_RS2]], [Rh[c]], out=h[c], in0=h[c],
                   scalar=vec[:, 75 + c:76 + c], in1=rs4, op0=ALU.mult, op1=ALU.mult)
                DMA('sp', f'st{c}', outT[c * 128:(c + 1) * 128, tsl], h[c], [Rh[c]], [R_out])

        try:
            CK(0)
            load_x_tile(0)
            for tau in range(ntiles):
                tile_body(tau)
        except _Stop:
            pass

        for key in [f'st{c}' for c in range(8)]:
            if key in T.dma_cnt:
                T.streams['sp'].append(('wait', 'dma:' + key, T.dma_cnt[key] * 16))
        block = es.enter_context(nc.Block())
        T.emit(nc, block, es)
    return nc


def _panels(W, pw):
    K, N = W.shape
    kc = K // 128
    out = []
    for c0 in range(0, N, pw):
        blk = W[:, c0:c0 + pw].reshape(kc, 128, pw).transpose(1, 0, 2).reshape(128, kc * pw)
        out.append(blk)
    return out


def _prep_shared(inp):
    f = lambda k: np.asarray(inp[k], dtype=np.float32)
    w_in = f('w_in')[0]
    zpad = np.zeros((1024, 256), np.float32)
    w_in_perm = np.concatenate([w_in[:, :2560], w_in[:, 2560:2816], zpad, w_in[:, 2816:3328]], axis=1)
    pans = _panels(w_in_perm, 512)
    for k in ('w_out', 'w_mq', 'w_mo'):
        pans += _panels(f(k)[0], 512)
    pans += _panels(f('w_up')[0], 512)
    pans += _panels(f('w_down')[0], 128)
    wpan = np.ascontiguousarray(np.stack(pans, 0))
    assert wpan.shape == (NPAN, 128, 4096)
    wmkv = np.stack([_panels(f('w_mk')[0], 1024)[0], _panels(f('w_mv')[0], 1024)[0]], 0)
    wlora = np.zeros((128, 1024), np.float32)
    wlora[0:64, 0:512] = f('w_decay_up')[0]
    wlora[64:128, 0:512] = f('a_up')[0]
    wlora[:, 512:1024] = f('g_up')[0]
    vec = np.zeros((128, NV), np.float32)
    col = lambda v: np.asarray(v, np.float32).reshape(-1, 128).T
    vec[:, 0:8] = col(f('norm_mix_w')[0])
    vec[:, 8:22] = col(f('mu_shift')[0])
    vec[:, 22:26] = col(f('w_decay0')[0])
    vec[:, 26:30] = col(f('a0')[0])
    vec[:, 30:34] = col(f('k_k')[0])
    vec[:, 34:38] = col(f('k_a')[0])
    vec[:, 38:42] = col(f('r_k')[0].reshape(-1))
    vec[:, 42:46] = col(f('lnx_w')[0])
    vec[:, 46:50] = col(f('lnx_b')[0])
    vec[:, 50] = f('subln_w')[0]
    vec[:, 51:59] = col(f('norm_mem_w')[0])
    vec[:, 59:67] = col(f('norm_src_w')[0])
    vec[:, 67:75] = col(f('norm_mlp_w')[0])
    vec[:, 75:83] = col(f('norm_final_w'))
    for i, k in enumerate(('lam_q1', 'lam_k1', 'lam_q2', 'lam_k2')):
        vec[0:64, 83 + i] = f(k)[0]
    cst = np.zeros((128, NCST), np.float32)
    idx = np.arange(128)
    same = (idx[:, None] // 64) == (idx[None, :] // 64)
    mLT = ((idx[:, None] < idx[None, :]) & same).astype(np.float32)
    mInc = ((idx[:, None] <= idx[None, :]) & same).astype(np.float32)
    cst[:, CI:CI + 128] = np.eye(128, dtype=np.float32)
    cst[:, CM5:CM5 + 640] = np.concatenate([mLT.T, mLT, mInc, mLT, mInc], axis=1)
    rs = np.ones(512, np.float32)
    rs[::64] = 0.0
    cst[:, CRS:CRS + 512] = rs[None, :]
    cst[:, CBO:CBO + 128] = same.astype(np.float32)
    return dict(wpan=wpan, wmkv=np.ascontiguousarray(wmkv), wlora=wlora, vecs=vec, cst=cst)


_NC_CACHE = {}
MARKS = []


def kernel(**inputs):
    x = np.asarray(inputs['x'], dtype=np.float32)
    mem = np.asarray(inputs['mem'], dtype=np.float32)
    shared = _prep_shared(inputs)
    B = x.shape[0]
    in_maps = []
    for b in range(B):
        m = dict(shared)
        m['xT'] = np.ascontiguousarray(x[b].T)
        m['memT'] = np.ascontiguousarray(mem[b].T)
        in_maps.append(m)
    if 'nc' not in _NC_CACHE:
        _NC_CACHE['nc'] = build_nc()
    nc = _NC_CACHE['nc']
    res = run_bass_kernel_spmd(nc, in_maps, core_ids=list(range(B)))
    out = np.stack([np.ascontiguousarray(res.results[b]['outT'].T) for b in range(B)], 0)
    return out.astype(np.float32)
```

```python
import os
import numpy as np
from contextlib import ExitStack
import concourse.bass as bass
import concourse.mybir as mybir
from concourse.bass_utils import run_bass_kernel_spmd

F32 = mybir.dt.float32
BF16 = mybir.dt.bfloat16
AF = mybir.ActivationFunctionType
ALU = mybir.AluOpType
ENGS = ('pe', 'act', 'dve', 'pool', 'sp')

S = 4096
D = 1024
TT_ = 512
NT = S // TT_
NV = 87
C0 = float(np.exp(-0.5))
CI, CM5, CRS, CBO, NCST = 0, 128, 768, 1280, 1408
NPAN = 29
WIN_POS = [(c // 4, c % 4) for c in range(16)] + [(4, 0), (4, 1), (4, 2), (4, 3), (5, 0), (5, 1),
                                                    (6, 0), (6, 1), (6, 2), (6, 3)]


class Res:
    __slots__ = ('name', 'w', 'rs')

    def __init__(self, name=''):
        self.name = name
        self.w = None
        self.rs = {}


class Trk:
    def __init__(self):
        self.streams = {e: [] for e in ENGS}
        self.cnt = {e: 0 for e in ENGS}
        self.known = {e: {} for e in ENGS}
        self.dma_cnt = {}

    def _deps(self, eng, reads, writes):
        deps = {}
        for r in reads:
            if r.w is not None and deps.get(r.w[0], 0) < r.w[1]:
                deps[r.w[0]] = r.w[1]
        for r in writes:
            if r.w is not None and deps.get(r.w[0], 0) < r.w[1]:
                deps[r.w[0]] = r.w[1]
            for k, i in r.rs.items():
                if deps.get(k, 0) < i:
                    deps[k] = i
        kn = self.known[eng]
        for k, i in deps.items():
            if k == eng and eng in ('pe', 'sp'):
                continue
            if kn.get(k, 0) >= i:
                continue
            kn[k] = i
            self.streams[eng].append(('wait', k, i))

    def op(self, eng, fn, reads=(), writes=()):
        self._deps(eng, reads, writes)
        self.cnt[eng] += 1
        i = self.cnt[eng]
        self.streams[eng].append(('op', fn))
        for r in reads:
            if r.rs.get(eng, 0) < i:
                r.rs[eng] = i
        for r in writes:
            r.w = (eng, i)
            r.rs = {}

    def dma(self, eng, semkey, fn, reads=(), writes=()):
        self._deps(eng, reads, writes)
        self.dma_cnt[semkey] = self.dma_cnt.get(semkey, 0) + 1
        k = 'dma:' + semkey
        i = self.dma_cnt[semkey] * 16
        self.streams[eng].append(('dma', fn, semkey))
        for r in reads:
            if r.rs.get(k, 0) < i:
                r.rs[k] = i
        for r in writes:
            r.w = (k, i)
            r.rs = {}

    def emit(self, nc, block, es):
        sems = {}
        for e in ENGS:
            sems[e] = es.enter_context(nc.semaphore('sem_' + e))
        for k in self.dma_cnt:
            sems['dma:' + k] = es.enter_context(nc.semaphore('dsem_' + k))
        streams = self.streams

        def run(engname, eng):
            own = sems[engname]
            for ent in streams[engname]:
                if ent[0] == 'wait':
                    eng.wait_ge(sems[ent[1]], ent[2])
                elif ent[0] == 'op':
                    ent[1](eng).then_inc(own, 1)
                else:
                    ent[1](eng).then_inc(sems['dma:' + ent[2]], 16)

        block.tensor(lambda eng: run('pe', eng))
        block.scalar(lambda eng: run('act', eng))
        block.vector(lambda eng: run('dve', eng))
        block.gpsimd(lambda eng: run('pool', eng))
        block.sync(lambda eng: run('sp', eng))


def build_nc(ntiles=NT, dbg=None, stage_limit=99):
    nc = bass.Bass("TRN2", target_bir_lowering=False)
    xT = nc.dram_tensor("xT", [D, S], F32, kind="ExternalInput").ap()
    memT = nc.dram_tensor("memT", [D, 256], F32, kind="ExternalInput").ap()
    wpan = nc.dram_tensor("wpan", [NPAN, 128, 4096], F32, kind="ExternalInput").ap()
    wmkv = nc.dram_tensor("wmkv", [2, 128, 8192], F32, kind="ExternalInput").ap()
    wlora = nc.dram_tensor("wlora", [128, 1024], F32, kind="ExternalInput").ap()
    vecs = nc.dram_tensor("vecs", [128, NV], F32, kind="ExternalInput").ap()
    cst = nc.dram_tensor("cst", [128, NCST], F32, kind="ExternalInput").ap()
    outT = nc.dram_tensor("outT", [D, S], F32, kind="ExternalOutput").ap()
    wbf = nc.dram_tensor("wbf", [NPAN, 128, 4096], BF16).ap()
    dbg_out = None
    if dbg:
        dbg_out = nc.dram_tensor("dbg", [128, 16, 512], F32, kind="ExternalOutput").ap()

    T = Trk()
    es = ExitStack()
    with es:
        def sb(n, s, d):
            return es.enter_context(nc.sbuf_tensor(n, s, d))

        vec = sb('vec', [128, NV], F32)
        dv = sb('dvec', [128, 32], F32)
        c32 = sb('c32', [128, 128], F32)
        m5b = sb('m5b', [128, 640], BF16)
        rsb = sb('rsb', [128, 512], BF16)
        cb = sb('cb', [128, 6, 128], BF16)
        Kc = sb('Kc', [128, 4, S], BF16)
        Vc = sb('Vc', [128, S // 128, 512], BF16)
        KmT = sb('KmT', [128, 8, 256], BF16)
        Vm = sb('Vm', [128, 2, 1024], BF16)
        lorab = sb('lorab', [128, 512], BF16)
        gupb = sb('gupb', [128, 512], BF16)
        ring = sb('ring', [128, 3, 4096], BF16)
        abf = sb('abf', [128, 8, 512], BF16)
        NPIECE = 36
        AR = sb('AR', [128, NPIECE * 512], F32)
        SM = sb('SM', [128, 8, 1408], BF16)
        carry = sb('carry', [128, 16], F32)
        ST = sb('ST', [128, 2, 4, 64], BF16)
        wcb = sb('wcb', [128, 4, 8], F32)
        rtok = sb('rtok', [128, 4], F32)
        ones32 = sb('ones32', [128, 128], F32)
        ps = [es.enter_context(nc.psum_tensor(f'ps{i}', [128, 512], F32)) for i in range(8)]

        R_vec, R_dv, R_c32, R_cb = Res('vec'), Res('dv'), Res('c32'), Res('cb')
        R_Kc = [Res(f'Kc{t}') for t in range(NT)]
        R_Vc = [Res(f'Vc{t}') for t in range(NT)]
        R_KmT, R_Vm, R_lora, R_gup = Res('KmT'), Res('Vm'), Res('lora'), Res('gup')
        R_ring = [Res(f'ring{i}') for i in range(3)]
        R_abf = [Res(f'abf{i}') for i in range(8)]
        R_pc = [Res(f'pc{i}') for i in range(NPIECE)]
        R_carry, R_wcb, R_rtok = Res('carry'), Res('wcb'), Res('rtok')
        R_ST = [[[Res(f'ST{p}_{j}_{h}') for h in range(2)] for j in range(4)] for p in range(2)]
        st_par = [[0, 0] for _ in range(4)]
        R_ps = [Res(f'ps{i}') for i in range(8)]
        R_wbf = [Res(f'wbf{i}') for i in range(NPAN)]
        R_out = Res('out')

        auto_pump = {'w': 0.0, 'fn': None, 'busy': False}

        def _ap():
            if auto_pump['w'] > 0 and auto_pump['fn'] is not None and not auto_pump['busy']:
                auto_pump['busy'] = True
                auto_pump['fn'](auto_pump['w'])
                auto_pump['busy'] = False

        def OP(eng, meth, reads, writes, *a, **k):
            T.op(eng, (lambda e: getattr(e, meth)(*a, **k)), reads, writes)
            _ap()

        def MM(out, lhsT, rhs, start, stop, reads, writes):
            T.op('pe', (lambda e: e.matmul(out, lhsT=lhsT, rhs=rhs, start=start, stop=stop)), reads, writes)
            if stop:
                _ap()

        def DMA(eng, key, out, in_, reads, writes):
            T.dma(eng, key, (lambda e: e.dma_start(out=out, in_=in_)), reads, writes)

        def pc(i, n=1):
            return AR[:, i * 512:(i + n) * 512]

        def pcb(i, n=1):
            return AR[:, i * 512:(i + n) * 512].bitcast(BF16)

        def Rp(i, n=1):
            return R_pc[i:i + n]

        bank_ctr = [0]

        bank_pool = list(range(7))

        def nb():
            b = bank_pool[bank_ctr[0] % len(bank_pool)]
            bank_ctr[0] += 1
            return b

        identb, blk1, blk64, ones1024, ones128, ones1 = [cb[:, i, :] for i in range(6)]
        ident32 = c32[:, CI:CI + 128]
        eps_rms = dv[:, 20:21]
        eps_gn = dv[:, 21:22]

        DMA('sp', 'ld0', vec[:], vecs, [], [R_vec])
        DMA('sp', 'ld1', c32[:], cst[:, 0:128], [], [R_c32])
        DMA('pool', 'ld5a', m5b[:], cst[:, CM5:CM5 + 640], [], [R_c32])
        DMA('pool', 'ld5b', rsb[:], cst[:, CRS:CRS + 512], [], [R_c32])
        DMA('pool', 'ld5c', cb[:, 1, :], cst[:, CBO:CBO + 128], [], [R_cb])
        SKIP = []

        def cast_panel(pi, deps=()):
            key = f'wc{pi}' if pi < 7 else ('wcB' if pi < 13 else 'wcC')
            DMA('pool', key, wbf[pi], wpan[pi], list(deps), [R_wbf[pi]])

        DMA('pool', 'ld2', lorab[:], wlora[:, 0:512], [], [R_lora])
        DMA('pool', 'ld3', gupb[:], wlora[:, 512:1024], [], [R_gup])
        wkv = [pcb(8, 8), pcb(16, 8)]
        DMA('pool', 'ld4a', wkv[0], wmkv[0], [], Rp(8, 8))
        DMA('pool', 'ld4b', wkv[1], wmkv[1], [], Rp(16, 8))
        for pi in range(13):
            cast_panel(pi)
        OP('dve', 'tensor_copy', [R_c32], [R_cb], out=identb, in_=ident32)
        OP('dve', 'tensor_scalar', [R_cb], [R_cb], out=blk64, in0=blk1, scalar1=1.0 / 64, scalar2=None,
           op0=ALU.mult)
        OP('pool', 'memset', [], [R_cb], ones1024, 1.0 / 1024)
        OP('pool', 'memset', [], [R_cb], ones128, 1.0 / 128)
        OP('pool', 'memset', [], [R_cb], ones1, 1.0)
        OP('pool', 'memset', [], [R_cb], ones32[:], 1.0)
        OP('dve', 'tensor_scalar', [R_vec], [R_dv], out=dv[:, 0:14], in0=vec[:, 8:22], scalar1=-1.0, scalar2=1.0,
           op0=ALU.mult, op1=ALU.add)
        OP('dve', 'tensor_scalar', [R_vec], [R_dv], out=dv[:, 14:18], in0=vec[:, 34:38], scalar1=-1.0, scalar2=1.0,
           op0=ALU.mult, op1=ALU.add)
        OP('dve', 'tensor_scalar', [R_vec], [R_dv], out=dv[:, 18:19], in0=vec[:, 50:51], scalar1=0.8, scalar2=None,
           op0=ALU.mult)
        OP('pool', 'memset', [], [R_dv], dv[:, 20:21], 1e-5)
        OP('pool', 'memset', [], [R_dv], dv[:, 21:22], 64e-5)
        OP('pool', 'memset', [], [R_carry], carry[:], 0.0)
        OP('pool', 'memset', [], [r for a in R_ST for b in a for r in b], ST[:], 0.0)
        OP('dve', 'tensor_tensor', [R_vec], [R_dv], out=dv[:, 22:23], in0=vec[:, 83:84], in1=vec[:, 84:85], op=ALU.mult)
        OP('dve', 'tensor_tensor', [R_vec], [R_dv], out=dv[:, 23:24], in0=vec[:, 85:86], in1=vec[:, 86:87], op=ALU.mult)
        OP('pool', 'memset', [], [R_pc[0]], pc(0)[:, 0:128], 1.0)
        if 'lam' not in SKIP:
          MM(ps[0][:, 0:2], pc(0)[:, 0:128], dv[:, 22:24], True, True, [R_pc[0], R_dv], [R_ps[0]])
        OP('act', 'activation', [R_ps[0]], [R_dv], out=dv[:, 24:26], in_=ps[0][:, 0:2], func=AF.Exp)
        OP('dve', 'tensor_tensor', [R_dv], [R_dv], out=dv[:, 26:27], in0=dv[:, 25:26], in1=dv[:, 24:25], op=ALU.subtract)
        OP('dve', 'tensor_scalar', [R_dv], [R_dv], out=dv[:, 19:20], in0=dv[:, 26:27], scalar1=-0.2, scalar2=None,
           op0=ALU.add)

        if 'mem' in SKIP:
            stage_limit = -1
        m32 = pc(0, 4).rearrange('p (k m) -> p k m', m=256)
        DMA('sp', 'ld2b', m32, memT.rearrange('(k p) m -> p k m', p=128), [], Rp(0, 4))
        msq = pcb(6, 1)
        msq = pcb(6, 2).rearrange('p (k m) -> p k m', m=256)
        memn = pcb(4, 2).rearrange('p (k m) -> p k m', m=256)
        OP('act', 'activation', Rp(0, 4), Rp(6, 2), out=msq, in_=m32, func=AF.Square)
        for k in range(8):
            MM(ps[1][:, 0:256], ones1024, msq[:, k, :], k == 0, k == 7, [R_cb] + Rp(6, 2), [R_ps[1]])
        mr = pc(24)[:, 0:256]
        OP('act', 'activation', [R_ps[1], R_dv], [R_pc[24]], out=mr, in_=ps[1][:, 0:256], func=AF.Ln, bias=eps_rms, scale=1.0)
        OP('act', 'activation', [R_pc[24]], [R_pc[24]], out=mr, in_=mr, func=AF.Exp, scale=-0.5)
        for k in range(8):
            OP('dve', 'scalar_tensor_tensor', Rp(0, 4) + [R_pc[24], R_vec], Rp(4, 2), out=memn[:, k, :], in0=m32[:, k, :],
               scalar=vec[:, 59 + k:60 + k], in1=mr, op0=ALU.mult, op1=ALU.mult)
        for which in range(2):
            wv = wkv[which].rearrange('p (k n) -> p k n', n=1024)
            Rw = Rp(8 + 8 * which, 8)
            if which == 0:
                for c in range(8):
                    b = nb()
                    for k in range(8):
                        MM(ps[b][:, 0:256], wv[:, k, c * 128:(c + 1) * 128], memn[:, k, :], k == 0, k == 7,
                           Rw + Rp(4, 2), [R_ps[b]])
                    OP('act', 'activation', [R_ps[b]], [R_KmT], out=KmT[:, c, :], in_=ps[b][:, 0:256], func=AF.Copy)
            else:
                for mb in range(2):
                    for nh in range(2):
                        b = nb()
                        for k in range(8):
                            MM(ps[b][:, :], memn[:, k, mb * 128:(mb + 1) * 128], wv[:, k, nh * 512:(nh + 1) * 512], k == 0, k == 7,
                               Rw + Rp(4, 2), [R_ps[b]])
                        OP('act', 'activation', [R_ps[b]], [R_Vm], out=Vm[:, mb, nh * 512:(nh + 1) * 512], in_=ps[b][:, :], func=AF.Copy)

        ring_state = {'next': 0}
        panel_slot = {}

        def load_panel(seq):
            if seq >= ntiles * NPAN or seq in panel_slot:
                return
            assert seq == ring_state['next']
            ring_state['next'] += 1
            slot = seq % 3
            panel_slot[seq] = slot
            pi = seq % NPAN
            gl = pi if pi < 7 else (12 if pi < 13 else NPAN - 1)
            DMA('sp', f'ring{slot}', ring[:, slot, :], wbf[pi], [R_wbf[gl]], [R_ring[slot]])

        def use_panel(seq):
            load_panel(seq)
            load_panel(seq + 1)
            load_panel(seq + 2)
            return panel_slot[seq]

        P_PRW = 0
        P_F = 14
        P_B = 21
        P_TOK = 27
        P_YSB = 16
        P_TMP = 29
        P_RSTD = 31

        def rms_stats(src_chunks_fn, nsrc_reads, sq_piece, rstd_piece, bank):
            sqv = pcb(sq_piece).rearrange('p (h t) -> p h t', t=512)
            for k in range(8):
                src, rr = src_chunks_fn(k)
                OP('act', 'activation', rr, [R_pc[sq_piece]], out=sqv[:, k % 2, :], in_=src, func=AF.Square)
                MM(ps[bank][:, :], ones1024, sqv[:, k % 2, :], k == 0, k == 7, [R_cb, R_pc[sq_piece]], [R_ps[bank]])
            rs = pc(rstd_piece)
            OP('act', 'activation', [R_ps[bank], R_dv], [R_pc[rstd_piece]], out=rs, in_=ps[bank][:, :], func=AF.Ln, bias=eps_rms, scale=1.0)
            OP('act', 'activation', [R_pc[rstd_piece]], [R_pc[rstd_piece]], out=rs, in_=rs, func=AF.Exp, scale=-0.5)
            return rs

        R_sm = [[Res(f'sm{ci}_{n}') for n in range(13)] for ci in range(8)]
        R_pt = [Res(f'pt{i}') for i in range(4)]
        R_kk2rk = (Res('kk2'), Res('rk'))
        R_wcbj = [Res(f'wcb{j}') for j in range(4)]

        class _Stop(Exception):
            pass

        def CK(n):
            MARKS.append((n, dict(T.cnt)))
            if stage_limit <= n:
                raise _Stop()

        P_XS, P_XSQ, P_XR = 24, 26, 27

        def load_x_tile(tau):
            tsl = slice(tau * 512, (tau + 1) * 512)
            sqv = pcb(P_XSQ).rearrange('p (h t) -> p h t', t=512)
            bank_ms = nb()
            for k in range(8):
                stg = pc(P_XS + (k % 2))
                DMA('sp', f'xs{k % 2}', stg, xT[k * 128:(k + 1) * 128, tsl], [], [R_pc[P_XS + (k % 2)]])
                OP('act', 'activation', [R_pc[P_XS + (k % 2)]], [R_pc[P_XSQ]], out=sqv[:, k % 2, :], in_=stg, func=AF.Square)
                MM(ps[bank_ms][:, :], ones1024, sqv[:, k % 2, :], k == 0, k == 7, [R_cb, R_pc[P_XSQ]], [R_ps[bank_ms]])
                if k % 2 == 0:
                    OP('dve', 'tensor_scalar', [R_pc[P_XS + (k % 2)], R_vec], [R_abf[k]], out=abf[:, k, :], in0=stg,
                       scalar1=vec[:, k:k + 1], scalar2=None, op0=ALU.mult)
                else:
                    OP('act', 'activation', [R_pc[P_XS + (k % 2)], R_vec], [R_abf[k]], out=abf[:, k, :], in_=stg, func=AF.Copy, scale=vec[:, k:k + 1])
            rstd = pc(P_XR)
            OP('act', 'activation', [R_ps[bank_ms], R_dv], [R_pc[P_XR]], out=rstd, in_=ps[bank_ms][:, :], func=AF.Ln, bias=eps_rms, scale=1.0)
            OP('act', 'activation', [R_pc[P_XR]], [R_pc[P_XR]], out=rstd, in_=rstd, func=AF.Exp, scale=-0.5)
            bt_ = nb()
            for tb in range(4):
                MM(ps[bt_][:, tb:tb + 1], rstd[:, tb * 128:(tb + 1) * 128], ident32[:, 0:1], True, True, [R_pc[P_XR], R_c32], [R_ps[bt_]])
            OP('dve', 'tensor_copy', [R_ps[bt_]], [R_rtok], out=rtok[:, :], in_=ps[bt_][:, 0:4])

        def tile_body(tau):
            tsl = slice(tau * 512, (tau + 1) * 512)
            seq0 = tau * NPAN
            use_panel(seq0)
            rstd = pc(P_XR)
            CK(1)
            qbf = pcb(P_TMP, 2).rearrange('p (c t) -> p c t', t=512)
            R_qbf = Rp(P_TMP, 2)
            for c in range(22):
                pan, pos = WIN_POS[c]
                slot = use_panel(seq0 + pan)
                b = nb()
                for k in range(8):
                    MM(ps[b][:, :], ring[:, slot, k * 512 + pos * 128:k * 512 + pos * 128 + 128], abf[:, k, :], k == 0, k == 7,
                       [R_ring[slot], R_abf[k]], [R_ps[b]])
                if c < 14:
                    p = pc(P_PRW + c)
                    Rc = R_pc[P_PRW + c]
                    OP('dve', 'tensor_tensor', [R_ps[b], R_pc[P_XR]], [Rc], out=p, in0=ps[b][:, :], in1=rstd, op=ALU.mult)
                    tp = P_TMP + (c % 2)
                    tmp = pc(tp)
                    OP('act', 'activation', [Rc, R_vec], [R_pc[tp]], out=tmp[:, 1:512], in_=p[:, 0:511], func=AF.Copy, scale=vec[:, 8 + c:9 + c])
                    OP('act', 'activation', [R_carry, R_vec], [R_pc[tp]], out=tmp[:, 0:1], in_=carry[:, c:c + 1], func=AF.Copy, scale=vec[:, 8 + c:9 + c])
                    OP('act', 'activation', [Rc], [R_carry], out=carry[:, c:c + 1], in_=p[:, 511:512], func=AF.Copy)
                    OP('dve', 'scalar_tensor_tensor', [Rc, R_pc[tp], R_dv], [Rc], out=p, in0=p, scalar=dv[:, c:c + 1], in1=tmp,
                       op0=ALU.mult, op1=ALU.add)
                elif c < 18:
                    OP('dve', 'scalar_tensor_tensor', [R_ps[b], R_pc[P_XR]], R_qbf, out=qbf[:, c - 14, :], in0=ps[b][:, :], scalar=0.125,
                       in1=rstd, op0=ALU.mult, op1=ALU.mult)
                else:
                    OP('dve', 'tensor_tensor', [R_ps[b], R_pc[P_XR]], [R_Kc[tau]], out=Kc[:, c - 18, tsl], in0=ps[b][:, :], in1=rstd, op=ALU.mult)
            slot = use_panel(seq0 + 6)
            for tb in range(4):
                b = nb()
                for k in range(8):
                    MM(ps[b][:, :], abf[:, k, tb * 128:(tb + 1) * 128], ring[:, slot, k * 512:(k + 1) * 512], k == 0, k == 7,
                       [R_ring[slot], R_abf[k]], [R_ps[b]])
                OP('act', 'activation', [R_ps[b], R_rtok], [R_Vc[tau]], out=Vc[:, tau * 4 + tb, :], in_=ps[b][:, :], func=AF.Copy,
                   scale=rtok[:, tb:tb + 1])

            if tau == 0:
                for pi in range(13, NPAN):
                    cast_panel(pi, deps=[R_Kc[0], R_Vc[0]])
            nkb = 4 * tau + 4
            ot1, ot2, rl = pc(34), pc(35), pc(31)
            accL = [pc(32), pc(33)]
            R_ot1, R_ot2, R_rl = R_pc[34], R_pc[35], R_pc[31]
            R_acc = [R_pc[32], R_pc[33]]
            bO = [4, 5]
            bST = [6, 7]
            st_ctr = [0]

            def attn_gen():
                for hd in range(4):
                    units = [(kb, m) for kb in range(nkb) for m in range(2)]
                    LA = 1
                    uinfo = {}

                    def u_qk(i):
                        kb, m = units[i]
                        qoff = max(0, kb - 4 * tau) * 128
                        ktile = kb // 4
                        if PAIR:
                            bs = bST[m]
                            ip = (kb % 2) * 2 + m
                        else:
                            bs = bST[st_ctr[0] % 2]
                            ip = st_ctr[0] % 4
                            st_ctr[0] += 1
                        mp = slice(64 * m, 64 * m + 64)
                        MM(ps[bs][:, qoff:512], Kc[mp, hd, kb * 128:(kb + 1) * 128], qbf[mp, hd, qoff:512], True, True, [R_Kc[ktile]] + R_qbf, [R_ps[bs]])
                        pt = pcb(P_PRW + 12 + (ip // 2))[:, (ip % 2) * 512:(ip % 2) * 512 + 512]
                        Rpt = R_pt[ip]
                        uinfo[i] = (pt, Rpt, qoff, ktile, bs)
                        if not PAIR:
                            u_exp(i)

                    def u_exp(i):
                        kb, m = units[i]
                        pt, Rpt, qoff, ktile, bs = uinfo[i]
                        OP('act', 'activation', [R_ps[bs]], [Rpt], out=pt[:, qoff:512], in_=ps[bs][:, qoff:512], func=AF.Exp)
                        if kb >= 4 * tau:
                            OP('dve', 'memset', [], [Rpt], pt[64:128, qoff:qoff + 64], 0.0)

                    def u_pv(i):
                        kb, m = units[i]
                        pt, Rpt, qoff, ktile, _bs = uinfo.pop(i)
                        MM(ps[bO[m]][:, qoff:512], Vc[:, kb, hd * 128:(hd + 1) * 128], pt[:, qoff:512], kb == 0, kb == nkb - 1, [R_Vc[ktile], Rpt], [R_ps[bO[m]]])
                        if kb == 0:
                            OP(ACC_ENG, 'tensor_copy', [Rpt], [R_acc[m]], out=accL[m], in_=pt)
                        else:
                            OP(ACC_ENG, 'tensor_tensor', [Rpt, R_acc[m]], [R_acc[m]], out=accL[m][:, qoff:512], in0=accL[m][:, qoff:512], in1=pt[:, qoff:512], op=ALU.add)

                    if PAIR:
                        for kb_ in range(nkb + 1):
                            if kb_ < nkb:
                                u_qk(2 * kb_)
                                u_qk(2 * kb_ + 1)
                                u_exp(2 * kb_)
                                u_exp(2 * kb_ + 1)
                            if kb_ >= 1:
                                u_pv(2 * kb_ - 2)
                                u_pv(2 * kb_ - 1)
                            yield
                            yield
                    else:
                        for i in range(len(units) + LA):
                            if i < len(units):
                                u_qk(i)
                            if i >= LA:
                                u_pv(i - LA)
                            yield
                    bl0 = bST[st_ctr[0] % 2]
                    st_ctr[0] += 1
                    MM(ps[bl0][:, :], ones32[:], accL[0], True, True, [R_cb, R_acc[0]], [R_ps[bl0]])
                    OP('act', 'activation', [R_ps[bl0]], [R_rl], out=rl, in_=ps[bl0][:, :], func=AF.Ln)
                    OP('act', 'activation', [R_rl], [R_rl], out=rl, in_=rl, func=AF.Exp, scale=-1.0)
                    OP('dve', 'tensor_tensor', [R_ps[bO[0]], R_rl], [R_ot1], out=ot1, in0=ps[bO[0]][:, :], in1=rl, op=ALU.mult)
                    yield
                    bl1 = bST[st_ctr[0] % 2]
                    st_ctr[0] += 1
                    MM(ps[bl1][:, :], ones32[:], accL[1], True, True, [R_cb, R_acc[1]], [R_ps[bl1]])
                    OP('act', 'activation', [R_ps[bl1]], [R_rl], out=rl, in_=ps[bl1][:, :], func=AF.Ln)
                    OP('act', 'activation', [R_rl], [R_rl], out=rl, in_=rl, func=AF.Exp, scale=-1.0)
                    OP('dve', 'tensor_scalar', [R_rl, R_dv], [R_rl], out=rl, in0=rl, scalar1=dv[:, 19:20], scalar2=None, op0=ALU.mult)
                    OP('dve', 'tensor_tensor', [R_ps[bO[1]], R_rl], [R_ot2], out=ot2, in0=ps[bO[1]][:, :], in1=rl, op=ALU.mult)
                    yield
                    OP('dve', 'tensor_tensor', [R_ot1, R_ot2], [R_ot1], out=ot1, in0=ot1, in1=ot2, op=ALU.add)
                    osq = rl.bitcast(BF16)[:, 0:512]
                    OP('act', 'activation', [R_ot1], [R_rl], out=osq, in_=ot1, func=AF.Square)
                    bq = bST[st_ctr[0] % 2]
                    st_ctr[0] += 1
                    MM(ps[bq][:, :], ones128, osq, True, True, [R_cb, R_rl], [R_ps[bq]])
                    OP('act', 'activation', [R_ps[bq], R_dv], [R_ot2], out=ot2, in_=ps[bq][:, :], func=AF.Ln, bias=eps_rms, scale=1.0)
                    OP('act', 'activation', [R_ot2], [R_ot2], out=ot2, in_=ot2, func=AF.Exp, scale=-0.5)
                    OP('dve', 'scalar_tensor_tensor', [R_ot1, R_ot2, R_dv], [R_abf[4 + hd]], out=abf[:, 4 + hd, :], in0=ot1, scalar=dv[:, 18:19],
                       in1=ot2, op0=ALU.mult, op1=ALU.mult)
                    yield

            ag = attn_gen()
            n_attn_steps = 4 * (2 * nkb + 1 + 3)
            WP = 1.0
            ACC_ENG = 'dve'
            PAIR = True
            BURST = 1.0
            pump_rate = 1.08 * n_attn_steps / (4 * (45 * WP + 14))
            pump_acc = [0.0]

            def pump(w=1.0):
                pump_acc[0] += w * pump_rate
                was = auto_pump['busy']
                auto_pump['busy'] = True
                if pump_acc[0] >= BURST:
                    while pump_acc[0] >= 1.0:
                        pump_acc[0] -= 1.0
                        next(ag, None)
                auto_pump['busy'] = was

            auto_pump['fn'] = pump

            bank_pool[:] = [0, 1, 2]

            CK(2)
            tw = pcb(P_B)[:, 0:512]
            sgd = pcb(P_B)[:, 512:1024]
            wda = pc(P_PRW + 12)
            OP('act', 'activation', [R_pc[P_PRW + 12]], [R_pc[P_B]], out=tw[0:64, :], in_=wda[0:64, :], func=AF.Tanh)
            OP('act', 'activation', [R_pc[P_PRW + 12]], [R_pc[P_B]], out=tw[64:128, :], in_=wda[64:128, :], func=AF.Copy)
            OP('act', 'activation', [R_pc[P_PRW + 13]], [R_pc[P_B]], out=sgd, in_=pc(P_PRW + 13), func=AF.Sigmoid)
            f = [pc(P_F + i) for i in range(7)]
            Rf = [R_pc[P_F + i] for i in range(7)]
            kk2 = pcb(P_B + 1)[:, 0:512]
            rk = pcb(P_B + 1)[:, 512:1024]
            btT = pcb(P_B + 2)[:, 0:512]
            ktT = pcb(P_B + 2)[:, 512:1024]
            bhT = pcb(P_B + 3)[:, 0:512]
            khT = pcb(P_B + 3)[:, 512:1024]
            vbf = pcb(P_B + 4)[:, 0:512]
            arT = pcb(P_B + 5).rearrange('p (b s t) -> p b s t', s=2, t=128)
            arF = pcb(P_B + 5)
            Atok = pcb(P_TOK)[:, 0:512].rearrange('p (b c) -> p b c', c=128)
            Vtok = pcb(P_TOK)[:, 512:1024].rearrange('p (b c) -> p b c', c=128)
            Bhtok = pcb(P_TOK + 1)[:, 0:512].rearrange('p (b c) -> p b c', c=128)
            Khtok = pcb(P_TOK + 1)[:, 512:1024].rearrange('p (b c) -> p b c', c=128)
            R_B = [R_pc[P_B + i] for i in range(6)]
            v4 = lambda ap: ap.rearrange('p (b t) -> p b t', t=128)
            v8 = lambda ap: ap.rearrange('p (c t) -> p c t', t=64)
            R_kk2, R_rk = R_kk2rk
            R_wj = R_wcbj

            def prepA(j):
                Kr = pc(P_PRW + 4 + j)
                RKr = R_pc[P_PRW + 4 + j]
                bd = nb()
                MM(ps[bd][:, :], lorab[0:64, j * 128:(j + 1) * 128], tw[0:64, :], True, True, [R_lora, R_pc[P_B]], [R_ps[bd]])
                ba = nb()
                MM(ps[ba][:, :], lorab[64:128, j * 128:(j + 1) * 128], tw[64:128, :], True, True, [R_lora, R_pc[P_B]], [R_ps[ba]])
                OP('act', 'activation', [R_ps[bd], R_vec], [Rf[0]], out=f[0], in_=ps[bd][:, :], func=AF.Sigmoid, bias=vec[:, 22 + j:23 + j], scale=1.0)
                OP('act', 'activation', [R_ps[ba], R_vec], [Rf[4]], out=f[4], in_=ps[ba][:, :], func=AF.Sigmoid, bias=vec[:, 26 + j:27 + j], scale=1.0)
                yield
                OP('act', 'activation', [RKr, R_vec], [Rf[5]], out=f[5], in_=Kr, func=AF.Copy, scale=vec[:, 30 + j:31 + j])
                OP('dve', 'tensor_tensor_scan', [Rf[0], R_c32], [Rf[1]], out=f[1], data0=rsb[:], data1=f[0], initial=0.0,
                   op0=ALU.mult, op1=ALU.add)
                yield
                OP('act', 'activation', [Rf[5]], [R_kk2], out=kk2, in_=f[5], func=AF.Square)
                OP('dve', 'tensor_tensor', [Rf[0], Rf[1]], [Rf[2]], out=f[2], in0=f[1], in1=f[0], op=ALU.subtract)
                yield
                OP('act', 'activation', [Rf[1]], [Rf[3]], out=f[3], in_=f[1], func=AF.Exp, scale=-C0)
                bq = nb()
                MM(ps[bq][:, :], blk1, kk2, True, True, [R_cb, R_kk2], [R_ps[bq]])
                OP('dve', 'tensor_scalar', [R_ps[bq]], [Rf[6]], out=f[6], in0=ps[bq][:, :], scalar1=1e-24, scalar2=None, op0=ALU.max)
                yield
                OP('act', 'activation', [Rf[2]], [Rf[2]], out=f[2], in_=f[2], func=AF.Exp, scale=-C0)
                OP('act', 'activation', [Rf[1]], [Rf[1]], out=f[1], in_=f[1], func=AF.Exp, scale=C0)
                yield
                OP('act', 'activation', [Rf[6]], [Rf[6]], out=f[6], in_=f[6], func=AF.Ln)
                OP('dve', 'tensor_copy', [Rf[3]], [R_wj[j]], out=wcb[:, j, :], in_=v8(f[3])[:, :, 63])
                yield
                OP('act', 'activation', [Rf[6]], [Rf[6]], out=f[6], in_=f[6], func=AF.Exp, scale=-0.5)
                OP('dve', 'tensor_tensor', [Rf[1], Rf[3]], [Rf[0]], out=v8(f[0]), in0=v8(f[1]), in1=v8(f[3])[:, :, 63:64].broadcast_to([128, 8, 64]),
                   op=ALU.mult)
                yield
                OP('dve', 'tensor_tensor', [Rf[5], Rf[6]], [Rf[5]], out=f[5], in0=f[5], in1=f[6], op=ALU.mult)
                yield
                OP('dve', 'tensor_scalar', [Rf[4], R_vec, R_dv], [Rf[6]], out=f[6], in0=f[4], scalar1=vec[:, 34 + j:35 + j], scalar2=dv[:, 14 + j:15 + j],
                   op0=ALU.mult, op1=ALU.add)
                yield
                OP('dve', 'tensor_tensor', [Rf[6], RKr], [Rf[6]], out=f[6], in0=f[6], in1=Kr, op=ALU.mult)
                yield
                OP('dve', 'tensor_tensor', [Rf[5], Rf[4]], [Rf[4]], out=f[4], in0=f[5], in1=f[4], op=ALU.mult)
                yield

            def prepB(j):
                Rr, Vr = pc(P_PRW + j), pc(P_PRW + 8 + j)
                RRr, RVr = R_pc[P_PRW + j], R_pc[P_PRW + 8 + j]
                OP('dve', 'scalar_tensor_tensor', [Rf[5], Rf[2]], [R_B[5]], out=arT[:, :, 0, :], in0=v4(f[5]), scalar=-1.0, in1=v4(f[2]),
                   op0=ALU.mult, op1=ALU.mult)
                OP('act', 'activation', [RVr], [R_B[4]], out=vbf, in_=Vr, func=AF.Copy)
                OP('dve', 'tensor_tensor', [Rf[4], Rf[0]], [R_B[3]], out=bhT, in0=f[4], in1=f[0], op=ALU.mult)
                OP('dve', 'tensor_tensor', [Rf[6], Rf[0]], [R_B[3]], out=khT, in0=f[6], in1=f[0], op=ALU.mult)
                OP('dve', 'tensor_tensor', [RRr, Rf[3]], [R_B[5]], out=arT[:, :, 1, :], in0=v4(Rr), in1=v4(f[3]), op=ALU.mult)
                OP('dve', 'tensor_tensor', [Rf[4], Rf[1]], [R_B[2]], out=btT, in0=f[4], in1=f[1], op=ALU.mult)
                OP('dve', 'tensor_tensor', [Rf[6], Rf[1]], [R_B[2]], out=ktT, in0=f[6], in1=f[1], op=ALU.mult)
                OP('dve', 'scalar_tensor_tensor', [RRr, R_vec, Rf[6]], [R_rk], out=rk, in0=Rr, scalar=vec[:, 38 + j:39 + j], in1=f[6],
                   op0=ALU.mult, op1=ALU.mult)
                for (src, rsrc, dst) in ((None, R_B[5], Atok), (vbf, R_B[4], Vtok), (bhT, R_B[3], Bhtok), (khT, R_B[3], Khtok)):
                    b = nb()
                    pvb = ps[b][:, :].bitcast(BF16)
                    for blk in range(4):
                        s_ap = arF[:, blk * 256:blk * 256 + 128] if src is None else src[:, blk * 128:(blk + 1) * 128]
                        T.op('pe', (lambda e, o=pvb[:, blk * 128:(blk + 1) * 128], i=s_ap: e.transpose(o, i, identb)), [rsrc, R_cb], [R_ps[b]])
                    rdst = R_pc[P_TOK] if (dst is Atok or dst is Vtok) else R_pc[P_TOK + 1]
                    OP('act', 'activation', [R_ps[b]], [rdst], out=dst, in_=pvb[:, 0:512].rearrange('p (b c) -> p b c', c=128), func=AF.Copy)

            auto_pump['w'] = WP
            for _ in prepA(0):
                pass
            for j in range(4):
                Vr = pc(P_PRW + 8 + j)
                RVr = R_pc[P_PRW + 8 + j]
                auto_pump['w'] = WP
                prepB(j)
                gA = prepA(j + 1) if j < 3 else iter(())

                def rpump(w=1.0, nA=2):
                    pump(w)
                    for _ in range(nA):
                        if next(gA, 'done') != 'done':
                            pump(WP)
                auto_pump['w'] = 0.0
                pump(2)
                CK(3)
                bY = 3
                if True:
                    chains = [(hh, blk) for hh in range(2) for blk in range(4)]
                    smv = [SM[:, ci, :] for ci in range(8)]

                    def RS(k, cis=range(8)):
                        return [R_sm[ci][k] for ci in cis]

                    def m_b(lo, hi, n):
                        return m5b[:, lo:hi].rearrange('p (o c) -> p o c', o=1).broadcast_to([128, n, hi - lo])

                    def psv(b, n, w):
                        return ps[b][:, 0:n * w].rearrange('p (c n) -> p c n', n=w)
                    for hh in range(2):
                        pb = 64 * hh
                        c0 = 4 * hh
                        b1 = nb()
                        for blk in range(4):
                            aT_ = arF[pb:pb + 64, blk * 256:blk * 256 + 128]
                            bT_ = btT[pb:pb + 64, blk * 128:(blk + 1) * 128]
                            MM(ps[b1][:, blk * 128:(blk + 1) * 128], aT_, bT_, True, True, [R_B[5], R_B[2]], [R_ps[b1]])
                        OP('dve', 'tensor_tensor', [R_ps[b1], R_c32], RS(10, range(c0, c0 + 4)), out=SM[:, c0:c0 + 4, 0:128], in0=psv(b1, 4, 128), in1=m_b(0, 128, 4), op=ALU.mult)
                        for (srcT, lo, rk_) in ((btT, 128, (11, 0)), (ktT, 384, (1, 12))):
                            for pr in range(2):
                                b2 = nb()
                                for q in range(2):
                                    blk = 2 * pr + q
                                    ar_ = arF[pb:pb + 64, blk * 256:blk * 256 + 256]
                                    xT_ = srcT[pb:pb + 64, blk * 128:(blk + 1) * 128]
                                    MM(ps[b2][:, q * 256:(q + 1) * 256], xT_, ar_, True, True, [R_B[5], R_B[2]], [R_ps[b2]])
                                cis = range(c0 + 2 * pr, c0 + 2 * pr + 2)
                                OP('dve', 'tensor_tensor', [R_ps[b2], R_c32], RS(rk_[0], cis) + RS(rk_[1], cis), out=SM[:, cis[0]:cis[0] + 2, lo:lo + 256], in0=psv(b2, 2, 256),
                                   in1=m_b(lo, lo + 256, 2), op=ALU.mult)
                    OP('dve', 'tensor_tensor', RS(11) + [R_cb], RS(4), out=SM[:, 0:8, 896:1024], in0=SM[:, 0:8, 128:256],
                       in1=identb.rearrange('p (o c) -> p o c', o=1).broadcast_to([128, 8, 128]), op=ALU.add)
                    bZ = nb()
                    for ci, (hh, blk) in enumerate(chains):
                        MM(ps[bZ][:, ci * 64:(ci + 1) * 64], smv[ci][:, 384:512], Vtok[:, blk, hh * 64:(hh + 1) * 64], True, True, [R_sm[ci][1], R_pc[P_TOK]], [R_ps[bZ]])
                    OP('act', 'activation', [R_ps[bZ]], RS(7), out=SM[:, 0:8, 1152:1216], in_=psv(bZ, 8, 64), func=AF.Copy)
                    rpump()
                    CK(3.1)
                    for lvl in range(5):
                        co, rcur_k = (640, (2,)) if lvl % 2 == 1 else (0, (10, 11))
                        no, rnxt_k = (640, (2,)) if (lvl + 1) % 2 == 1 else (0, (10, 11))
                        for p4 in range(4):
                            cis = (2 * p4, 2 * p4 + 1)
                            bC = nb()
                            for q, ci in enumerate(cis):
                                cur = smv[ci][:, co:co + 256]
                                rcur = [R_sm[ci][k_] for k_ in rcur_k]
                                MM(ps[bC][:, q * 256:q * 256 + 128], cur[:, 128:256], cur[:, 0:128], True, True, rcur, [R_ps[bC]])
                                MM(ps[bC][:, q * 256 + 128:q * 256 + 256], cur[:, 0:128], cur[:, 128:256], True, True, rcur, [R_ps[bC]])
                            wr = [R_sm[ci][k_] for ci in cis for k_ in rnxt_k]
                            if p4 % 2 == 0:
                                OP('act', 'activation', [R_ps[bC]], wr, out=SM[:, cis[0]:cis[0] + 2, no:no + 256], in_=psv(bC, 2, 256), func=AF.Copy)
                            else:
                                OP('dve', 'tensor_copy', [R_ps[bC]], wr, out=SM[:, cis[0]:cis[0] + 2, no:no + 256], in_=psv(bC, 2, 256))
                        tc_o, tck = (896, 4) if lvl % 2 == 0 else (1024, 5)
                        tn_o, tnk = (1024, 5) if lvl % 2 == 0 else (896, 4)
                        for p2 in range(2):
                            cis = range(4 * p2, 4 * p2 + 4)
                            bD = nb()
                            for q, ci in enumerate(cis):
                                MM(ps[bD][:, q * 128:(q + 1) * 128], smv[ci][:, no:no + 128], smv[ci][:, tc_o:tc_o + 128], True, True,
                                   [R_sm[ci][k_] for k_ in rnxt_k] + [R_sm[ci][tck]], [R_ps[bD]])
                            OP('dve', 'tensor_tensor', [R_ps[bD]] + RS(tck, cis), RS(tnk, cis), out=SM[:, cis[0]:cis[0] + 4, tn_o:tn_o + 128], in0=psv(bD, 4, 128),
                               in1=SM[:, cis[0]:cis[0] + 4, tc_o:tc_o + 128], op=ALU.add)
                        rpump()
                    CK(3.2)
                    for p2 in range(2):
                        cis = range(4 * p2, 4 * p2 + 4)
                        bF = nb()
                        for q, ci in enumerate(cis):
                            hh, blk = chains[ci]
                            MM(ps[bF][:, q * 128:q * 128 + 64], smv[ci][:, 1024:1152], Atok[:, blk, hh * 64:(hh + 1) * 64], True, True, [R_sm[ci][5], R_pc[P_TOK]], [R_ps[bF]])
                            MM(ps[bF][:, q * 128 + 64:q * 128 + 128], smv[ci][:, 1024:1152], smv[ci][:, 1152:1216], True, True, [R_sm[ci][5], R_sm[ci][7]], [R_ps[bF]])
                        OP('act', 'activation', [R_ps[bF]], RS(7, cis), out=SM[:, cis[0]:cis[0] + 4, 1152:1280], in_=psv(bF, 4, 128), func=AF.Copy)
                    rpump()
                    CK(3.3)
                    bG = nb()
                    for ci, (hh, blk) in enumerate(chains):
                        pb = 64 * hh
                        MM(ps[bG][pb:pb + 64, blk * 128:(blk + 1) * 128], smv[ci][:, 1152:1216], smv[ci][:, 256:384], True, True, [R_sm[ci][7], R_sm[ci][0]], [R_ps[bG]])
                    for hh in range(2):
                        pb = 64 * hh
                        OP('dve', 'tensor_tensor', [R_ps[bG], R_B[5]], RS(1, range(4 * hh, 4 * hh + 4)), out=SM[pb:pb + 64, 4 * hh:4 * hh + 4, 384:512],
                           in0=ps[bG][pb:pb + 64, 0:512].rearrange('p (c n) -> p c n', n=128), in1=arT[pb:pb + 64, 0:4, 1, :], op=ALU.add)
                    for c in range(2):
                        cs = slice(c * 64, (c + 1) * 64)
                        bH = nb()
                        for ci, (hh, blk) in enumerate(chains):
                            pb = 64 * hh
                            MM(ps[bH][pb:pb + 64, blk * 64:(blk + 1) * 64], smv[ci][cs, 1152:1216], Bhtok[cs, blk, hh * 64:(hh + 1) * 64],
                               True, True, [R_sm[ci][7], R_pc[P_TOK + 1]], [R_ps[bH]])
                        for ci, (hh, blk) in enumerate(chains):
                            pb = 64 * hh
                            OP('dve', 'scalar_tensor_tensor', [R_ps[bH], R_c32, R_wcbj[j]], [R_sm[ci][9]], out=smv[ci][pb:pb + 64, 1280 + c * 64:1280 + (c + 1) * 64],
                               in0=c32[pb:pb + 64, CI + pb:CI + pb + 64], scalar=wcb[pb:pb + 64, j, blk * 2 + c:blk * 2 + c + 1],
                               in1=ps[bH][pb:pb + 64, blk * 64:(blk + 1) * 64], op0=ALU.mult, op1=ALU.add)
                        bN = nb()
                        for ci, (hh, blk) in enumerate(chains):
                            pb = 64 * hh
                            MM(ps[bN][pb:pb + 64, blk * 64:(blk + 1) * 64], Bhtok[cs, blk, hh * 64:(hh + 1) * 64], smv[ci][cs, 1216:1280], True, False,
                               [R_pc[P_TOK + 1], R_sm[ci][7]], [R_ps[bN]])
                            MM(ps[bN][pb:pb + 64, blk * 64:(blk + 1) * 64], Khtok[cs, blk, hh * 64:(hh + 1) * 64], Vtok[cs, blk, hh * 64:(hh + 1) * 64], False, True,
                               [R_pc[P_TOK + 1], R_pc[P_TOK]], [R_ps[bN]])
                        for hh in range(2):
                            pb = 64 * hh
                            OP('act', 'activation', [R_ps[bN]], RS(2, range(4 * hh, 4 * hh + 4)), out=SM[pb:pb + 64, 4 * hh:4 * hh + 4, 640 + c * 64:640 + (c + 1) * 64],
                               in_=ps[bN][pb:pb + 64, 0:256].rearrange('p (c n) -> p c n', n=64), func=AF.Copy)
                    rpump()
                    CK(3.4)
                    for ci, (hh, blk), c in [(hh_ * 4 + blk_, (hh_, blk_), c_) for blk_ in range(4) for c_ in range(2) for hh_ in range(2)]:
                        pb = 64 * hh
                        s = smv[ci]
                        if True:
                            par = st_par[j][hh]
                            st_cur, r_cur = ST[pb:pb + 64, par, j, :], R_ST[par][j][hh]
                            st_new, r_new = ST[pb:pb + 64, 1 - par, j, :], R_ST[1 - par][j][hh]
                            st_par[j][hh] = 1 - par
                            bS = nb()
                            so = ps[bS][pb:pb + 64, 0:64]
                            MM(so, s[pb:pb + 64, 1280 + c * 64:1280 + (c + 1) * 64], st_cur, True, False, [R_sm[ci][9], r_cur], [R_ps[bS]])
                            MM(so, identb[pb:pb + 64, pb:pb + 64], s[pb:pb + 64, 640 + c * 64:640 + (c + 1) * 64], False, True, [R_cb, R_sm[ci][2]], [R_ps[bS]])
                            OP('act', 'activation', [R_ps[bS]], [r_new], out=st_new, in_=so, func=AF.Copy)
                            yo = ps[bY][pb:pb + 64, blk * 128 + c * 64:blk * 128 + (c + 1) * 64]
                            MM(yo, st_cur, s[pb:pb + 64, 384 + c * 64:384 + (c + 1) * 64], True, False, [r_cur, R_sm[ci][1]], [R_ps[bY]])
                            MM(yo, s[:, 1216:1280], s[:, 256 + c * 64:256 + (c + 1) * 64], False, False, [R_sm[ci][7], R_sm[ci][0]], [R_ps[bY]])
                            MM(yo, Vtok[:, blk, hh * 64:(hh + 1) * 64], s[:, 512 + c * 64:512 + (c + 1) * 64], False, True, [R_pc[P_TOK], R_sm[ci][12]], [R_ps[bY]])
                        if (blk * 2 + c) % 2 == 1 and hh == 1:
                            rpump(0.5, 1)

                    rpump()
                while next(gA, 'done') != 'done':
                    pump(WP)
                CK(4)
                auto_pump['w'] = 1.6 * WP
                ysb, Ry = pc(P_PRW + j), R_pc[P_PRW + j]
                tA, RtA = pc(P_PRW + 4 + j), R_pc[P_PRW + 4 + j]
                OP('act', 'activation', [R_ps[bY]], [Ry], out=ysb, in_=ps[bY][:, :], func=AF.Copy)
                ybf_ = pcb(P_B + 4)[:, 512:1024]
                OP('dve', 'tensor_copy', [Ry], [R_B[4]], out=ybf_, in_=ysb)
                OP('act', 'activation', [Ry], [R_kk2], out=kk2, in_=ysb, func=AF.Square)
                bm, bv_ = nb(), nb()
                MM(ps[bm][:, :], blk64, ybf_, True, True, [R_cb, R_B[4]], [R_ps[bm]])
                MM(ps[bv_][:, :], blk64, kk2, True, True, [R_cb, R_kk2], [R_ps[bv_]])
                bb_ = nb()
                MM(ps[bb_][:, :], blk1, rk, True, True, [R_cb, R_rk], [R_ps[bb_]])
                OP('act', 'activation', [R_ps[bm]], [RtA], out=tA, in_=ps[bm][:, :], func=AF.Square)
                OP('dve', 'tensor_tensor', [R_ps[bv_], RtA], [RtA], out=tA, in0=ps[bv_][:, :], in1=tA, op=ALU.subtract)
                OP('act', 'activation', [RtA, R_dv], [RtA], out=tA, in_=tA, func=AF.Ln, bias=eps_gn, scale=1.0)
                OP('dve', 'tensor_tensor', [Ry, R_ps[bm]], [Ry], out=ysb, in0=ysb, in1=ps[bm][:, :], op=ALU.subtract)
                OP('act', 'activation', [RtA], [RtA], out=tA, in_=tA, func=AF.Exp, scale=-0.5)
                OP('dve', 'tensor_tensor', [R_ps[bb_], RVr], [RVr], out=Vr, in0=ps[bb_][:, :], in1=Vr, op=ALU.mult)
                OP('dve', 'scalar_tensor_tensor', [Ry, RtA, R_vec], [Ry], out=ysb, in0=ysb, scalar=vec[:, 42 + j:43 + j], in1=tA, op0=ALU.mult, op1=ALU.mult)
                OP('dve', 'scalar_tensor_tensor', [Ry, RVr, R_vec], [Ry], out=ysb, in0=ysb, scalar=vec[:, 46 + j:47 + j], in1=Vr, op0=ALU.add, op1=ALU.add)
                bg = nb()
                MM(ps[bg][:, :], gupb[:, j * 128:(j + 1) * 128], sgd, True, True, [R_gup, R_pc[P_B]], [R_ps[bg]])
                OP('dve', 'tensor_tensor', [R_ps[bg], Ry], [R_abf[j]], out=abf[:, j, :], in0=ps[bg][:, :], in1=ysb, op=ALU.mult)

            CK(5)
            auto_pump['w'] = 0.0
            auto_pump['fn'] = None
            for _ in ag:
                pass
            bank_pool[:] = list(range(7))

            CK(6)
            def proj8(seq_base, src_reads_k, rhs_k, evac):
                for c in range(8):
                    slot = use_panel(seq_base + c // 4)
                    pos = c % 4
                    b = nb()
                    for k in range(8):
                        MM(ps[b][:, :], ring[:, slot, k * 512 + pos * 128:k * 512 + pos * 128 + 128], rhs_k(k), k == 0, k == 7,
                           [R_ring[slot]] + src_reads_k(k), [R_ps[b]])
                    evac(c, b)

            h = [pc(i) for i in range(8)]
            Rh = [R_pc[i] for i in range(8)]
            for c in range(8):
                DMA('sp', f'xh{c}', h[c], xT[c * 128:(c + 1) * 128, tsl], [], [Rh[c]])

            def ev_res(c, b):
                OP('dve', 'tensor_tensor', [R_ps[b], Rh[c]], [Rh[c]], out=h[c], in0=ps[b][:, :], in1=h[c], op=ALU.add)
            proj8(seq0 + 7, lambda k: [R_abf[k]], lambda k: abf[:, k, :], ev_res)

            CK(7)
            P_SQ2, P_RS2 = 30, 31

            def norm_cast(gcol):
                rs = rms_stats(lambda k: (h[k], [Rh[k]]), None, P_SQ2, P_RS2, nb())
                for k in range(8):
                    if k % 2 == 0:
                        OP('dve', 'tensor_scalar', [Rh[k], R_vec], [R_abf[k]], out=abf[:, k, :], in0=h[k],
                           scalar1=vec[:, gcol + k:gcol + k + 1], scalar2=None, op0=ALU.mult)
                    else:
                        OP('act', 'activation', [Rh[k], R_vec], [R_abf[k]], out=abf[:, k, :], in_=h[k], func=AF.Copy, scale=vec[:, gcol + k:gcol + k + 1])
                return rs
            rs2 = norm_cast(51)
            qm = pcb(24, 4).rearrange('p (c t) -> p c t', t=512)
            Rqm = Rp(24, 4)

            def ev_qm(c, b):
                OP('dve', 'scalar_tensor_tensor', [R_ps[b], R_pc[P_RS2]], Rqm, out=qm[:, c, :], in0=ps[b][:, :], scalar=1.0 / 16, in1=rs2,
                   op0=ALU.mult, op1=ALU.mult)
            proj8(seq0 + 9, lambda k: [R_abf[k]], lambda k: abf[:, k, :], ev_qm)

            CK(7.3)
            pm = pcb(28, 2).rearrange('p (m t) -> p m t', t=512)
            for hd in range(4):
                for mb in range(2):
                    b = nb()
                    for dc in range(2):
                        MM(ps[b][:, :], KmT[:, hd * 2 + dc, mb * 128:(mb + 1) * 128], qm[:, hd * 2 + dc, :], dc == 0, dc == 1, [R_KmT] + Rqm, [R_ps[b]])
                    OP('act', 'activation', [R_ps[b]], [R_pc[28 + mb // 2]], out=pm[:, mb, :], in_=ps[b][:, :], func=AF.Exp)
                bl = nb()
                for mb in range(2):
                    MM(ps[bl][:, :], ones1, pm[:, mb, :], mb == 0, mb == 1, [R_cb, R_pc[28]], [R_ps[bl]])
                rlm = pc(29)
                OP('act', 'activation', [R_ps[bl]], [R_pc[29]], out=rlm, in_=ps[bl][:, :], func=AF.Ln)
                OP('act', 'activation', [R_pc[29]], [R_pc[29]], out=rlm, in_=rlm, func=AF.Exp, scale=-1.0)
                for dc in range(2):
                    b = nb()
                    for mb in range(2):
                        MM(ps[b][:, :], Vm[:, mb, (hd * 2 + dc) * 128:(hd * 2 + dc + 1) * 128], pm[:, mb, :], mb == 0, mb == 1, [R_Vm, R_pc[28]], [R_ps[b]])
                    OP('dve', 'tensor_tensor', [R_ps[b], R_pc[29]], [R_abf[hd * 2 + dc]], out=abf[:, hd * 2 + dc, :], in0=ps[b][:, :], in1=rlm, op=ALU.mult)

            CK(7.6)
            proj8(seq0 + 11, lambda k: [R_abf[k]], lambda k: abf[:, k, :], ev_res)

            CK(8)
            rs3 = norm_cast(67)
            ubf = pcb(8, 16).rearrange('p (f t) -> p f t', t=512)
            for fo in range(32):
                slot = use_panel(seq0 + 13 + fo // 4)
                pos = fo % 4
                b = nb()
                for k in range(8):
                    MM(ps[b][:, :], ring[:, slot, k * 512 + pos * 128:k * 512 + pos * 128 + 128], abf[:, k, :], k == 0, k == 7,
                       [R_ring[slot], R_abf[k]], [R_ps[b]])
                tr = pc(28 + fo % 2)
                OP('dve', 'scalar_tensor_tensor', [R_ps[b], R_pc[P_RS2]], [R_pc[28 + fo % 2]], out=tr, in0=ps[b][:, :], scalar=0.0, in1=rs3,
                   op0=ALU.max, op1=ALU.mult)
                OP('act', 'activation', [R_pc[28 + fo % 2]], [R_pc[8 + fo // 2]], out=ubf[:, fo, :], in_=tr, func=AF.Square)
            if tau + 1 < ntiles:
                load_x_tile(tau + 1)
            CK(8.5)
            for c in range(8):
                slot = use_panel(seq0 + 21 + c)
                b = nb()
                for fo in range(32):
                    MM(ps[b][:, :], ring[:, slot, fo * 128:(fo + 1) * 128], ubf[:, fo, :], fo == 0, fo == 31, [R_ring[slot], R_pc[8 + fo // 2]], [R_ps[b]])
                ev_res(c, b)
            CK(9)
            rs4 = rms_stats(lambda k: (h[k], [Rh[k]]), None, P_SQ2, P_RS2, nb())
            for c in range(8):
                OP('dve', 'scalar_tensor_tensor', [Rh[c], R_vec, R_pc[P_RS2]], [Rh[c]], out=h[c], in0=h[c],
                   scalar=vec[:, 75 + c:76 + c], in1=rs4, op0=ALU.mult, op1=ALU.mult)
                DMA('sp', f'st{c}', outT[c * 128:(c + 1) * 128, tsl], h[c], [Rh[c]], [R_out])

        try:
            CK(0)
            load_x_tile(0)
            for tau in range(ntiles):
                tile_body(tau)
        except _Stop:
            pass

        for key in [f'st{c}' for c in range(8)]:
            if key in T.dma_cnt:
                T.streams['sp'].append(('wait', 'dma:' + key, T.dma_cnt[key] * 16))
        block = es.enter_context(nc.Block())
        T.emit(nc, block, es)
    return nc


def _panels(W, pw):
    K, N = W.shape
    kc = K // 128
    out = []
    for c0 in range(0, N, pw):
        blk = W[:, c0:c0 + pw].reshape(kc, 128, pw).transpose(1, 0, 2).reshape(128, kc * pw)
        out.append(blk)
    return out


def _prep_shared(inp):
    f = lambda k: np.asarray(inp[k], dtype=np.float32)
    w_in = f('w_in')[0]
    zpad = np.zeros((1024, 256), np.float32)
    w_in_perm = np.concatenate([w_in[:, :2560], w_in[:, 2560:2816], zpad, w_in[:, 2816:3328]], axis=1)
    pans = _panels(w_in_perm, 512)
    for k in ('w_out', 'w_mq', 'w_mo'):
        pans += _panels(f(k)[0], 512)
    pans += _panels(f('w_up')[0], 512)
    pans += _panels(f('w_down')[0], 128)
    wpan = np.ascontiguousarray(np.stack(pans, 0))
    assert wpan.shape == (NPAN, 128, 4096)
    wmkv = np.stack([_panels(f('w_mk')[0], 1024)[0], _panels(f('w_mv')[0], 1024)[0]], 0)
    wlora = np.zeros((128, 1024), np.float32)
    wlora[0:64, 0:512] = f('w_decay_up')[0]
    wlora[64:128, 0:512] = f('a_up')[0]
    wlora[:, 512:1024] = f('g_up')[0]
    vec = np.zeros((128, NV), np.float32)
    col = lambda v: np.asarray(v, np.float32).reshape(-1, 128).T
    vec[:, 0:8] = col(f('norm_mix_w')[0])
    vec[:, 8:22] = col(f('mu_shift')[0])
    vec[:, 22:26] = col(f('w_decay0')[0])
    vec[:, 26:30] = col(f('a0')[0])
    vec[:, 30:34] = col(f('k_k')[0])
    vec[:, 34:38] = col(f('k_a')[0])
    vec[:, 38:42] = col(f('r_k')[0].reshape(-1))
    vec[:, 42:46] = col(f('lnx_w')[0])
    vec[:, 46:50] = col(f('lnx_b')[0])
    vec[:, 50] = f('subln_w')[0]
    vec[:, 51:59] = col(f('norm_mem_w')[0])
    vec[:, 59:67] = col(f('norm_src_w')[0])
    vec[:, 67:75] = col(f('norm_mlp_w')[0])
    vec[:, 75:83] = col(f('norm_final_w'))
    for i, k in enumerate(('lam_q1', 'lam_k1', 'lam_q2', 'lam_k2')):
        vec[0:64, 83 + i] = f(k)[0]
    cst = np.zeros((128, NCST), np.float32)
    idx = np.arange(128)
    same = (idx[:, None] // 64) == (idx[None, :] // 64)
    mLT = ((idx[:, None] < idx[None, :]) & same).astype(np.float32)
    mInc = ((idx[:, None] <= idx[None, :]) & same).astype(np.float32)
    cst[:, CI:CI + 128] = np.eye(128, dtype=np.float32)
    cst[:, CM5:CM5 + 640] = np.concatenate([mLT.T, mLT, mInc, mLT, mInc], axis=1)
    rs = np.ones(512, np.float32)
    rs[::64] = 0.0
    cst[:, CRS:CRS + 512] = rs[None, :]
    cst[:, CBO:CBO + 128] = same.astype(np.float32)
    return dict(wpan=wpan, wmkv=np.ascontiguousarray(wmkv), wlora=wlora, vecs=vec, cst=cst)


_NC_CACHE = {}
MARKS = []


def kernel(**inputs):
    x = np.asarray(inputs['x'], dtype=np.float32)
    mem = np.asarray(inputs['mem'], dtype=np.float32)
    shared = _prep_shared(inputs)
    B = x.shape[0]
    in_maps = []
    for b in range(B):
        m = dict(shared)
        m['xT'] = np.ascontiguousarray(x[b].T)
        m['memT'] = np.ascontiguousarray(mem[b].T)
        in_maps.append(m)
    if 'nc' not in _NC_CACHE:
        _NC_CACHE['nc'] = build_nc()
    nc = _NC_CACHE['nc']
    res = run_bass_kernel_spmd(nc, in_maps, core_ids=list(range(B)))
    out = np.stack([np.ascontiguousarray(res.results[b]['outT'].T) for b in range(B)], 0)
    return out.astype(np.float32)
```

```python
import os
import numpy as np
from contextlib import ExitStack
import concourse.bass as bass
import concourse.mybir as mybir
from concourse.bass_utils import run_bass_kernel_spmd

F32 = mybir.dt.float32
BF16 = mybir.dt.bfloat16
AF = mybir.ActivationFunctionType
ALU = mybir.AluOpType
ENGS = ('pe', 'act', 'dve', 'pool', 'sp')

S = 4096
D = 1024
TT_ = 512
NT = S // TT_
NV = 87
C0 = float(np.exp(-0.5))
CI, CM5, CRS, CBO, NCST = 0, 128, 768, 1280, 1408
NPAN = 29
WIN_POS = [(c // 4, c % 4) for c in range(16)] + [(4, 0), (4, 1), (4, 2), (4, 3), (5, 0), (5, 1),
                                                    (6, 0), (6, 1), (6, 2), (6, 3)]


class Res:
    __slots__ = ('name', 'w', 'rs')

    def __init__(self, name=''):
        self.name = name
        self.w = None
        self.rs = {}


class Trk:
    def __init__(self):
        self.streams = {e: [] for e in ENGS}
        self.cnt = {e: 0 for e in ENGS}
        self.known = {e: {} for e in ENGS}
        self.dma_cnt = {}

    def _deps(self, eng, reads, writes):
        deps = {}
        for r in reads:
            if r.w is not None and deps.get(r.w[0], 0) < r.w[1]:
                deps[r.w[0]] = r.w[1]
        for r in writes:
            if r.w is not None and deps.get(r.w[0], 0) < r.w[1]:
                deps[r.w[0]] = r.w[1]
            for k, i in r.rs.items():
                if deps.get(k, 0) < i:
                    deps[k] = i
        kn = self.known[eng]
        for k, i in deps.items():
            if k == eng and eng in ('pe', 'sp'):
                continue
            if kn.get(k, 0) >= i:
                continue
            kn[k] = i
            self.streams[eng].append(('wait', k, i))

    def op(self, eng, fn, reads=(), writes=()):
        self._deps(eng, reads, writes)
        self.cnt[eng] += 1
        i = self.cnt[eng]
        self.streams[eng].append(('op', fn))
        for r in reads:
            if r.rs.get(eng, 0) < i:
                r.rs[eng] = i
        for r in writes:
            r.w = (eng, i)
            r.rs = {}

    def dma(self, eng, semkey, fn, reads=(), writes=()):
        self._deps(eng, reads, writes)
        self.dma_cnt[semkey] = self.dma_cnt.get(semkey, 0) + 1
        k = 'dma:' + semkey
        i = self.dma_cnt[semkey] * 16
        self.streams[eng].append(('dma', fn, semkey))
        for r in reads:
            if r.rs.get(k, 0) < i:
                r.rs[k] = i
        for r in writes:
            r.w = (k, i)
            r.rs = {}

    def emit(self, nc, block, es):
        sems = {}
        for e in ENGS:
            sems[e] = es.enter_context(nc.semaphore('sem_' + e))
        for k in self.dma_cnt:
            sems['dma:' + k] = es.enter_context(nc.semaphore('dsem_' + k))
        streams = self.streams

        def run(engname, eng):
            own = sems[engname]
            for ent in streams[engname]:
                if ent[0] == 'wait':
                    eng.wait_ge(sems[ent[1]], ent[2])
                elif ent[0] == 'op':
                    ent[1](eng).then_inc(own, 1)
                else:
                    ent[1](eng).then_inc(sems['dma:' + ent[2]], 16)

        block.tensor(lambda eng: run('pe', eng))
        block.scalar(lambda eng: run('act', eng))
        block.vector(lambda eng: run('dve', eng))
        block.gpsimd(lambda eng: run('pool', eng))
        block.sync(lambda eng: run('sp', eng))


def build_nc(ntiles=NT, dbg=None, stage_limit=99):
    nc = bass.Bass("TRN2", target_bir_lowering=False)
    xT = nc.dram_tensor("xT", [D, S], F32, kind="ExternalInput").ap()
    memT = nc.dram_tensor("memT", [D, 256], F32, kind="ExternalInput").ap()
    wpan = nc.dram_tensor("wpan", [NPAN, 128, 4096], F32, kind="ExternalInput").ap()
    wmkv = nc.dram_tensor("wmkv", [2, 128, 8192], F32, kind="ExternalInput").ap()
    wlora = nc.dram_tensor("wlora", [128, 1024], F32, kind="ExternalInput").ap()
    vecs = nc.dram_tensor("vecs", [128, NV], F32, kind="ExternalInput").ap()
    cst = nc.dram_tensor("cst", [128, NCST], F32, kind="ExternalInput").ap()
    outT = nc.dram_tensor("outT", [D, S], F32, kind="ExternalOutput").ap()
    wbf = nc.dram_tensor("wbf", [NPAN, 128, 4096], BF16).ap()
    dbg_out = None
    if dbg:
        dbg_out = nc.dram_tensor("dbg", [128, 16, 512], F32, kind="ExternalOutput").ap()

    T = Trk()
    es = ExitStack()
    with es:
        def sb(n, s, d):
            return es.enter_context(nc.sbuf_tensor(n, s, d))

        vec = sb('vec', [128, NV], F32)
        dv = sb('dvec', [128, 32], F32)
        c32 = sb('c32', [128, 128], F32)
        m5b = sb('m5b', [128, 640], BF16)
        rsb = sb('rsb', [128, 512], BF16)
        cb = sb('cb', [128, 6, 128], BF16)
        Kc = sb('Kc', [128, 4, S], BF16)
        Vc = sb('Vc', [128, S // 128, 512], BF16)
        KmT = sb('KmT', [128, 8, 256], BF16)
        Vm = sb('Vm', [128, 2, 1024], BF16)
        lorab = sb('lorab', [128, 512], BF16)
        gupb = sb('gupb', [128, 512], BF16)
        ring = sb('ring', [128, 3, 4096], BF16)
        abf = sb('abf', [128, 8, 512], BF16)
        NPIECE = 36
        AR = sb('AR', [128, NPIECE * 512], F32)
        SM = sb('SM', [128, 8, 1408], BF16)
        carry = sb('carry', [128, 16], F32)
        ST = sb('ST', [128, 2, 4, 64], BF16)
        wcb = sb('wcb', [128, 4, 8], F32)
        rtok = sb('rtok', [128, 4], F32)
        ones32 = sb('ones32', [128, 128], F32)
        ps = [es.enter_context(nc.psum_tensor(f'ps{i}', [128, 512], F32)) for i in range(8)]

        R_vec, R_dv, R_c32, R_cb = Res('vec'), Res('dv'), Res('c32'), Res('cb')
        R_Kc = [Res(f'Kc{t}') for t in range(NT)]
        R_Vc = [Res(f'Vc{t}') for t in range(NT)]
        R_KmT, R_Vm, R_lora, R_gup = Res('KmT'), Res('Vm'), Res('lora'), Res('gup')
        R_ring = [Res(f'ring{i}') for i in range(3)]
        R_abf = [Res(f'abf{i}') for i in range(8)]
        R_pc = [Res(f'pc{i}') for i in range(NPIECE)]
        R_carry, R_wcb, R_rtok = Res('carry'), Res('wcb'), Res('rtok')
        R_ST = [[[Res(f'ST{p}_{j}_{h}') for h in range(2)] for j in range(4)] for p in range(2)]
        st_par = [[0, 0] for _ in range(4)]
        R_ps = [Res(f'ps{i}') for i in range(8)]
        R_wbf = [Res(f'wbf{i}') for i in range(NPAN)]
        R_out = Res('out')

        auto_pump = {'w': 0.0, 'fn': None, 'busy': False}

        def _ap():
            if auto_pump['w'] > 0 and auto_pump['fn'] is not None and not auto_pump['busy']:
                auto_pump['busy'] = True
                auto_pump['fn'](auto_pump['w'])
                auto_pump['busy'] = False

        def OP(eng, meth, reads, writes, *a, **k):
            T.op(eng, (lambda e: getattr(e, meth)(*a, **k)), reads, writes)
            _ap()

        def MM(out, lhsT, rhs, start, stop, reads, writes):
            T.op('pe', (lambda e: e.matmul(out, lhsT=lhsT, rhs=rhs, start=start, stop=stop)), reads, writes)
            if stop:
                _ap()

        def DMA(eng, key, out, in_, reads, writes):
            T.dma(eng, key, (lambda e: e.dma_start(out=out, in_=in_)), reads, writes)

        def pc(i, n=1):
            return AR[:, i * 512:(i + n) * 512]

        def pcb(i, n=1):
            return AR[:, i * 512:(i + n) * 512].bitcast(BF16)

        def Rp(i, n=1):
            return R_pc[i:i + n]

        bank_ctr = [0]

        bank_pool = list(range(7))

        def nb():
            b = bank_pool[bank_ctr[0] % len(bank_pool)]
            bank_ctr[0] += 1
            return b

        identb, blk1, blk64, ones1024, ones128, ones1 = [cb[:, i, :] for i in range(6)]
        ident32 = c32[:, CI:CI + 128]
        eps_rms = dv[:, 20:21]
        eps_gn = dv[:, 21:22]

        DMA('sp', 'ld0', vec[:], vecs, [], [R_vec])
        DMA('sp', 'ld1', c32[:], cst[:, 0:128], [], [R_c32])
        DMA('pool', 'ld5a', m5b[:], cst[:, CM5:CM5 + 640], [], [R_c32])
        DMA('pool', 'ld5b', rsb[:], cst[:, CRS:CRS + 512], [], [R_c32])
        DMA('pool', 'ld5c', cb[:, 1, :], cst[:, CBO:CBO + 128], [], [R_cb])
        SKIP = []

        def cast_panel(pi, deps=()):
            key = f'wc{pi}' if pi < 7 else ('wcB' if pi < 13 else 'wcC')
            DMA('pool', key, wbf[pi], wpan[pi], list(deps), [R_wbf[pi]])

        DMA('pool', 'ld2', lorab[:], wlora[:, 0:512], [], [R_lora])
        DMA('pool', 'ld3', gupb[:], wlora[:, 512:1024], [], [R_gup])
        wkv = [pcb(8, 8), pcb(16, 8)]
        DMA('pool', 'ld4a', wkv[0], wmkv[0], [], Rp(8, 8))
        DMA('pool', 'ld4b', wkv[1], wmkv[1], [], Rp(16, 8))
        for pi in range(13):
            cast_panel(pi)
        OP('dve', 'tensor_copy', [R_c32], [R_cb], out=identb, in_=ident32)
        OP('dve', 'tensor_scalar', [R_cb], [R_cb], out=blk64, in0=blk1, scalar1=1.0 / 64, scalar2=None,
           op0=ALU.mult)
        OP('pool', 'memset', [], [R_cb], ones1024, 1.0 / 1024)
        OP('pool', 'memset', [], [R_cb], ones128, 1.0 / 128)
        OP('pool', 'memset', [], [R_cb], ones1, 1.0)
        OP('pool', 'memset', [], [R_cb], ones32[:], 1.0)
        OP('dve', 'tensor_scalar', [R_vec], [R_dv], out=dv[:, 0:14], in0=vec[:, 8:22], scalar1=-1.0, scalar2=1.0,
           op0=ALU.mult, op1=ALU.add)
        OP('dve', 'tensor_scalar', [R_vec], [R_dv], out=dv[:, 14:18], in0=vec[:, 34:38], scalar1=-1.0, scalar2=1.0,
           op0=ALU.mult, op1=ALU.add)
        OP('dve', 'tensor_scalar', [R_vec], [R_dv], out=dv[:, 18:19], in0=vec[:, 50:51], scalar1=0.8, scalar2=None,
           op0=ALU.mult)
        OP('pool', 'memset', [], [R_dv], dv[:, 20:21], 1e-5)
        OP('pool', 'memset', [], [R_dv], dv[:, 21:22], 64e-5)
        OP('pool', 'memset', [], [R_carry], carry[:], 0.0)
        OP('pool', 'memset', [], [r for a in R_ST for b in a for r in b], ST[:], 0.0)
        OP('dve', 'tensor_tensor', [R_vec], [R_dv], out=dv[:, 22:23], in0=vec[:, 83:84], in1=vec[:, 84:85], op=ALU.mult)
        OP('dve', 'tensor_tensor', [R_vec], [R_dv], out=dv[:, 23:24], in0=vec[:, 85:86], in1=vec[:, 86:87], op=ALU.mult)
        OP('pool', 'memset', [], [R_pc[0]], pc(0)[:, 0:128], 1.0)
        if 'lam' not in SKIP:
          MM(ps[0][:, 0:2], pc(0)[:, 0:128], dv[:, 22:24], True, True, [R_pc[0], R_dv], [R_ps[0]])
        OP('act', 'activation', [R_ps[0]], [R_dv], out=dv[:, 24:26], in_=ps[0][:, 0:2], func=AF.Exp)
        OP('dve', 'tensor_tensor', [R_dv], [R_dv], out=dv[:, 26:27], in0=dv[:, 25:26], in1=dv[:, 24:25], op=ALU.subtract)
        OP('dve', 'tensor_scalar', [R_dv], [R_dv], out=dv[:, 19:20], in0=dv[:, 26:27], scalar1=-0.2, scalar2=None,
           op0=ALU.add)

        if 'mem' in SKIP:
            stage_limit = -1
        m32 = pc(0, 4).rearrange('p (k m) -> p k m', m=256)
        DMA('sp', 'ld2b', m32, memT.rearrange('(k p) m -> p k m', p=128), [], Rp(0, 4))
        msq = pcb(6, 1)
        msq = pcb(6, 2).rearrange('p (k m) -> p k m', m=256)
        memn = pcb(4, 2).rearrange('p (k m) -> p k m', m=256)
        OP('act', 'activation', Rp(0, 4), Rp(6, 2), out=msq, in_=m32, func=AF.Square)
        for k in range(8):
            MM(ps[1][:, 0:256], ones1024, msq[:, k, :], k == 0, k == 7, [R_cb] + Rp(6, 2), [R_ps[1]])
        mr = pc(24)[:, 0:256]
        OP('act', 'activation', [R_ps[1], R_dv], [R_pc[24]], out=mr, in_=ps[1][:, 0:256], func=AF.Ln, bias=eps_rms, scale=1.0)
        OP('act', 'activation', [R_pc[24]], [R_pc[24]], out=mr, in_=mr, func=AF.Exp, scale=-0.5)
        for k in range(8):
            OP('dve', 'scalar_tensor_tensor', Rp(0, 4) + [R_pc[24], R_vec], Rp(4, 2), out=memn[:, k, :], in0=m32[:, k, :],
               scalar=vec[:, 59 + k:60 + k], in1=mr, op0=ALU.mult, op1=ALU.mult)
        for which in range(2):
            wv = wkv[which].rearrange('p (k n) -> p k n', n=1024)
            Rw = Rp(8 + 8 * which, 8)
            if which == 0:
                for c in range(8):
                    b = nb()
                    for k in range(8):
                        MM(ps[b][:, 0:256], wv[:, k, c * 128:(c + 1) * 128], memn[:, k, :], k == 0, k == 7,
                           Rw + Rp(4, 2), [R_ps[b]])
                    OP('act', 'activation', [R_ps[b]], [R_KmT], out=KmT[:, c, :], in_=ps[b][:, 0:256], func=AF.Copy)
            else:
                for mb in range(2):
                    for nh in range(2):
                        b = nb()
                        for k in range(8):
                            MM(ps[b][:, :], memn[:, k, mb * 128:(mb + 1) * 128], wv[:, k, nh * 512:(nh + 1) * 512], k == 0, k == 7,
                               Rw + Rp(4, 2), [R_ps[b]])
                        OP('act', 'activation', [R_ps[b]], [R_Vm], out=Vm[:, mb, nh * 512:(nh + 1) * 512], in_=ps[b][:, :], func=AF.Copy)

        ring_state = {'next': 0}
        panel_slot = {}

        def load_panel(seq):
            if seq >= ntiles * NPAN or seq in panel_slot:
                return
            assert seq == ring_state['next']
            ring_state['next'] += 1
            slot = seq % 3
            panel_slot[seq] = slot
            pi = seq % NPAN
            gl = pi if pi < 7 else (12 if pi < 13 else NPAN - 1)
            DMA('sp', f'ring{slot}', ring[:, slot, :], wbf[pi], [R_wbf[gl]], [R_ring[slot]])

        def use_panel(seq):
            load_panel(seq)
            load_panel(seq + 1)
            load_panel(seq + 2)
            return panel_slot[seq]

        P_PRW = 0
        P_F = 14
        P_B = 21
        P_TOK = 27
        P_YSB = 16
        P_TMP = 29
        P_RSTD = 31

        def rms_stats(src_chunks_fn, nsrc_reads, sq_piece, rstd_piece, bank):
            sqv = pcb(sq_piece).rearrange('p (h t) -> p h t', t=512)
            for k in range(8):
                src, rr = src_chunks_fn(k)
                OP('act', 'activation', rr, [R_pc[sq_piece]], out=sqv[:, k % 2, :], in_=src, func=AF.Square)
                MM(ps[bank][:, :], ones1024, sqv[:, k % 2, :], k == 0, k == 7, [R_cb, R_pc[sq_piece]], [R_ps[bank]])
            rs = pc(rstd_piece)
            OP('act', 'activation', [R_ps[bank], R_dv], [R_pc[rstd_piece]], out=rs, in_=ps[bank][:, :], func=AF.Ln, bias=eps_rms, scale=1.0)
            OP('act', 'activation', [R_pc[rstd_piece]], [R_pc[rstd_piece]], out=rs, in_=rs, func=AF.Exp, scale=-0.5)
            return rs

        R_sm = [[Res(f'sm{ci}_{n}') for n in range(13)] for ci in range(8)]
        R_pt = [Res(f'pt{i}') for i in range(4)]
        R_kk2rk = (Res('kk2'), Res('rk'))
        R_wcbj = [Res(f'wcb{j}') for j in range(4)]

        class _Stop(Exception):
            pass

        def CK(n):
            MARKS.append((n, dict(T.cnt)))
            if stage_limit <= n:
                raise _Stop()

        P_XS, P_XSQ, P_XR = 24, 26, 27

        def load_x_tile(tau):
            tsl = slice(tau * 512, (tau + 1) * 512)
            sqv = pcb(P_XSQ).rearrange('p (h t) -> p h t', t=512)
            bank_ms = nb()
            for k in range(8):
                stg = pc(P_XS + (k % 2))
                DMA('sp', f'xs{k % 2}', stg, xT[k * 128:(k + 1) * 128, tsl], [], [R_pc[P_XS + (k % 2)]])
                OP('act', 'activation', [R_pc[P_XS + (k % 2)]], [R_pc[P_XSQ]], out=sqv[:, k % 2, :], in_=stg, func=AF.Square)
                MM(ps[bank_ms][:, :], ones1024, sqv[:, k % 2, :], k == 0, k == 7, [R_cb, R_pc[P_XSQ]], [R_ps[bank_ms]])
                if k % 2 == 0:
                    OP('dve', 'tensor_scalar', [R_pc[P_XS + (k % 2)], R_vec], [R_abf[k]], out=abf[:, k, :], in0=stg,
                       scalar1=vec[:, k:k + 1], scalar2=None, op0=ALU.mult)
                else:
                    OP('act', 'activation', [R_pc[P_XS + (k % 2)], R_vec], [R_abf[k]], out=abf[:, k, :], in_=stg, func=AF.Copy, scale=vec[:, k:k + 1])
            rstd = pc(P_XR)
            OP('act', 'activation', [R_ps[bank_ms], R_dv], [R_pc[P_XR]], out=rstd, in_=ps[bank_ms][:, :], func=AF.Ln, bias=eps_rms, scale=1.0)
            OP('act', 'activation', [R_pc[P_XR]], [R_pc[P_XR]], out=rstd, in_=rstd, func=AF.Exp, scale=-0.5)
            bt_ = nb()
            for tb in range(4):
                MM(ps[bt_][:, tb:tb + 1], rstd[:, tb * 128:(tb + 1) * 128], ident32[:, 0:1], True, True, [R_pc[P_XR], R_c32], [R_ps[bt_]])
            OP('dve', 'tensor_copy', [R_ps[bt_]], [R_rtok], out=rtok[:, :], in_=ps[bt_][:, 0:4])

        def tile_body(tau):
            tsl = slice(tau * 512, (tau + 1) * 512)
            seq0 = tau * NPAN
            use_panel(seq0)
            rstd = pc(P_XR)
            CK(1)
            qbf = pcb(P_TMP, 2).rearrange('p (c t) -> p c t', t=512)
            R_qbf = Rp(P_TMP, 2)
            for c in range(22):
                pan, pos = WIN_POS[c]
                slot = use_panel(seq0 + pan)
                b = nb()
                for k in range(8):
                    MM(ps[b][:, :], ring[:, slot, k * 512 + pos * 128:k * 512 + pos * 128 + 128], abf[:, k, :], k == 0, k == 7,
                       [R_ring[slot], R_abf[k]], [R_ps[b]])
                if c < 14:
                    p = pc(P_PRW + c)
                    Rc = R_pc[P_PRW + c]
                    OP('dve', 'tensor_tensor', [R_ps[b], R_pc[P_XR]], [Rc], out=p, in0=ps[b][:, :], in1=rstd, op=ALU.mult)
                    tp = P_TMP + (c % 2)
                    tmp = pc(tp)
                    OP('act', 'activation', [Rc, R_vec], [R_pc[tp]], out=tmp[:, 1:512], in_=p[:, 0:511], func=AF.Copy, scale=vec[:, 8 + c:9 + c])
                    OP('act', 'activation', [R_carry, R_vec], [R_pc[tp]], out=tmp[:, 0:1], in_=carry[:, c:c + 1], func=AF.Copy, scale=vec[:, 8 + c:9 + c])
                    OP('act', 'activation', [Rc], [R_carry], out=carry[:, c:c + 1], in_=p[:, 511:512], func=AF.Copy)
                    OP('dve', 'scalar_tensor_tensor', [Rc, R_pc[tp], R_dv], [Rc], out=p, in0=p, scalar=dv[:, c:c + 1], in1=tmp,
                       op0=ALU.mult, op1=ALU.add)
                elif c < 18:
                    OP('dve', 'scalar_tensor_tensor', [R_ps[b], R_pc[P_XR]], R_qbf, out=qbf[:, c - 14, :], in0=ps[b][:, :], scalar=0.125,
                       in1=rstd, op0=ALU.mult, op1=ALU.mult)
                else:
                    OP('dve', 'tensor_tensor', [R_ps[b], R_pc[P_XR]], [R_Kc[tau]], out=Kc[:, c - 18, tsl], in0=ps[b][:, :], in1=rstd, op=ALU.mult)
            slot = use_panel(seq0 + 6)
            for tb in range(4):
                b = nb()
                for k in range(8):
                    MM(ps[b][:, :], abf[:, k, tb * 128:(tb + 1) * 128], ring[:, slot, k * 512:(k + 1) * 512], k == 0, k == 7,
                       [R_ring[slot], R_abf[k]], [R_ps[b]])
                OP('act', 'activation', [R_ps[b], R_rtok], [R_Vc[tau]], out=Vc[:, tau * 4 + tb, :], in_=ps[b][:, :], func=AF.Copy,
                   scale=rtok[:, tb:tb + 1])

            if tau == 0:
                for pi in range(13, NPAN):
                    cast_panel(pi, deps=[R_Kc[0], R_Vc[0]])
            nkb = 4 * tau + 4
            ot1, ot2, rl = pc(34), pc(35), pc(31)
            accL = [pc(32), pc(33)]
            R_ot1, R_ot2, R_rl = R_pc[34], R_pc[35], R_pc[31]
            R_acc = [R_pc[32], R_pc[33]]
            bO = [4, 5]
            bST = [6, 7]
            st_ctr = [0]

            def attn_gen():
                for hd in range(4):
                    units = [(kb, m) for kb in range(nkb) for m in range(2)]
                    LA = 1
                    uinfo = {}

                    def u_qk(i):
                        kb, m = units[i]
                        qoff = max(0, kb - 4 * tau) * 128
                        ktile = kb // 4
                        if PAIR:
                            bs = bST[m]
                            ip = (kb % 2) * 2 + m
                        else:
                            bs = bST[st_ctr[0] % 2]
                            ip = st_ctr[0] % 4
                            st_ctr[0] += 1
                        mp = slice(64 * m, 64 * m + 64)
                        MM(ps[bs][:, qoff:512], Kc[mp, hd, kb * 128:(kb + 1) * 128], qbf[mp, hd, qoff:512], True, True, [R_Kc[ktile]] + R_qbf, [R_ps[bs]])
                        pt = pcb(P_PRW + 12 + (ip // 2))[:, (ip % 2) * 512:(ip % 2) * 512 + 512]
                        Rpt = R_pt[ip]
                        uinfo[i] = (pt, Rpt, qoff, ktile, bs)
                        if not PAIR:
                            u_exp(i)

                    def u_exp(i):
                        kb, m = units[i]
                        pt, Rpt, qoff, ktile, bs = uinfo[i]
                        OP('act', 'activation', [R_ps[bs]], [Rpt], out=pt[:, qoff:512], in_=ps[bs][:, qoff:512], func=AF.Exp)
                        if kb >= 4 * tau:
                            OP('pool', 'memset', [], [Rpt], pt[64:128, qoff:qoff + 64], 0.0)

                    def u_pv(i):
                        kb, m = units[i]
                        pt, Rpt, qoff, ktile, _bs = uinfo.pop(i)
                        MM(ps[bO[m]][:, qoff:512], Vc[:, kb, hd * 128:(hd + 1) * 128], pt[:, qoff:512], kb == 0, kb == nkb - 1, [R_Vc[ktile], Rpt], [R_ps[bO[m]]])
                        if kb == 0:
                            OP(ACC_ENG, 'tensor_copy', [Rpt], [R_acc[m]], out=accL[m], in_=pt)
                        else:
                            OP(ACC_ENG, 'tensor_tensor', [Rpt, R_acc[m]], [R_acc[m]], out=accL[m][:, qoff:512], in0=accL[m][:, qoff:512], in1=pt[:, qoff:512], op=ALU.add)

                    if PAIR:
                        for kb_ in range(nkb + 1):
                            if kb_ < nkb:
                                u_qk(2 * kb_)
                                u_qk(2 * kb_ + 1)
                                u_exp(2 * kb_)
                                u_exp(2 * kb_ + 1)
                            if kb_ >= 1:
                                u_pv(2 * kb_ - 2)
                                u_pv(2 * kb_ - 1)
                            yield
                            yield
                    else:
                        for i in range(len(units) + LA):
                            if i < len(units):
                                u_qk(i)
                            if i >= LA:
                                u_pv(i - LA)
                            yield
                    bl0 = bST[st_ctr[0] % 2]
                    st_ctr[0] += 1
                    MM(ps[bl0][:, :], ones32[:], accL[0], True, True, [R_cb, R_acc[0]], [R_ps[bl0]])
                    OP('act', 'activation', [R_ps[bl0]], [R_rl], out=rl, in_=ps[bl0][:, :], func=AF.Ln)
                    OP('act', 'activation', [R_rl], [R_rl], out=rl, in_=rl, func=AF.Exp, scale=-1.0)
                    OP('dve', 'tensor_tensor', [R_ps[bO[0]], R_rl], [R_ot1], out=ot1, in0=ps[bO[0]][:, :], in1=rl, op=ALU.mult)
                    yield
                    bl1 = bST[st_ctr[0] % 2]
                    st_ctr[0] += 1
                    MM(ps[bl1][:, :], ones32[:], accL[1], True, True, [R_cb, R_acc[1]], [R_ps[bl1]])
                    OP('act', 'activation', [R_ps[bl1]], [R_rl], out=rl, in_=ps[bl1][:, :], func=AF.Ln)
                    OP('act', 'activation', [R_rl], [R_rl], out=rl, in_=rl, func=AF.Exp, scale=-1.0)
                    OP('dve', 'tensor_scalar', [R_rl, R_dv], [R_rl], out=rl, in0=rl, scalar1=dv[:, 19:20], scalar2=None, op0=ALU.mult)
                    OP('dve', 'tensor_tensor', [R_ps[bO[1]], R_rl], [R_ot2], out=ot2, in0=ps[bO[1]][:, :], in1=rl, op=ALU.mult)
                    yield
                    OP('dve', 'tensor_tensor', [R_ot1, R_ot2], [R_ot1], out=ot1, in0=ot1, in1=ot2, op=ALU.add)
                    osq = rl.bitcast(BF16)[:, 0:512]
                    OP('act', 'activation', [R_ot1], [R_rl], out=osq, in_=ot1, func=AF.Square)
                    bq = bST[st_ctr[0] % 2]
                    st_ctr[0] += 1
                    MM(ps[bq][:, :], ones128, osq, True, True, [R_cb, R_rl], [R_ps[bq]])
                    OP('act', 'activation', [R_ps[bq], R_dv], [R_ot2], out=ot2, in_=ps[bq][:, :], func=AF.Ln, bias=eps_rms, scale=1.0)
                    OP('act', 'activation', [R_ot2], [R_ot2], out=ot2, in_=ot2, func=AF.Exp, scale=-0.5)
                    OP('dve', 'scalar_tensor_tensor', [R_ot1, R_ot2, R_dv], [R_abf[4 + hd]], out=abf[:, 4 + hd, :], in0=ot1, scalar=dv[:, 18:19],
                       in1=ot2, op0=ALU.mult, op1=ALU.mult)
                    yield

            ag = attn_gen()
            n_attn_steps = 4 * (2 * nkb + 1 + 3)
            WP = 1.0
            ACC_ENG = 'dve'
            PAIR = True
            BURST = 1.0
            pump_rate = 1.08 * n_attn_steps / (4 * (45 * WP + 14))
            pump_acc = [0.0]

            def pump(w=1.0):
                pump_acc[0] += w * pump_rate
                was = auto_pump['busy']
                auto_pump['busy'] = True
                if pump_acc[0] >= BURST:
                    while pump_acc[0] >= 1.0:
                        pump_acc[0] -= 1.0
                        next(ag, None)
                auto_pump['busy'] = was

            auto_pump['fn'] = pump

            bank_pool[:] = [0, 1, 2]

            CK(2)
            tw = pcb(P_B)[:, 0:512]
            sgd = pcb(P_B)[:, 512:1024]
            wda = pc(P_PRW + 12)
            OP('act', 'activation', [R_pc[P_PRW + 12]], [R_pc[P_B]], out=tw[0:64, :], in_=wda[0:64, :], func=AF.Tanh)
            OP('act', 'activation', [R_pc[P_PRW + 12]], [R_pc[P_B]], out=tw[64:128, :], in_=wda[64:128, :], func=AF.Copy)
            OP('act', 'activation', [R_pc[P_PRW + 13]], [R_pc[P_B]], out=sgd, in_=pc(P_PRW + 13), func=AF.Sigmoid)
            f = [pc(P_F + i) for i in range(7)]
            Rf = [R_pc[P_F + i] for i in range(7)]
            kk2 = pcb(P_B + 1)[:, 0:512]
            rk = pcb(P_B + 1)[:, 512:1024]
            btT = pcb(P_B + 2)[:, 0:512]
            ktT = pcb(P_B + 2)[:, 512:1024]
            bhT = pcb(P_B + 3)[:, 0:512]
            khT = pcb(P_B + 3)[:, 512:1024]
            vbf = pcb(P_B + 4)[:, 0:512]
            arT = pcb(P_B + 5).rearrange('p (b s t) -> p b s t', s=2, t=128)
            arF = pcb(P_B + 5)
            Atok = pcb(P_TOK)[:, 0:512].rearrange('p (b c) -> p b c', c=128)
            Vtok = pcb(P_TOK)[:, 512:1024].rearrange('p (b c) -> p b c', c=128)
            Bhtok = pcb(P_TOK + 1)[:, 0:512].rearrange('p (b c) -> p b c', c=128)
            Khtok = pcb(P_TOK + 1)[:, 512:1024].rearrange('p (b c) -> p b c', c=128)
            R_B = [R_pc[P_B + i] for i in range(6)]
            v4 = lambda ap: ap.rearrange('p (b t) -> p b t', t=128)
            v8 = lambda ap: ap.rearrange('p (c t) -> p c t', t=64)
            R_kk2, R_rk = R_kk2rk
            R_wj = R_wcbj

            def prepA(j):
                Kr = pc(P_PRW + 4 + j)
                RKr = R_pc[P_PRW + 4 + j]
                bd = nb()
                MM(ps[bd][:, :], lorab[0:64, j * 128:(j + 1) * 128], tw[0:64, :], True, True, [R_lora, R_pc[P_B]], [R_ps[bd]])
                ba = nb()
                MM(ps[ba][:, :], lorab[64:128, j * 128:(j + 1) * 128], tw[64:128, :], True, True, [R_lora, R_pc[P_B]], [R_ps[ba]])
                OP('act', 'activation', [R_ps[bd], R_vec], [Rf[0]], out=f[0], in_=ps[bd][:, :], func=AF.Sigmoid, bias=vec[:, 22 + j:23 + j], scale=1.0)
                OP('act', 'activation', [R_ps[ba], R_vec], [Rf[4]], out=f[4], in_=ps[ba][:, :], func=AF.Sigmoid, bias=vec[:, 26 + j:27 + j], scale=1.0)
                yield
                OP('act', 'activation', [RKr, R_vec], [Rf[5]], out=f[5], in_=Kr, func=AF.Copy, scale=vec[:, 30 + j:31 + j])
                OP('dve', 'tensor_tensor_scan', [Rf[0], R_c32], [Rf[1]], out=f[1], data0=rsb[:], data1=f[0], initial=0.0,
                   op0=ALU.mult, op1=ALU.add)
                yield
                OP('act', 'activation', [Rf[5]], [R_kk2], out=kk2, in_=f[5], func=AF.Square)
                OP('dve', 'tensor_tensor', [Rf[0], Rf[1]], [Rf[2]], out=f[2], in0=f[1], in1=f[0], op=ALU.subtract)
                yield
                OP('act', 'activation', [Rf[1]], [Rf[3]], out=f[3], in_=f[1], func=AF.Exp, scale=-C0)
                bq = nb()
                MM(ps[bq][:, :], blk1, kk2, True, True, [R_cb, R_kk2], [R_ps[bq]])
                OP('dve', 'tensor_scalar', [R_ps[bq]], [Rf[6]], out=f[6], in0=ps[bq][:, :], scalar1=1e-24, scalar2=None, op0=ALU.max)
                yield
                OP('act', 'activation', [Rf[2]], [Rf[2]], out=f[2], in_=f[2], func=AF.Exp, scale=-C0)
                OP('act', 'activation', [Rf[1]], [Rf[1]], out=f[1], in_=f[1], func=AF.Exp, scale=C0)
                yield
                OP('act', 'activation', [Rf[6]], [Rf[6]], out=f[6], in_=f[6], func=AF.Ln)
                OP('dve', 'tensor_copy', [Rf[3]], [R_wj[j]], out=wcb[:, j, :], in_=v8(f[3])[:, :, 63])
                yield
                OP('act', 'activation', [Rf[6]], [Rf[6]], out=f[6], in_=f[6], func=AF.Exp, scale=-0.5)
                OP('dve', 'tensor_tensor', [Rf[1], Rf[3]], [Rf[0]], out=v8(f[0]), in0=v8(f[1]), in1=v8(f[3])[:, :, 63:64].broadcast_to([128, 8, 64]),
                   op=ALU.mult)
                yield
                OP('dve', 'tensor_tensor', [Rf[5], Rf[6]], [Rf[5]], out=f[5], in0=f[5], in1=f[6], op=ALU.mult)
                yield
                OP('dve', 'tensor_scalar', [Rf[4], R_vec, R_dv], [Rf[6]], out=f[6], in0=f[4], scalar1=vec[:, 34 + j:35 + j], scalar2=dv[:, 14 + j:15 + j],
                   op0=ALU.mult, op1=ALU.add)
                yield
                OP('dve', 'tensor_tensor', [Rf[6], RKr], [Rf[6]], out=f[6], in0=f[6], in1=Kr, op=ALU.mult)
                yield
                OP('dve', 'tensor_tensor', [Rf[5], Rf[4]], [Rf[4]], out=f[4], in0=f[5], in1=f[4], op=ALU.mult)
                yield

            def prepB(j):
                Rr, Vr = pc(P_PRW + j), pc(P_PRW + 8 + j)
                RRr, RVr = R_pc[P_PRW + j], R_pc[P_PRW + 8 + j]
                OP('dve', 'scalar_tensor_tensor', [Rf[5], Rf[2]], [R_B[5]], out=arT[:, :, 0, :], in0=v4(f[5]), scalar=-1.0, in1=v4(f[2]),
                   op0=ALU.mult, op1=ALU.mult)
                OP('act', 'activation', [RVr], [R_B[4]], out=vbf, in_=Vr, func=AF.Copy)
                OP('dve', 'tensor_tensor', [Rf[4], Rf[0]], [R_B[3]], out=bhT, in0=f[4], in1=f[0], op=ALU.mult)
                OP('dve', 'tensor_tensor', [Rf[6], Rf[0]], [R_B[3]], out=khT, in0=f[6], in1=f[0], op=ALU.mult)
                OP('dve', 'tensor_tensor', [RRr, Rf[3]], [R_B[5]], out=arT[:, :, 1, :], in0=v4(Rr), in1=v4(f[3]), op=ALU.mult)
                OP('dve', 'tensor_tensor', [Rf[4], Rf[1]], [R_B[2]], out=btT, in0=f[4], in1=f[1], op=ALU.mult)
                OP('dve', 'tensor_tensor', [Rf[6], Rf[1]], [R_B[2]], out=ktT, in0=f[6], in1=f[1], op=ALU.mult)
                OP('dve', 'scalar_tensor_tensor', [RRr, R_vec, Rf[6]], [R_rk], out=rk, in0=Rr, scalar=vec[:, 38 + j:39 + j], in1=f[6],
                   op0=ALU.mult, op1=ALU.mult)
                for (src, rsrc, dst) in ((None, R_B[5], Atok), (vbf, R_B[4], Vtok), (bhT, R_B[3], Bhtok), (khT, R_B[3], Khtok)):
                    b = nb()
                    pvb = ps[b][:, :].bitcast(BF16)
                    for blk in range(4):
                        s_ap = arF[:, blk * 256:blk * 256 + 128] if src is None else src[:, blk * 128:(blk + 1) * 128]
                        T.op('pe', (lambda e, o=pvb[:, blk * 128:(blk + 1) * 128], i=s_ap: e.transpose(o, i, identb)), [rsrc, R_cb], [R_ps[b]])
                    rdst = R_pc[P_TOK] if (dst is Atok or dst is Vtok) else R_pc[P_TOK + 1]
                    OP('act', 'activation', [R_ps[b]], [rdst], out=dst, in_=pvb[:, 0:512].rearrange('p (b c) -> p b c', c=128), func=AF.Copy)

            auto_pump['w'] = WP
            for _ in prepA(0):
                pass
            for j in range(4):
                Vr = pc(P_PRW + 8 + j)
                RVr = R_pc[P_PRW + 8 + j]
                auto_pump['w'] = WP
                prepB(j)
                gA = prepA(j + 1) if j < 3 else iter(())

                def rpump(w=1.0, nA=2):
                    pump(w)
                    for _ in range(nA):
                        if next(gA, 'done') != 'done':
                            pump(WP)
                auto_pump['w'] = 0.0
                pump(2)
                CK(3)
                bY = 3
                if True:
                    chains = [(hh, blk) for hh in range(2) for blk in range(4)]
                    smv = [SM[:, ci, :] for ci in range(8)]

                    def RS(k, cis=range(8)):
                        return [R_sm[ci][k] for ci in cis]

                    def m_b(lo, hi, n):
                        return m5b[:, lo:hi].rearrange('p (o c) -> p o c', o=1).broadcast_to([128, n, hi - lo])

                    def psv(b, n, w):
                        return ps[b][:, 0:n * w].rearrange('p (c n) -> p c n', n=w)
                    for hh in range(2):
                        pb = 64 * hh
                        c0 = 4 * hh
                        b1 = nb()
                        for blk in range(4):
                            aT_ = arF[pb:pb + 64, blk * 256:blk * 256 + 128]
                            bT_ = btT[pb:pb + 64, blk * 128:(blk + 1) * 128]
                            MM(ps[b1][:, blk * 128:(blk + 1) * 128], aT_, bT_, True, True, [R_B[5], R_B[2]], [R_ps[b1]])
                        OP('dve', 'tensor_tensor', [R_ps[b1], R_c32], RS(10, range(c0, c0 + 4)), out=SM[:, c0:c0 + 4, 0:128], in0=psv(b1, 4, 128), in1=m_b(0, 128, 4), op=ALU.mult)
                        for (srcT, lo, rk_) in ((btT, 128, (11, 0)), (ktT, 384, (1, 12))):
                            for pr in range(2):
                                b2 = nb()
                                for q in range(2):
                                    blk = 2 * pr + q
                                    ar_ = arF[pb:pb + 64, blk * 256:blk * 256 + 256]
                                    xT_ = srcT[pb:pb + 64, blk * 128:(blk + 1) * 128]
                                    MM(ps[b2][:, q * 256:(q + 1) * 256], xT_, ar_, True, True, [R_B[5], R_B[2]], [R_ps[b2]])
                                cis = range(c0 + 2 * pr, c0 + 2 * pr + 2)
                                OP('dve', 'tensor_tensor', [R_ps[b2], R_c32], RS(rk_[0], cis) + RS(rk_[1], cis), out=SM[:, cis[0]:cis[0] + 2, lo:lo + 256], in0=psv(b2, 2, 256),
                                   in1=m_b(lo, lo + 256, 2), op=ALU.mult)
                    OP('dve', 'tensor_tensor', RS(11) + [R_cb], RS(4), out=SM[:, 0:8, 896:1024], in0=SM[:, 0:8, 128:256],
                       in1=identb.rearrange('p (o c) -> p o c', o=1).broadcast_to([128, 8, 128]), op=ALU.add)
                    bZ = nb()
                    for ci, (hh, blk) in enumerate(chains):
                        MM(ps[bZ][:, ci * 64:(ci + 1) * 64], smv[ci][:, 384:512], Vtok[:, blk, hh * 64:(hh + 1) * 64], True, True, [R_sm[ci][1], R_pc[P_TOK]], [R_ps[bZ]])
                    OP('act', 'activation', [R_ps[bZ]], RS(7), out=SM[:, 0:8, 1152:1216], in_=psv(bZ, 8, 64), func=AF.Copy)
                    rpump()
                    CK(3.1)
                    for lvl in range(5):
                        co, rcur_k = (640, (2,)) if lvl % 2 == 1 else (0, (10, 11))
                        no, rnxt_k = (640, (2,)) if (lvl + 1) % 2 == 1 else (0, (10, 11))
                        for p4 in range(4):
                            cis = (2 * p4, 2 * p4 + 1)
                            bC = nb()
                            for q, ci in enumerate(cis):
                                cur = smv[ci][:, co:co + 256]
                                rcur = [R_sm[ci][k_] for k_ in rcur_k]
                                MM(ps[bC][:, q * 256:q * 256 + 128], cur[:, 128:256], cur[:, 0:128], True, True, rcur, [R_ps[bC]])
                                MM(ps[bC][:, q * 256 + 128:q * 256 + 256], cur[:, 0:128], cur[:, 128:256], True, True, rcur, [R_ps[bC]])
                            wr = [R_sm[ci][k_] for ci in cis for k_ in rnxt_k]
                            if p4 % 2 == 0:
                                OP('act', 'activation', [R_ps[bC]], wr, out=SM[:, cis[0]:cis[0] + 2, no:no + 256], in_=psv(bC, 2, 256), func=AF.Copy)
                            else:
                                OP('dve', 'tensor_copy', [R_ps[bC]], wr, out=SM[:, cis[0]:cis[0] + 2, no:no + 256], in_=psv(bC, 2, 256))
                        tc_o, tck = (896, 4) if lvl % 2 == 0 else (1024, 5)
                        tn_o, tnk = (1024, 5) if lvl % 2 == 0 else (896, 4)
                        for p2 in range(2):
                            cis = range(4 * p2, 4 * p2 + 4)
                            bD = nb()
                            for q, ci in enumerate(cis):
                                MM(ps[bD][:, q * 128:(q + 1) * 128], smv[ci][:, no:no + 128], smv[ci][:, tc_o:tc_o + 128], True, True,
                                   [R_sm[ci][k_] for k_ in rnxt_k] + [R_sm[ci][tck]], [R_ps[bD]])
                            OP('dve', 'tensor_tensor', [R_ps[bD]] + RS(tck, cis), RS(tnk, cis), out=SM[:, cis[0]:cis[0] + 4, tn_o:tn_o + 128], in0=psv(bD, 4, 128),
                               in1=SM[:, cis[0]:cis[0] + 4, tc_o:tc_o + 128], op=ALU.add)
                        rpump()
                    CK(3.2)
                    for p2 in range(2):
                        cis = range(4 * p2, 4 * p2 + 4)
                        bF = nb()
                        for q, ci in enumerate(cis):
                            hh, blk = chains[ci]
                            MM(ps[bF][:, q * 128:q * 128 + 64], smv[ci][:, 1024:1152], Atok[:, blk, hh * 64:(hh + 1) * 64], True, True, [R_sm[ci][5], R_pc[P_TOK]], [R_ps[bF]])
                            MM(ps[bF][:, q * 128 + 64:q * 128 + 128], smv[ci][:, 1024:1152], smv[ci][:, 1152:1216], True, True, [R_sm[ci][5], R_sm[ci][7]], [R_ps[bF]])
                        OP('act', 'activation', [R_ps[bF]], RS(7, cis), out=SM[:, cis[0]:cis[0] + 4, 1152:1280], in_=psv(bF, 4, 128), func=AF.Copy)
                    rpump()
                    CK(3.3)
                    bG = nb()
                    for ci, (hh, blk) in enumerate(chains):
                        pb = 64 * hh
                        MM(ps[bG][pb:pb + 64, blk * 128:(blk + 1) * 128], smv[ci][:, 1152:1216], smv[ci][:, 256:384], True, True, [R_sm[ci][7], R_sm[ci][0]], [R_ps[bG]])
                    for hh in range(2):
                        pb = 64 * hh
                        OP('dve', 'tensor_tensor', [R_ps[bG], R_B[5]], RS(1, range(4 * hh, 4 * hh + 4)), out=SM[pb:pb + 64, 4 * hh:4 * hh + 4, 384:512],
                           in0=ps[bG][pb:pb + 64, 0:512].rearrange('p (c n) -> p c n', n=128), in1=arT[pb:pb + 64, 0:4, 1, :], op=ALU.add)
                    for c in range(2):
                        cs = slice(c * 64, (c + 1) * 64)
                        bH = nb()
                        for ci, (hh, blk) in enumerate(chains):
                            pb = 64 * hh
                            MM(ps[bH][pb:pb + 64, blk * 64:(blk + 1) * 64], smv[ci][cs, 1152:1216], Bhtok[cs, blk, hh * 64:(hh + 1) * 64],
                               True, True, [R_sm[ci][7], R_pc[P_TOK + 1]], [R_ps[bH]])
                        for ci, (hh, blk) in enumerate(chains):
                            pb = 64 * hh
                            OP('dve', 'scalar_tensor_tensor', [R_ps[bH], R_c32, R_wcbj[j]], [R_sm[ci][9]], out=smv[ci][pb:pb + 64, 1280 + c * 64:1280 + (c + 1) * 64],
                               in0=c32[pb:pb + 64, CI + pb:CI + pb + 64], scalar=wcb[pb:pb + 64, j, blk * 2 + c:blk * 2 + c + 1],
                               in1=ps[bH][pb:pb + 64, blk * 64:(blk + 1) * 64], op0=ALU.mult, op1=ALU.add)
                        bN = nb()
                        for ci, (hh, blk) in enumerate(chains):
                            pb = 64 * hh
                            MM(ps[bN][pb:pb + 64, blk * 64:(blk + 1) * 64], Bhtok[cs, blk, hh * 64:(hh + 1) * 64], smv[ci][cs, 1216:1280], True, False,
                               [R_pc[P_TOK + 1], R_sm[ci][7]], [R_ps[bN]])
                            MM(ps[bN][pb:pb + 64, blk * 64:(blk + 1) * 64], Khtok[cs, blk, hh * 64:(hh + 1) * 64], Vtok[cs, blk, hh * 64:(hh + 1) * 64], False, True,
                               [R_pc[P_TOK + 1], R_pc[P_TOK]], [R_ps[bN]])
                        for hh in range(2):
                            pb = 64 * hh
                            OP('act', 'activation', [R_ps[bN]], RS(2, range(4 * hh, 4 * hh + 4)), out=SM[pb:pb + 64, 4 * hh:4 * hh + 4, 640 + c * 64:640 + (c + 1) * 64],
                               in_=ps[bN][pb:pb + 64, 0:256].rearrange('p (c n) -> p c n', n=64), func=AF.Copy)
                    rpump()
                    CK(3.4)
                    for ci, (hh, blk), c in [(hh_ * 4 + blk_, (hh_, blk_), c_) for blk_ in range(4) for c_ in range(2) for hh_ in range(2)]:
                        pb = 64 * hh
                        s = smv[ci]
                        if True:
                            par = st_par[j][hh]
                            st_cur, r_cur = ST[pb:pb + 64, par, j, :], R_ST[par][j][hh]
                            st_new, r_new = ST[pb:pb + 64, 1 - par, j, :], R_ST[1 - par][j][hh]
                            st_par[j][hh] = 1 - par
                            bS = nb()
                            so = ps[bS][pb:pb + 64, 0:64]
                            MM(so, s[pb:pb + 64, 1280 + c * 64:1280 + (c + 1) * 64], st_cur, True, False, [R_sm[ci][9], r_cur], [R_ps[bS]])
                            MM(so, identb[pb:pb + 64, pb:pb + 64], s[pb:pb + 64, 640 + c * 64:640 + (c + 1) * 64], False, True, [R_cb, R_sm[ci][2]], [R_ps[bS]])
                            OP('act', 'activation', [R_ps[bS]], [r_new], out=st_new, in_=so, func=AF.Copy)
                            yo = ps[bY][pb:pb + 64, blk * 128 + c * 64:blk * 128 + (c + 1) * 64]
                            MM(yo, st_cur, s[pb:pb + 64, 384 + c * 64:384 + (c + 1) * 64], True, False, [r_cur, R_sm[ci][1]], [R_ps[bY]])
                            MM(yo, s[:, 1216:1280], s[:, 256 + c * 64:256 + (c + 1) * 64], False, False, [R_sm[ci][7], R_sm[ci][0]], [R_ps[bY]])
                            MM(yo, Vtok[:, blk, hh * 64:(hh + 1) * 64], s[:, 512 + c * 64:512 + (c + 1) * 64], False, True, [R_pc[P_TOK], R_sm[ci][12]], [R_ps[bY]])
                        if (blk * 2 + c) % 2 == 1 and hh == 1:
                            rpump(0.5, 1)

                    rpump()
                while next(gA, 'done') != 'done':
                    pump(WP)
                CK(4)
                auto_pump['w'] = 1.6 * WP
                ysb, Ry = pc(P_PRW + j), R_pc[P_PRW + j]
                tA, RtA = pc(P_PRW + 4 + j), R_pc[P_PRW + 4 + j]
                OP('act', 'activation', [R_ps[bY]], [Ry], out=ysb, in_=ps[bY][:, :], func=AF.Copy)
                ybf_ = pcb(P_B + 4)[:, 512:1024]
                OP('dve', 'tensor_copy', [Ry], [R_B[4]], out=ybf_, in_=ysb)
                OP('act', 'activation', [Ry], [R_kk2], out=kk2, in_=ysb, func=AF.Square)
                bm, bv_ = nb(), nb()
                MM(ps[bm][:, :], blk64, ybf_, True, True, [R_cb, R_B[4]], [R_ps[bm]])
                MM(ps[bv_][:, :], blk64, kk2, True, True, [R_cb, R_kk2], [R_ps[bv_]])
                bb_ = nb()
                MM(ps[bb_][:, :], blk1, rk, True, True, [R_cb, R_rk], [R_ps[bb_]])
                OP('act', 'activation', [R_ps[bm]], [RtA], out=tA, in_=ps[bm][:, :], func=AF.Square)
                OP('dve', 'tensor_tensor', [R_ps[bv_], RtA], [RtA], out=tA, in0=ps[bv_][:, :], in1=tA, op=ALU.subtract)
                OP('act', 'activation', [RtA, R_dv], [RtA], out=tA, in_=tA, func=AF.Ln, bias=eps_gn, scale=1.0)
                OP('dve', 'tensor_tensor', [Ry, R_ps[bm]], [Ry], out=ysb, in0=ysb, in1=ps[bm][:, :], op=ALU.subtract)
                OP('act', 'activation', [RtA], [RtA], out=tA, in_=tA, func=AF.Exp, scale=-0.5)
                OP('dve', 'tensor_tensor', [R_ps[bb_], RVr], [RVr], out=Vr, in0=ps[bb_][:, :], in1=Vr, op=ALU.mult)
                OP('dve', 'scalar_tensor_tensor', [Ry, RtA, R_vec], [Ry], out=ysb, in0=ysb, scalar=vec[:, 42 + j:43 + j], in1=tA, op0=ALU.mult, op1=ALU.mult)
                OP('dve', 'scalar_tensor_tensor', [Ry, RVr, R_vec], [Ry], out=ysb, in0=ysb, scalar=vec[:, 46 + j:47 + j], in1=Vr, op0=ALU.add, op1=ALU.add)
                bg = nb()
                MM(ps[bg][:, :], gupb[:, j * 128:(j + 1) * 128], sgd, True, True, [R_gup, R_pc[P_B]], [R_ps[bg]])
                OP('dve', 'tensor_tensor', [R_ps[bg], Ry], [R_abf[j]], out=abf[:, j, :], in0=ps[bg][:, :], in1=ysb, op=ALU.mult)

            CK(5)
            auto_pump['w'] = 0.0
            auto_pump['fn'] = None
            for _ in ag:
                pass
            bank_pool[:] = list(range(7))

            CK(6)
            def proj8(seq_base, src_reads_k, rhs_k, evac):
                for c in range(8):
                    slot = use_panel(seq_base + c // 4)
                    pos = c % 4
                    b = nb()
                    for k in range(8):
                        MM(ps[b][:, :], ring[:, slot, k * 512 + pos * 128:k * 512 + pos * 128 + 128], rhs_k(k), k == 0, k == 7,
                           [R_ring[slot]] + src_reads_k(k), [R_ps[b]])
                    evac(c, b)

            h = [pc(i) for i in range(8)]
            Rh = [R_pc[i] for i in range(8)]
            for c in range(8):
                DMA('sp', f'xh{c}', h[c], xT[c * 128:(c + 1) * 128, tsl], [], [Rh[c]])

            def ev_res(c, b):
                OP('dve', 'tensor_tensor', [R_ps[b], Rh[c]], [Rh[c]], out=h[c], in0=ps[b][:, :], in1=h[c], op=ALU.add)
            proj8(seq0 + 7, lambda k: [R_abf[k]], lambda k: abf[:, k, :], ev_res)

            CK(7)
            P_SQ2, P_RS2 = 30, 31

            def norm_cast(gcol):
                rs = rms_stats(lambda k: (h[k], [Rh[k]]), None, P_SQ2, P_RS2, nb())
                for k in range(8):
                    if k % 2 == 0:
                        OP('dve', 'tensor_scalar', [Rh[k], R_vec], [R_abf[k]], out=abf[:, k, :], in0=h[k],
                           scalar1=vec[:, gcol + k:gcol + k + 1], scalar2=None, op0=ALU.mult)
                    else:
                        OP('act', 'activation', [Rh[k], R_vec], [R_abf[k]], out=abf[:, k, :], in_=h[k], func=AF.Copy, scale=vec[:, gcol + k:gcol + k + 1])
                return rs
            rs2 = norm_cast(51)
            qm = pcb(24, 4).rearrange('p (c t) -> p c t', t=512)
            Rqm = Rp(24, 4)

            def ev_qm(c, b):
                OP('dve', 'scalar_tensor_tensor', [R_ps[b], R_pc[P_RS2]], Rqm, out=qm[:, c, :], in0=ps[b][:, :], scalar=1.0 / 16, in1=rs2,
                   op0=ALU.mult, op1=ALU.mult)
            proj8(seq0 + 9, lambda k: [R_abf[k]], lambda k: abf[:, k, :], ev_qm)

            CK(7.3)
            pm = pcb(28, 2).rearrange('p (m t) -> p m t', t=512)
            for hd in range(4):
                for mb in range(2):
                    b = nb()
                    for dc in range(2):
                        MM(ps[b][:, :], KmT[:, hd * 2 + dc, mb * 128:(mb + 1) * 128], qm[:, hd * 2 + dc, :], dc == 0, dc == 1, [R_KmT] + Rqm, [R_ps[b]])
                    OP('act', 'activation', [R_ps[b]], [R_pc[28 + mb // 2]], out=pm[:, mb, :], in_=ps[b][:, :], func=AF.Exp)
                bl = nb()
                for mb in range(2):
                    MM(ps[bl][:, :], ones1, pm[:, mb, :], mb == 0, mb == 1, [R_cb, R_pc[28]], [R_ps[bl]])
                rlm = pc(29)
                OP('act', 'activation', [R_ps[bl]], [R_pc[29]], out=rlm, in_=ps[bl][:, :], func=AF.Ln)
                OP('act', 'activation', [R_pc[29]], [R_pc[29]], out=rlm, in_=rlm, func=AF.Exp, scale=-1.0)
                for dc in range(2):
                    b = nb()
                    for mb in range(2):
                        MM(ps[b][:, :], Vm[:, mb, (hd * 2 + dc) * 128:(hd * 2 + dc + 1) * 128], pm[:, mb, :], mb == 0, mb == 1, [R_Vm, R_pc[28]], [R_ps[b]])
                    OP('dve', 'tensor_tensor', [R_ps[b], R_pc[29]], [R_abf[hd * 2 + dc]], out=abf[:, hd * 2 + dc, :], in0=ps[b][:, :], in1=rlm, op=ALU.mult)

            CK(7.6)
            proj8(seq0 + 11, lambda k: [R_abf[k]], lambda k: abf[:, k, :], ev_res)

            CK(8)
            rs3 = norm_cast(67)
            ubf = pcb(8, 16).rearrange('p (f t) -> p f t', t=512)
            for fo in range(32):
                slot = use_panel(seq0 + 13 + fo // 4)
                pos = fo % 4
                b = nb()
                for k in range(8):
                    MM(ps[b][:, :], ring[:, slot, k * 512 + pos * 128:k * 512 + pos * 128 + 128], abf[:, k, :], k == 0, k == 7,
                       [R_ring[slot], R_abf[k]], [R_ps[b]])
                tr = pc(28 + fo % 2)
                OP('dve', 'scalar_tensor_tensor', [R_ps[b], R_pc[P_RS2]], [R_pc[28 + fo % 2]], out=tr, in0=ps[b][:, :], scalar=0.0, in1=rs3,
                   op0=ALU.max, op1=ALU.mult)
                OP('act', 'activation', [R_pc[28 + fo % 2]], [R_pc[8 + fo // 2]], out=ubf[:, fo, :], in_=tr, func=AF.Square)
            if tau + 1 < ntiles:
                load_x_tile(tau + 1)
            CK(8.5)
            for c in range(8):
                slot = use_panel(seq0 + 21 + c)
                b = nb()
                for fo in range(32):
                    MM(ps[b][:, :], ring[:, slot, fo * 128:(fo + 1) * 128], ubf[:, fo, :], fo == 0, fo == 31, [R_ring[slot], R_pc[8 + fo // 2]], [R_ps[b]])
                ev_res(c, b)
            CK(9)
            rs4 = rms_stats(lambda k: (h[k], [Rh[k]]), None, P_SQ2, P_RS2, nb())
            for c in range(8):
                OP('dve', 'scalar_tensor_tensor', [Rh[c], R_vec, R_pc[P_RS2]], [Rh[c]], out=h[c], in0=h[c],
                   scalar=vec[:, 75 + c:76 + c], in1=rs4, op0=ALU.mult, op1=ALU.mult)
                DMA('sp', f'st{c}', outT[c * 128:(c + 1) * 128, tsl], h[c], [Rh[c]], [R_out])

        try:
            CK(0)
            load_x_tile(0)
            for tau in range(ntiles):
                tile_body(tau)
        except _Stop:
            pass

        for key in [f'st{c}' for c in range(8)]:
            if key in T.dma_cnt:
                T.streams['sp'].append(('wait', 'dma:' + key, T.dma_cnt[key] * 16))
        block = es.enter_context(nc.Block())
        T.emit(nc, block, es)
    return nc


def _panels(W, pw):
    K, N = W.shape
    kc = K // 128
    out = []
    for c0 in range(0, N, pw):
        blk = W[:, c0:c0 + pw].reshape(kc, 128, pw).transpose(1, 0, 2).reshape(128, kc * pw)
        out.append(blk)
    return out


def _prep_shared(inp):
    f = lambda k: np.asarray(inp[k], dtype=np.float32)
    w_in = f('w_in')[0]
    zpad = np.zeros((1024, 256), np.float32)
    w_in_perm = np.concatenate([w_in[:, :2560], w_in[:, 2560:2816], zpad, w_in[:, 2816:3328]], axis=1)
    pans = _panels(w_in_perm, 512)
    for k in ('w_out', 'w_mq', 'w_mo'):
        pans += _panels(f(k)[0], 512)
    pans += _panels(f('w_up')[0], 512)
    pans += _panels(f('w_down')[0], 128)
    wpan = np.ascontiguousarray(np.stack(pans, 0))
    assert wpan.shape == (NPAN, 128, 4096)
    wmkv = np.stack([_panels(f('w_mk')[0], 1024)[0], _panels(f('w_mv')[0], 1024)[0]], 0)
    wlora = np.zeros((128, 1024), np.float32)
    wlora[0:64, 0:512] = f('w_decay_up')[0]
    wlora[64:128, 0:512] = f('a_up')[0]
    wlora[:, 512:1024] = f('g_up')[0]
    vec = np.zeros((128, NV), np.float32)
    col = lambda v: np.asarray(v, np.float32).reshape(-1, 128).T
    vec[:, 0:8] = col(f('norm_mix_w')[0])
    vec[:, 8:22] = col(f('mu_shift')[0])
    vec[:, 22:26] = col(f('w_decay0')[0])
    vec[:, 26:30] = col(f('a0')[0])
    vec[:, 30:34] = col(f('k_k')[0])
    vec[:, 34:38] = col(f('k_a')[0])
    vec[:, 38:42] = col(f('r_k')[0].reshape(-1))
    vec[:, 42:46] = col(f('lnx_w')[0])
    vec[:, 46:50] = col(f('lnx_b')[0])
    vec[:, 50] = f('subln_w')[0]
    vec[:, 51:59] = col(f('norm_mem_w')[0])
    vec[:, 59:67] = col(f('norm_src_w')[0])
    vec[:, 67:75] = col(f('norm_mlp_w')[0])
    vec[:, 75:83] = col(f('norm_final_w'))
    for i, k in enumerate(('lam_q1', 'lam_k1', 'lam_q2', 'lam_k2')):
        vec[0:64, 83 + i] = f(k)[0]
    cst = np.zeros((128, NCST), np.float32)
    idx = np.arange(128)
    same = (idx[:, None] // 64) == (idx[None, :] // 64)
    mLT = ((idx[:, None] < idx[None, :]) & same).astype(np.float32)
    mInc = ((idx[:, None] <= idx[None, :]) & same).astype(np.float32)
    cst[:, CI:CI + 128] = np.eye(128, dtype=np.float32)
    cst[:, CM5:CM5 + 640] = np.concatenate([mLT.T, mLT, mInc, mLT, mInc], axis=1)
    rs = np.ones(512, np.float32)
    rs[::64] = 0.0
    cst[:, CRS:CRS + 512] = rs[None, :]
    cst[:, CBO:CBO + 128] = same.astype(np.float32)
    return dict(wpan=wpan, wmkv=np.ascontiguousarray(wmkv), wlora=wlora, vecs=vec, cst=cst)


_NC_CACHE = {}
MARKS = []


def kernel(**inputs):
    x = np.asarray(inputs['x'], dtype=np.float32)
    mem = np.asarray(inputs['mem'], dtype=np.float32)
    shared = _prep_shared(inputs)
    B = x.shape[0]
    in_maps = []
    for b in range(B):
        m = dict(shared)
        m['xT'] = np.ascontiguousarray(x[b].T)
        m['memT'] = np.ascontiguousarray(mem[b].T)
        in_maps.append(m)
    if 'nc' not in _NC_CACHE:
        _NC_CACHE['nc'] = build_nc()
    nc = _NC_CACHE['nc']
    res = run_bass_kernel_spmd(nc, in_maps, core_ids=list(range(B)))
    out = np.stack([np.ascontiguousarray(res.results[b]['outT'].T) for b in range(B)], 0)
    return out.astype(np.float32)
```

```python
import os
import numpy as np
from contextlib import ExitStack
import concourse.bass as bass
import concourse.mybir as mybir
from concourse.bass_utils import run_bass_kernel_spmd

F32 = mybir.dt.float32
BF16 = mybir.dt.bfloat16
AF = mybir.ActivationFunctionType
ALU = mybir.AluOpType
ENGS = ('pe', 'act', 'dve', 'pool', 'sp')

S = 4096
D = 1024
TT_ = 512
NT = S // TT_
NV = 87
C0 = float(np.exp(-0.5))
CI, CM5, CRS, CBO, NCST = 0, 128, 768, 1280, 1408
NPAN = 29
WIN_POS = [(c // 4, c % 4) for c in range(16)] + [(4, 0), (4, 1), (4, 2), (4, 3), (5, 0), (5, 1),
                                                    (6, 0), (6, 1), (6, 2), (6, 3)]


class Res:
    __slots__ = ('name', 'w', 'rs')

    def __init__(self, name=''):
        self.name = name
        self.w = None
        self.rs = {}


class Trk:
    def __init__(self):
        self.streams = {e: [] for e in ENGS}
        self.cnt = {e: 0 for e in ENGS}
        self.known = {e: {} for e in ENGS}
        self.dma_cnt = {}

    def _deps(self, eng, reads, writes):
        deps = {}
        for r in reads:
            if r.w is not None and deps.get(r.w[0], 0) < r.w[1]:
                deps[r.w[0]] = r.w[1]
        for r in writes:
            if r.w is not None and deps.get(r.w[0], 0) < r.w[1]:
                deps[r.w[0]] = r.w[1]
            for k, i in r.rs.items():
                if deps.get(k, 0) < i:
                    deps[k] = i
        kn = self.known[eng]
        for k, i in deps.items():
            if k == eng and eng in ('pe', 'sp'):
                continue
            if kn.get(k, 0) >= i:
                continue
            kn[k] = i
            self.streams[eng].append(('wait', k, i))

    def op(self, eng, fn, reads=(), writes=()):
        self._deps(eng, reads, writes)
        self.cnt[eng] += 1
        i = self.cnt[eng]
        self.streams[eng].append(('op', fn))
        for r in reads:
            if r.rs.get(eng, 0) < i:
                r.rs[eng] = i
        for r in writes:
            r.w = (eng, i)
            r.rs = {}

    def dma(self, eng, semkey, fn, reads=(), writes=()):
        self._deps(eng, reads, writes)
        self.dma_cnt[semkey] = self.dma_cnt.get(semkey, 0) + 1
        k = 'dma:' + semkey
        i = self.dma_cnt[semkey] * 16
        self.streams[eng].append(('dma', fn, semkey))
        for r in reads:
            if r.rs.get(k, 0) < i:
                r.rs[k] = i
        for r in writes:
            r.w = (k, i)
            r.rs = {}

    def emit(self, nc, block, es):
        sems = {}
        for e in ENGS:
            sems[e] = es.enter_context(nc.semaphore('sem_' + e))
        for k in self.dma_cnt:
            sems['dma:' + k] = es.enter_context(nc.semaphore('dsem_' + k))
        streams = self.streams

        def run(engname, eng):
            own = sems[engname]
            for ent in streams[engname]:
                if ent[0] == 'wait':
                    eng.wait_ge(sems[ent[1]], ent[2])
                elif ent[0] == 'op':
                    ent[1](eng).then_inc(own, 1)
                else:
                    ent[1](eng).then_inc(sems['dma:' + ent[2]], 16)

        block.tensor(lambda eng: run('pe', eng))
        block.scalar(lambda eng: run('act', eng))
        block.vector(lambda eng: run('dve', eng))
        block.gpsimd(lambda eng: run('pool', eng))
        block.sync(lambda eng: run('sp', eng))


def build_nc(ntiles=NT, dbg=None, stage_limit=99):
    nc = bass.Bass("TRN2", target_bir_lowering=False)
    xT = nc.dram_tensor("xT", [D, S], F32, kind="ExternalInput").ap()
    memT = nc.dram_tensor("memT", [D, 256], F32, kind="ExternalInput").ap()
    wpan = nc.dram_tensor("wpan", [NPAN, 128, 4096], F32, kind="ExternalInput").ap()
    wmkv = nc.dram_tensor("wmkv", [2, 128, 8192], F32, kind="ExternalInput").ap()
    wlora = nc.dram_tensor("wlora", [128, 1024], F32, kind="ExternalInput").ap()
    vecs = nc.dram_tensor("vecs", [128, NV], F32, kind="ExternalInput").ap()
    cst = nc.dram_tensor("cst", [128, NCST], F32, kind="ExternalInput").ap()
    outT = nc.dram_tensor("outT", [D, S], F32, kind="ExternalOutput").ap()
    wbf = nc.dram_tensor("wbf", [NPAN, 128, 4096], BF16).ap()
    dbg_out = None
    if dbg:
        dbg_out = nc.dram_tensor("dbg", [128, 16, 512], F32, kind="ExternalOutput").ap()

    T = Trk()
    es = ExitStack()
    with es:
        def sb(n, s, d):
            return es.enter_context(nc.sbuf_tensor(n, s, d))

        vec = sb('vec', [128, NV], F32)
        dv = sb('dvec', [128, 32], F32)
        c32 = sb('c32', [128, 128], F32)
        m5b = sb('m5b', [128, 640], BF16)
        rsb = sb('rsb', [128, 512], BF16)
        cb = sb('cb', [128, 6, 128], BF16)
        Kc = sb('Kc', [128, 4, S], BF16)
        Vc = sb('Vc', [128, S // 128, 512], BF16)
        KmT = sb('KmT', [128, 8, 256], BF16)
        Vm = sb('Vm', [128, 2, 1024], BF16)
        lorab = sb('lorab', [128, 512], BF16)
        gupb = sb('gupb', [128, 512], BF16)
        ring = sb('ring', [128, 3, 4096], BF16)
        abf = sb('abf', [128, 8, 512], BF16)
        NPIECE = 36
        AR = sb('AR', [128, NPIECE * 512], F32)
        SM = sb('SM', [128, 8, 1408], BF16)
        carry = sb('carry', [128, 16], F32)
        ST = sb('ST', [128, 2, 4, 64], BF16)
        wcb = sb('wcb', [128, 4, 8], F32)
        rtok = sb('rtok', [128, 4], F32)
        ones32 = sb('ones32', [128, 128], F32)
        ps = [es.enter_context(nc.psum_tensor(f'ps{i}', [128, 512], F32)) for i in range(8)]

        R_vec, R_dv, R_c32, R_cb = Res('vec'), Res('dv'), Res('c32'), Res('cb')
        R_Kc = [Res(f'Kc{t}') for t in range(NT)]
        R_Vc = [Res(f'Vc{t}') for t in range(NT)]
        R_KmT, R_Vm, R_lora, R_gup = Res('KmT'), Res('Vm'), Res('lora'), Res('gup')
        R_ring = [Res(f'ring{i}') for i in range(3)]
        R_abf = [Res(f'abf{i}') for i in range(8)]
        R_pc = [Res(f'pc{i}') for i in range(NPIECE)]
        R_carry, R_wcb, R_rtok = Res('carry'), Res('wcb'), Res('rtok')
        R_ST = [[[Res(f'ST{p}_{j}_{h}') for h in range(2)] for j in range(4)] for p in range(2)]
        st_par = [[0, 0] for _ in range(4)]
        R_ps = [Res(f'ps{i}') for i in range(8)]
        R_wbf = [Res(f'wbf{i}') for i in range(NPAN)]
        R_out = Res('out')

        auto_pump = {'w': 0.0, 'fn': None, 'busy': False}

        def _ap():
            if auto_pump['w'] > 0 and auto_pump['fn'] is not None and not auto_pump['busy']:
                auto_pump['busy'] = True
                auto_pump['fn'](auto_pump['w'])
                auto_pump['busy'] = False

        def OP(eng, meth, reads, writes, *a, **k):
            T.op(eng, (lambda e: getattr(e, meth)(*a, **k)), reads, writes)
            _ap()

        def MM(out, lhsT, rhs, start, stop, reads, writes):
            T.op('pe', (lambda e: e.matmul(out, lhsT=lhsT, rhs=rhs, start=start, stop=stop)), reads, writes)
            if stop:
                _ap()

        def DMA(eng, key, out, in_, reads, writes):
            T.dma(eng, key, (lambda e: e.dma_start(out=out, in_=in_)), reads, writes)

        def pc(i, n=1):
            return AR[:, i * 512:(i + n) * 512]

        def pcb(i, n=1):
            return AR[:, i * 512:(i + n) * 512].bitcast(BF16)

        def Rp(i, n=1):
            return R_pc[i:i + n]

        bank_ctr = [0]

        bank_pool = list(range(7))

        def nb():
            b = bank_pool[bank_ctr[0] % len(bank_pool)]
            bank_ctr[0] += 1
            return b

        identb, blk1, blk64, ones1024, ones128, ones1 = [cb[:, i, :] for i in range(6)]
        ident32 = c32[:, CI:CI + 128]
        eps_rms = dv[:, 20:21]
        eps_gn = dv[:, 21:22]

        DMA('sp', 'ld0', vec[:], vecs, [], [R_vec])
        DMA('sp', 'ld1', c32[:], cst[:, 0:128], [], [R_c32])
        DMA('pool', 'ld5a', m5b[:], cst[:, CM5:CM5 + 640], [], [R_c32])
        DMA('pool', 'ld5b', rsb[:], cst[:, CRS:CRS + 512], [], [R_c32])
        DMA('pool', 'ld5c', cb[:, 1, :], cst[:, CBO:CBO + 128], [], [R_cb])
        SKIP = []

        def cast_panel(pi, deps=()):
            key = f'wc{pi}' if pi < 7 else ('wcB' if pi < 13 else 'wcC')
            DMA('pool', key, wbf[pi], wpan[pi], list(deps), [R_wbf[pi]])

        DMA('pool', 'ld2', lorab[:], wlora[:, 0:512], [], [R_lora])
        DMA('pool', 'ld3', gupb[:], wlora[:, 512:1024], [], [R_gup])
        wkv = [pcb(8, 8), pcb(16, 8)]
        DMA('pool', 'ld4a', wkv[0], wmkv[0], [], Rp(8, 8))
        DMA('pool', 'ld4b', wkv[1], wmkv[1], [], Rp(16, 8))
        for pi in range(13):
            cast_panel(pi)
        OP('dve', 'tensor_copy', [R_c32], [R_cb], out=identb, in_=ident32)
        OP('dve', 'tensor_scalar', [R_cb], [R_cb], out=blk64, in0=blk1, scalar1=1.0 / 64, scalar2=None,
           op0=ALU.mult)
        OP('pool', 'memset', [], [R_cb], ones1024, 1.0 / 1024)
        OP('pool', 'memset', [], [R_cb], ones128, 1.0 / 128)
        OP('pool', 'memset', [], [R_cb], ones1, 1.0)
        OP('pool', 'memset', [], [R_cb], ones32[:], 1.0)
        OP('dve', 'tensor_scalar', [R_vec], [R_dv], out=dv[:, 0:14], in0=vec[:, 8:22], scalar1=-1.0, scalar2=1.0,
           op0=ALU.mult, op1=ALU.add)
        OP('dve', 'tensor_scalar', [R_vec], [R_dv], out=dv[:, 14:18], in0=vec[:, 34:38], scalar1=-1.0, scalar2=1.0,
           op0=ALU.mult, op1=ALU.add)
        OP('dve', 'tensor_scalar', [R_vec], [R_dv], out=dv[:, 18:19], in0=vec[:, 50:51], scalar1=0.8, scalar2=None,
           op0=ALU.mult)
        OP('pool', 'memset', [], [R_dv], dv[:, 20:21], 1e-5)
        OP('pool', 'memset', [], [R_dv], dv[:, 21:22], 64e-5)
        OP('pool', 'memset', [], [R_carry], carry[:], 0.0)
        OP('pool', 'memset', [], [r for a in R_ST for b in a for r in b], ST[:], 0.0)
        OP('dve', 'tensor_tensor', [R_vec], [R_dv], out=dv[:, 22:23], in0=vec[:, 83:84], in1=vec[:, 84:85], op=ALU.mult)
        OP('dve', 'tensor_tensor', [R_vec], [R_dv], out=dv[:, 23:24], in0=vec[:, 85:86], in1=vec[:, 86:87], op=ALU.mult)
        OP('pool', 'memset', [], [R_pc[0]], pc(0)[:, 0:128], 1.0)
        if 'lam' not in SKIP:
          MM(ps[0][:, 0:2], pc(0)[:, 0:128], dv[:, 22:24], True, True, [R_pc[0], R_dv], [R_ps[0]])
        OP('act', 'activation', [R_ps[0]], [R_dv], out=dv[:, 24:26], in_=ps[0][:, 0:2], func=AF.Exp)
        OP('dve', 'tensor_tensor', [R_dv], [R_dv], out=dv[:, 26:27], in0=dv[:, 25:26], in1=dv[:, 24:25], op=ALU.subtract)
        OP('dve', 'tensor_scalar', [R_dv], [R_dv], out=dv[:, 19:20], in0=dv[:, 26:27], scalar1=-0.2, scalar2=None,
           op0=ALU.add)

        if 'mem' in SKIP:
            stage_limit = -1
        m32 = pc(0, 4).rearrange('p (k m) -> p k m', m=256)
        DMA('sp', 'ld2b', m32, memT.rearrange('(k p) m -> p k m', p=128), [], Rp(0, 4))
        msq = pcb(6, 1)
        msq = pcb(6, 2).rearrange('p (k m) -> p k m', m=256)
        memn = pcb(4, 2).rearrange('p (k m) -> p k m', m=256)
        OP('act', 'activation', Rp(0, 4), Rp(6, 2), out=msq, in_=m32, func=AF.Square)
        for k in range(8):
            MM(ps[1][:, 0:256], ones1024, msq[:, k, :], k == 0, k == 7, [R_cb] + Rp(6, 2), [R_ps[1]])
        mr = pc(24)[:, 0:256]
        OP('act', 'activation', [R_ps[1], R_dv], [R_pc[24]], out=mr, in_=ps[1][:, 0:256], func=AF.Ln, bias=eps_rms, scale=1.0)
        OP('act', 'activation', [R_pc[24]], [R_pc[24]], out=mr, in_=mr, func=AF.Exp, scale=-0.5)
        for k in range(8):
            OP('dve', 'scalar_tensor_tensor', Rp(0, 4) + [R_pc[24], R_vec], Rp(4, 2), out=memn[:, k, :], in0=m32[:, k, :],
               scalar=vec[:, 59 + k:60 + k], in1=mr, op0=ALU.mult, op1=ALU.mult)
        for which in range(2):
            wv = wkv[which].rearrange('p (k n) -> p k n', n=1024)
            Rw = Rp(8 + 8 * which, 8)
            if which == 0:
                for c in range(8):
                    b = nb()
                    for k in range(8):
                        MM(ps[b][:, 0:256], wv[:, k, c * 128:(c + 1) * 128], memn[:, k, :], k == 0, k == 7,
                           Rw + Rp(4, 2), [R_ps[b]])
                    OP('act', 'activation', [R_ps[b]], [R_KmT], out=KmT[:, c, :], in_=ps[b][:, 0:256], func=AF.Copy)
            else:
                for mb in range(2):
                    for nh in range(2):
                        b = nb()
                        for k in range(8):
                            MM(ps[b][:, :], memn[:, k, mb * 128:(mb + 1) * 128], wv[:, k, nh * 512:(nh + 1) * 512], k == 0, k == 7,
                               Rw + Rp(4, 2), [R_ps[b]])
                        OP('act', 'activation', [R_ps[b]], [R_Vm], out=Vm[:, mb, nh * 512:(nh + 1) * 512], in_=ps[b][:, :], func=AF.Copy)

        ring_state = {'next': 0}
        panel_slot = {}

        def load_panel(seq):
            if seq >= ntiles * NPAN or seq in panel_slot:
                return
            assert seq == ring_state['next']
            ring_state['next'] += 1
            slot = seq % 3
            panel_slot[seq] = slot
            pi = seq % NPAN
            gl = pi if pi < 7 else (12 if pi < 13 else NPAN - 1)
            DMA('sp', f'ring{slot}', ring[:, slot, :], wbf[pi], [R_wbf[gl]], [R_ring[slot]])

        def use_panel(seq):
            load_panel(seq)
            load_panel(seq + 1)
            load_panel(seq + 2)
            return panel_slot[seq]

        P_PRW = 0
        P_F = 14
        P_B = 21
        P_TOK = 27
        P_YSB = 16
        P_TMP = 29
        P_RSTD = 31

        def rms_stats(src_chunks_fn, nsrc_reads, sq_piece, rstd_piece, bank):
            sqv = pcb(sq_piece).rearrange('p (h t) -> p h t', t=512)
            for k in range(8):
                src, rr = src_chunks_fn(k)
                OP('act', 'activation', rr, [R_pc[sq_piece]], out=sqv[:, k % 2, :], in_=src, func=AF.Square)
                MM(ps[bank][:, :], ones1024, sqv[:, k % 2, :], k == 0, k == 7, [R_cb, R_pc[sq_piece]], [R_ps[bank]])
            rs = pc(rstd_piece)
            OP('act', 'activation', [R_ps[bank], R_dv], [R_pc[rstd_piece]], out=rs, in_=ps[bank][:, :], func=AF.Ln, bias=eps_rms, scale=1.0)
            OP('act', 'activation', [R_pc[rstd_piece]], [R_pc[rstd_piece]], out=rs, in_=rs, func=AF.Exp, scale=-0.5)
            return rs

        R_sm = [[Res(f'sm{ci}_{n}') for n in range(13)] for ci in range(8)]
        R_pt = [Res(f'pt{i}') for i in range(4)]
        R_kk2rk = (Res('kk2'), Res('rk'))
        R_wcbj = [Res(f'wcb{j}') for j in range(4)]

        class _Stop(Exception):
            pass

        def CK(n):
            MARKS.append((n, dict(T.cnt)))
            if stage_limit <= n:
                raise _Stop()

        P_XS, P_XSQ, P_XR = 24, 26, 27

        def load_x_tile(tau):
            tsl = slice(tau * 512, (tau + 1) * 512)
            sqv = pcb(P_XSQ).rearrange('p (h t) -> p h t', t=512)
            bank_ms = nb()
            for k in range(8):
                stg = pc(P_XS + (k % 2))
                DMA('sp', f'xs{k % 2}', stg, xT[k * 128:(k + 1) * 128, tsl], [], [R_pc[P_XS + (k % 2)]])
                OP('act', 'activation', [R_pc[P_XS + (k % 2)]], [R_pc[P_XSQ]], out=sqv[:, k % 2, :], in_=stg, func=AF.Square)
                MM(ps[bank_ms][:, :], ones1024, sqv[:, k % 2, :], k == 0, k == 7, [R_cb, R_pc[P_XSQ]], [R_ps[bank_ms]])
                if k % 2 == 0:
                    OP('dve', 'tensor_scalar', [R_pc[P_XS + (k % 2)], R_vec], [R_abf[k]], out=abf[:, k, :], in0=stg,
                       scalar1=vec[:, k:k + 1], scalar2=None, op0=ALU.mult)
                else:
                    OP('act', 'activation', [R_pc[P_XS + (k % 2)], R_vec], [R_abf[k]], out=abf[:, k, :], in_=stg, func=AF.Copy, scale=vec[:, k:k + 1])
            rstd = pc(P_XR)
            OP('act', 'activation', [R_ps[bank_ms], R_dv], [R_pc[P_XR]], out=rstd, in_=ps[bank_ms][:, :], func=AF.Ln, bias=eps_rms, scale=1.0)
            OP('act', 'activation', [R_pc[P_XR]], [R_pc[P_XR]], out=rstd, in_=rstd, func=AF.Exp, scale=-0.5)
            bt_ = nb()
            for tb in range(4):
                MM(ps[bt_][:, tb:tb + 1], rstd[:, tb * 128:(tb + 1) * 128], ident32[:, 0:1], True, True, [R_pc[P_XR], R_c32], [R_ps[bt_]])
            OP('dve', 'tensor_copy', [R_ps[bt_]], [R_rtok], out=rtok[:, :], in_=ps[bt_][:, 0:4])

        def tile_body(tau):
            tsl = slice(tau * 512, (tau + 1) * 512)
            seq0 = tau * NPAN
            use_panel(seq0)
            rstd = pc(P_XR)
            CK(1)
            qbf = pcb(P_TMP, 2).rearrange('p (c t) -> p c t', t=512)
            R_qbf = Rp(P_TMP, 2)
            for c in range(22):
                pan, pos = WIN_POS[c]
                slot = use_panel(seq0 + pan)
                b = nb()
                for k in range(8):
                    MM(ps[b][:, :], ring[:, slot, k * 512 + pos * 128:k * 512 + pos * 128 + 128], abf[:, k, :], k == 0, k == 7,
                       [R_ring[slot], R_abf[k]], [R_ps[b]])
                if c < 14:
                    p = pc(P_PRW + c)
                    Rc = R_pc[P_PRW + c]
                    OP('dve', 'tensor_tensor', [R_ps[b], R_pc[P_XR]], [Rc], out=p, in0=ps[b][:, :], in1=rstd, op=ALU.mult)
                    tp = P_TMP + (c % 2)
                    tmp = pc(tp)
                    OP('act', 'activation', [Rc, R_vec], [R_pc[tp]], out=tmp[:, 1:512], in_=p[:, 0:511], func=AF.Copy, scale=vec[:, 8 + c:9 + c])
                    OP('act', 'activation', [R_carry, R_vec], [R_pc[tp]], out=tmp[:, 0:1], in_=carry[:, c:c + 1], func=AF.Copy, scale=vec[:, 8 + c:9 + c])
                    OP('act', 'activation', [Rc], [R_carry], out=carry[:, c:c + 1], in_=p[:, 511:512], func=AF.Copy)
                    OP('dve', 'scalar_tensor_tensor', [Rc, R_pc[tp], R_dv], [Rc], out=p, in0=p, scalar=dv[:, c:c + 1], in1=tmp,
                       op0=ALU.mult, op1=ALU.add)
                elif c < 18:
                    OP('dve', 'scalar_tensor_tensor', [R_ps[b], R_pc[P_XR]], R_qbf, out=qbf[:, c - 14, :], in0=ps[b][:, :], scalar=0.125,
                       in1=rstd, op0=ALU.mult, op1=ALU.mult)
                else:
                    OP('dve', 'tensor_tensor', [R_ps[b], R_pc[P_XR]], [R_Kc[tau]], out=Kc[:, c - 18, tsl], in0=ps[b][:, :], in1=rstd, op=ALU.mult)
            slot = use_panel(seq0 + 6)
            for tb in range(4):
                b = nb()
                for k in range(8):
                    MM(ps[b][:, :], abf[:, k, tb * 128:(tb + 1) * 128], ring[:, slot, k * 512:(k + 1) * 512], k == 0, k == 7,
                       [R_ring[slot], R_abf[k]], [R_ps[b]])
                OP('act', 'activation', [R_ps[b], R_rtok], [R_Vc[tau]], out=Vc[:, tau * 4 + tb, :], in_=ps[b][:, :], func=AF.Copy,
                   scale=rtok[:, tb:tb + 1])

            if tau == 0:
                for pi in range(13, NPAN):
                    cast_panel(pi, deps=[R_Kc[0], R_Vc[0]])
            nkb = 4 * tau + 4
            ot1, ot2, rl = pc(34), pc(35), pc(31)
            accL = [pc(32), pc(33)]
            R_ot1, R_ot2, R_rl = R_pc[34], R_pc[35], R_pc[31]
            R_acc = [R_pc[32], R_pc[33]]
            bO = [4, 5]
            bST = [6, 7]
            st_ctr = [0]

            def attn_gen():
                for hd in range(4):
                    units = [(kb, m) for kb in range(nkb) for m in range(2)]
                    LA = 1
                    uinfo = {}

                    def u_qk(i):
                        kb, m = units[i]
                        qoff = max(0, kb - 4 * tau) * 128
                        ktile = kb // 4
                        if PAIR:
                            bs = bST[m]
                            ip = (kb % 2) * 2 + m
                        else:
                            bs = bST[st_ctr[0] % 2]
                            ip = st_ctr[0] % 4
                            st_ctr[0] += 1
                        mp = slice(64 * m, 64 * m + 64)
                        MM(ps[bs][:, qoff:512], Kc[mp, hd, kb * 128:(kb + 1) * 128], qbf[mp, hd, qoff:512], True, True, [R_Kc[ktile]] + R_qbf, [R_ps[bs]])
                        pt = pcb(P_PRW + 12 + (ip // 2))[:, (ip % 2) * 512:(ip % 2) * 512 + 512]
                        Rpt = R_pt[ip]
                        uinfo[i] = (pt, Rpt, qoff, ktile, bs)
                        if not PAIR:
                            u_exp(i)

                    def u_exp(i):
                        kb, m = units[i]
                        pt, Rpt, qoff, ktile, bs = uinfo[i]
                        OP('act', 'activation', [R_ps[bs]], [Rpt], out=pt[:, qoff:512], in_=ps[bs][:, qoff:512], func=AF.Exp)
                        if kb >= 4 * tau:
                            OP('pool', 'memset', [], [Rpt], pt[64:128, qoff:qoff + 64], 0.0)

                    def u_pv(i):
                        kb, m = units[i]
                        pt, Rpt, qoff, ktile, _bs = uinfo.pop(i)
                        MM(ps[bO[m]][:, qoff:512], Vc[:, kb, hd * 128:(hd + 1) * 128], pt[:, qoff:512], kb == 0, kb == nkb - 1, [R_Vc[ktile], Rpt], [R_ps[bO[m]]])
                        if kb == 0:
                            OP(ACC_ENG, 'tensor_copy', [Rpt], [R_acc[m]], out=accL[m], in_=pt)
                        else:
                            OP(ACC_ENG, 'tensor_tensor', [Rpt, R_acc[m]], [R_acc[m]], out=accL[m][:, qoff:512], in0=accL[m][:, qoff:512], in1=pt[:, qoff:512], op=ALU.add)

                    if PAIR:
                        for kb_ in range(nkb + 1):
                            if kb_ < nkb:
                                u_qk(2 * kb_)
                                u_qk(2 * kb_ + 1)
                                u_exp(2 * kb_)
                                u_exp(2 * kb_ + 1)
                            if kb_ >= 1:
                                u_pv(2 * kb_ - 2)
                                u_pv(2 * kb_ - 1)
                            yield
                            yield
                    else:
                        for i in range(len(units) + LA):
                            if i < len(units):
                                u_qk(i)
                            if i >= LA:
                                u_pv(i - LA)
                            yield
                    bl0 = bST[st_ctr[0] % 2]
                    st_ctr[0] += 1
                    MM(ps[bl0][:, :], ones32[:], accL[0], True, True, [R_cb, R_acc[0]], [R_ps[bl0]])
                    OP('act', 'activation', [R_ps[bl0]], [R_rl], out=rl, in_=ps[bl0][:, :], func=AF.Ln)
                    OP('act', 'activation', [R_rl], [R_rl], out=rl, in_=rl, func=AF.Exp, scale=-1.0)
                    OP('dve', 'tensor_tensor', [R_ps[bO[0]], R_rl], [R_ot1], out=ot1, in0=ps[bO[0]][:, :], in1=rl, op=ALU.mult)
                    yield
                    bl1 = bST[st_ctr[0] % 2]
                    st_ctr[0] += 1
                    MM(ps[bl1][:, :], ones32[:], accL[1], True, True, [R_cb, R_acc[1]], [R_ps[bl1]])
                    OP('act', 'activation', [R_ps[bl1]], [R_rl], out=rl, in_=ps[bl1][:, :], func=AF.Ln)
                    OP('act', 'activation', [R_rl], [R_rl], out=rl, in_=rl, func=AF.Exp, scale=-1.0)
                    OP('dve', 'tensor_scalar', [R_rl, R_dv], [R_rl], out=rl, in0=rl, scalar1=dv[:, 19:20], scalar2=None, op0=ALU.mult)
                    OP('dve', 'tensor_tensor', [R_ps[bO[1]], R_rl], [R_ot2], out=ot2, in0=ps[bO[1]][:, :], in1=rl, op=ALU.mult)
                    yield
                    OP('dve', 'tensor_tensor', [R_ot1, R_ot2], [R_ot1], out=ot1, in0=ot1, in1=ot2, op=ALU.add)
                    osq = rl.bitcast(BF16)[:, 0:512]
                    OP('act', 'activation', [R_ot1], [R_rl], out=osq, in_=ot1, func=AF.Square)
                    bq = bST[st_ctr[0] % 2]
                    st_ctr[0] += 1
                    MM(ps[bq][:, :], ones128, osq, True, True, [R_cb, R_rl], [R_ps[bq]])
                    OP('act', 'activation', [R_ps[bq], R_dv], [R_ot2], out=ot2, in_=ps[bq][:, :], func=AF.Ln, bias=eps_rms, scale=1.0)
                    OP('act', 'activation', [R_ot2], [R_ot2], out=ot2, in_=ot2, func=AF.Exp, scale=-0.5)
                    OP('dve', 'scalar_tensor_tensor', [R_ot1, R_ot2, R_dv], [R_abf[4 + hd]], out=abf[:, 4 + hd, :], in0=ot1, scalar=dv[:, 18:19],
                       in1=ot2, op0=ALU.mult, op1=ALU.mult)
                    yield

            ag = attn_gen()
            n_attn_steps = 4 * (2 * nkb + 1 + 3)
            WP = 1.0
            ACC_ENG = 'dve'
            PAIR = True
            BURST = 1.0
            pump_rate = 1.08 * n_attn_steps / (4 * (45 * WP + 14))
            pump_acc = [0.0]

            def pump(w=1.0):
                pump_acc[0] += w * pump_rate
                was = auto_pump['busy']
                auto_pump['busy'] = True
                if pump_acc[0] >= BURST:
                    while pump_acc[0] >= 1.0:
                        pump_acc[0] -= 1.0
                        next(ag, None)
                auto_pump['busy'] = was

            auto_pump['fn'] = pump

            bank_pool[:] = [0, 1, 2]

            CK(2)
            tw = pcb(P_B)[:, 0:512]
            sgd = pcb(P_B)[:, 512:1024]
            wda = pc(P_PRW + 12)
            OP('act', 'activation', [R_pc[P_PRW + 12]], [R_pc[P_B]], out=tw[0:64, :], in_=wda[0:64, :], func=AF.Tanh)
            OP('act', 'activation', [R_pc[P_PRW + 12]], [R_pc[P_B]], out=tw[64:128, :], in_=wda[64:128, :], func=AF.Copy)
            OP('act', 'activation', [R_pc[P_PRW + 13]], [R_pc[P_B]], out=sgd, in_=pc(P_PRW + 13), func=AF.Sigmoid)
            f = [pc(P_F + i) for i in range(7)]
            Rf = [R_pc[P_F + i] for i in range(7)]
            kk2 = pcb(P_B + 1)[:, 0:512]
            rk = pcb(P_B + 1)[:, 512:1024]
            btT = pcb(P_B + 2)[:, 0:512]
            ktT = pcb(P_B + 2)[:, 512:1024]
            bhT = pcb(P_B + 3)[:, 0:512]
            khT = pcb(P_B + 3)[:, 512:1024]
            vbf = pcb(P_B + 4)[:, 0:512]
            arT = pcb(P_B + 5).rearrange('p (b s t) -> p b s t', s=2, t=128)
            arF = pcb(P_B + 5)
            Atok = pcb(P_TOK)[:, 0:512].rearrange('p (b c) -> p b c', c=128)
            Vtok = pcb(P_TOK)[:, 512:1024].rearrange('p (b c) -> p b c', c=128)
            Bhtok = pcb(P_TOK + 1)[:, 0:512].rearrange('p (b c) -> p b c', c=128)
            Khtok = pcb(P_TOK + 1)[:, 512:1024].rearrange('p (b c) -> p b c', c=128)
            R_B = [R_pc[P_B + i] for i in range(6)]
            v4 = lambda ap: ap.rearrange('p (b t) -> p b t', t=128)
            v8 = lambda ap: ap.rearrange('p (c t) -> p c t', t=64)
            R_kk2, R_rk = R_kk2rk
            R_wj = R_wcbj

            def prepA(j):
                Kr = pc(P_PRW + 4 + j)
                RKr = R_pc[P_PRW + 4 + j]
                bd = nb()
                MM(ps[bd][:, :], lorab[0:64, j * 128:(j + 1) * 128], tw[0:64, :], True, True, [R_lora, R_pc[P_B]], [R_ps[bd]])
                ba = nb()
                MM(ps[ba][:, :], lorab[64:128, j * 128:(j + 1) * 128], tw[64:128, :], True, True, [R_lora, R_pc[P_B]], [R_ps[ba]])
                OP('act', 'activation', [R_ps[bd], R_vec], [Rf[0]], out=f[0], in_=ps[bd][:, :], func=AF.Sigmoid, bias=vec[:, 22 + j:23 + j], scale=1.0)
                OP('act', 'activation', [R_ps[ba], R_vec], [Rf[4]], out=f[4], in_=ps[ba][:, :], func=AF.Sigmoid, bias=vec[:, 26 + j:27 + j], scale=1.0)
                yield
                OP('act', 'activation', [RKr, R_vec], [Rf[5]], out=f[5], in_=Kr, func=AF.Copy, scale=vec[:, 30 + j:31 + j])
                OP('dve', 'tensor_tensor_scan', [Rf[0], R_c32], [Rf[1]], out=f[1], data0=rsb[:], data1=f[0], initial=0.0,
                   op0=ALU.mult, op1=ALU.add)
                yield
                OP('act', 'activation', [Rf[5]], [R_kk2], out=kk2, in_=f[5], func=AF.Square)
                OP('dve', 'tensor_tensor', [Rf[0], Rf[1]], [Rf[2]], out=f[2], in0=f[1], in1=f[0], op=ALU.subtract)
                yield
                OP('act', 'activation', [Rf[1]], [Rf[3]], out=f[3], in_=f[1], func=AF.Exp, scale=-C0)
                bq = nb()
                MM(ps[bq][:, :], blk1, kk2, True, True, [R_cb, R_kk2], [R_ps[bq]])
                OP('dve', 'tensor_scalar', [R_ps[bq]], [Rf[6]], out=f[6], in0=ps[bq][:, :], scalar1=1e-24, scalar2=None, op0=ALU.max)
                yield
                OP('act', 'activation', [Rf[2]], [Rf[2]], out=f[2], in_=f[2], func=AF.Exp, scale=-C0)
                OP('act', 'activation', [Rf[1]], [Rf[1]], out=f[1], in_=f[1], func=AF.Exp, scale=C0)
                yield
                OP('act', 'activation', [Rf[6]], [Rf[6]], out=f[6], in_=f[6], func=AF.Ln)
                OP('pool', 'tensor_copy', [Rf[3]], [R_wj[j]], out=wcb[:, j, :], in_=v8(f[3])[:, :, 63])
                yield
                OP('act', 'activation', [Rf[6]], [Rf[6]], out=f[6], in_=f[6], func=AF.Exp, scale=-0.5)
                OP('dve', 'tensor_tensor', [Rf[1], Rf[3]], [Rf[0]], out=v8(f[0]), in0=v8(f[1]), in1=v8(f[3])[:, :, 63:64].broadcast_to([128, 8, 64]),
                   op=ALU.mult)
                yield
                OP('dve', 'tensor_tensor', [Rf[5], Rf[6]], [Rf[5]], out=f[5], in0=f[5], in1=f[6], op=ALU.mult)
                yield
                OP('dve', 'tensor_scalar', [Rf[4], R_vec, R_dv], [Rf[6]], out=f[6], in0=f[4], scalar1=vec[:, 34 + j:35 + j], scalar2=dv[:, 14 + j:15 + j],
                   op0=ALU.mult, op1=ALU.add)
                yield
                OP('dve', 'tensor_tensor', [Rf[6], RKr], [Rf[6]], out=f[6], in0=f[6], in1=Kr, op=ALU.mult)
                yield
                OP('dve', 'tensor_tensor', [Rf[5], Rf[4]], [Rf[4]], out=f[4], in0=f[5], in1=f[4], op=ALU.mult)
                yield

            def prepB(j):
                Rr, Vr = pc(P_PRW + j), pc(P_PRW + 8 + j)
                RRr, RVr = R_pc[P_PRW + j], R_pc[P_PRW + 8 + j]
                OP('dve', 'scalar_tensor_tensor', [Rf[5], Rf[2]], [R_B[5]], out=arT[:, :, 0, :], in0=v4(f[5]), scalar=-1.0, in1=v4(f[2]),
                   op0=ALU.mult, op1=ALU.mult)
                OP('act', 'activation', [RVr], [R_B[4]], out=vbf, in_=Vr, func=AF.Copy)
                OP('dve', 'tensor_tensor', [Rf[4], Rf[0]], [R_B[3]], out=bhT, in0=f[4], in1=f[0], op=ALU.mult)
                OP('dve', 'tensor_tensor', [Rf[6], Rf[0]], [R_B[3]], out=khT, in0=f[6], in1=f[0], op=ALU.mult)
                OP('dve', 'tensor_tensor', [RRr, Rf[3]], [R_B[5]], out=arT[:, :, 1, :], in0=v4(Rr), in1=v4(f[3]), op=ALU.mult)
                OP('dve', 'tensor_tensor', [Rf[4], Rf[1]], [R_B[2]], out=btT, in0=f[4], in1=f[1], op=ALU.mult)
                OP('dve', 'tensor_tensor', [Rf[6], Rf[1]], [R_B[2]], out=ktT, in0=f[6], in1=f[1], op=ALU.mult)
                OP('dve', 'scalar_tensor_tensor', [RRr, R_vec, Rf[6]], [R_rk], out=rk, in0=Rr, scalar=vec[:, 38 + j:39 + j], in1=f[6],
                   op0=ALU.mult, op1=ALU.mult)
                for (src, rsrc, dst) in ((None, R_B[5], Atok), (vbf, R_B[4], Vtok), (bhT, R_B[3], Bhtok), (khT, R_B[3], Khtok)):
                    b = nb()
                    pvb = ps[b][:, :].bitcast(BF16)
                    for blk in range(4):
                        s_ap = arF[:, blk * 256:blk * 256 + 128] if src is None else src[:, blk * 128:(blk + 1) * 128]
                        T.op('pe', (lambda e, o=pvb[:, blk * 128:(blk + 1) * 128], i=s_ap: e.transpose(o, i, identb)), [rsrc, R_cb], [R_ps[b]])
                    rdst = R_pc[P_TOK] if (dst is Atok or dst is Vtok) else R_pc[P_TOK + 1]
                    OP('act', 'activation', [R_ps[b]], [rdst], out=dst, in_=pvb[:, 0:512].rearrange('p (b c) -> p b c', c=128), func=AF.Copy)

            auto_pump['w'] = WP
            for _ in prepA(0):
                pass
            for j in range(4):
                Vr = pc(P_PRW + 8 + j)
                RVr = R_pc[P_PRW + 8 + j]
                auto_pump['w'] = WP
                prepB(j)
                gA = prepA(j + 1) if j < 3 else iter(())

                def rpump(w=1.0, nA=2):
                    pump(w)
                    for _ in range(nA):
                        if next(gA, 'done') != 'done':
                            pump(WP)
                auto_pump['w'] = 0.0
                pump(2)
                CK(3)
                bY = 3
                if True:
                    chains = [(hh, blk) for hh in range(2) for blk in range(4)]
                    smv = [SM[:, ci, :] for ci in range(8)]

                    def RS(k, cis=range(8)):
                        return [R_sm[ci][k] for ci in cis]

                    def m_b(lo, hi, n):
                        return m5b[:, lo:hi].rearrange('p (o c) -> p o c', o=1).broadcast_to([128, n, hi - lo])

                    def psv(b, n, w):
                        return ps[b][:, 0:n * w].rearrange('p (c n) -> p c n', n=w)
                    for hh in range(2):
                        pb = 64 * hh
                        c0 = 4 * hh
                        b1 = nb()
                        for blk in range(4):
                            aT_ = arF[pb:pb + 64, blk * 256:blk * 256 + 128]
                            bT_ = btT[pb:pb + 64, blk * 128:(blk + 1) * 128]
                            MM(ps[b1][:, blk * 128:(blk + 1) * 128], aT_, bT_, True, True, [R_B[5], R_B[2]], [R_ps[b1]])
                        OP('dve', 'tensor_tensor', [R_ps[b1], R_c32], RS(10, range(c0, c0 + 4)), out=SM[:, c0:c0 + 4, 0:128], in0=psv(b1, 4, 128), in1=m_b(0, 128, 4), op=ALU.mult)
                        for (srcT, lo, rk_) in ((btT, 128, (11, 0)), (ktT, 384, (1, 12))):
                            for pr in range(2):
                                b2 = nb()
                                for q in range(2):
                                    blk = 2 * pr + q
                                    ar_ = arF[pb:pb + 64, blk * 256:blk * 256 + 256]
                                    xT_ = srcT[pb:pb + 64, blk * 128:(blk + 1) * 128]
                                    MM(ps[b2][:, q * 256:(q + 1) * 256], xT_, ar_, True, True, [R_B[5], R_B[2]], [R_ps[b2]])
                                cis = range(c0 + 2 * pr, c0 + 2 * pr + 2)
                                OP('dve', 'tensor_tensor', [R_ps[b2], R_c32], RS(rk_[0], cis) + RS(rk_[1], cis), out=SM[:, cis[0]:cis[0] + 2, lo:lo + 256], in0=psv(b2, 2, 256),
                                   in1=m_b(lo, lo + 256, 2), op=ALU.mult)
                    OP('pool', 'tensor_tensor', RS(11) + [R_cb], RS(4), out=SM[:, 0:8, 896:1024], in0=SM[:, 0:8, 128:256],
                       in1=identb.rearrange('p (o c) -> p o c', o=1).broadcast_to([128, 8, 128]), op=ALU.add)
                    bZ = nb()
                    for ci, (hh, blk) in enumerate(chains):
                        MM(ps[bZ][:, ci * 64:(ci + 1) * 64], smv[ci][:, 384:512], Vtok[:, blk, hh * 64:(hh + 1) * 64], True, True, [R_sm[ci][1], R_pc[P_TOK]], [R_ps[bZ]])
                    OP('act', 'activation', [R_ps[bZ]], RS(7), out=SM[:, 0:8, 1152:1216], in_=psv(bZ, 8, 64), func=AF.Copy)
                    rpump()
                    CK(3.1)
                    for lvl in range(5):
                        co, rcur_k = (640, (2,)) if lvl % 2 == 1 else (0, (10, 11))
                        no, rnxt_k = (640, (2,)) if (lvl + 1) % 2 == 1 else (0, (10, 11))
                        for p4 in range(4):
                            cis = (2 * p4, 2 * p4 + 1)
                            bC = nb()
                            for q, ci in enumerate(cis):
                                cur = smv[ci][:, co:co + 256]
                                rcur = [R_sm[ci][k_] for k_ in rcur_k]
                                MM(ps[bC][:, q * 256:q * 256 + 128], cur[:, 128:256], cur[:, 0:128], True, True, rcur, [R_ps[bC]])
                                MM(ps[bC][:, q * 256 + 128:q * 256 + 256], cur[:, 0:128], cur[:, 128:256], True, True, rcur, [R_ps[bC]])
                            wr = [R_sm[ci][k_] for ci in cis for k_ in rnxt_k]
                            if p4 % 2 == 0:
                                OP('act', 'activation', [R_ps[bC]], wr, out=SM[:, cis[0]:cis[0] + 2, no:no + 256], in_=psv(bC, 2, 256), func=AF.Copy)
                            else:
                                OP('dve', 'tensor_copy', [R_ps[bC]], wr, out=SM[:, cis[0]:cis[0] + 2, no:no + 256], in_=psv(bC, 2, 256))
                        tc_o, tck = (896, 4) if lvl % 2 == 0 else (1024, 5)
                        tn_o, tnk = (1024, 5) if lvl % 2 == 0 else (896, 4)
                        for p2 in range(2):
                            cis = range(4 * p2, 4 * p2 + 4)
                            bD = nb()
                            for q, ci in enumerate(cis):
                                MM(ps[bD][:, q * 128:(q + 1) * 128], smv[ci][:, no:no + 128], smv[ci][:, tc_o:tc_o + 128], True, True,
                                   [R_sm[ci][k_] for k_ in rnxt_k] + [R_sm[ci][tck]], [R_ps[bD]])
                            OP('dve', 'tensor_tensor', [R_ps[bD]] + RS(tck, cis), RS(tnk, cis), out=SM[:, cis[0]:cis[0] + 4, tn_o:tn_o + 128], in0=psv(bD, 4, 128),
                               in1=SM[:, cis[0]:cis[0] + 4, tc_o:tc_o + 128], op=ALU.add)
                        rpump()
                    CK(3.2)
                    for p2 in range(2):
                        cis = range(4 * p2, 4 * p2 + 4)
                        bF = nb()
                        for q, ci in enumerate(cis):
                            hh, blk = chains[ci]
                            MM(ps[bF][:, q * 128:q * 128 + 64], smv[ci][:, 1024:1152], Atok[:, blk, hh * 64:(hh + 1) * 64], True, True, [R_sm[ci][5], R_pc[P_TOK]], [R_ps[bF]])
                            MM(ps[bF][:, q * 128 + 64:q * 128 + 128], smv[ci][:, 1024:1152], smv[ci][:, 1152:1216], True, True, [R_sm[ci][5], R_sm[ci][7]], [R_ps[bF]])
                        OP('act', 'activation', [R_ps[bF]], RS(7, cis), out=SM[:, cis[0]:cis[0] + 4, 1152:1280], in_=psv(bF, 4, 128), func=AF.Copy)
                    rpump()
                    CK(3.3)
                    bG = nb()
                    for ci, (hh, blk) in enumerate(chains):
                        pb = 64 * hh
                        MM(ps[bG][pb:pb + 64, blk * 128:(blk + 1) * 128], smv[ci][:, 1152:1216], smv[ci][:, 256:384], True, True, [R_sm[ci][7], R_sm[ci][0]], [R_ps[bG]])
                    for hh in range(2):
                        pb = 64 * hh
                        OP('dve', 'tensor_tensor', [R_ps[bG], R_B[5]], RS(1, range(4 * hh, 4 * hh + 4)), out=SM[pb:pb + 64, 4 * hh:4 * hh + 4, 384:512],
                           in0=ps[bG][pb:pb + 64, 0:512].rearrange('p (c n) -> p c n', n=128), in1=arT[pb:pb + 64, 0:4, 1, :], op=ALU.add)
                    for c in range(2):
                        cs = slice(c * 64, (c + 1) * 64)
                        bH = nb()
                        for ci, (hh, blk) in enumerate(chains):
                            pb = 64 * hh
                            MM(ps[bH][pb:pb + 64, blk * 64:(blk + 1) * 64], smv[ci][cs, 1152:1216], Bhtok[cs, blk, hh * 64:(hh + 1) * 64],
                               True, True, [R_sm[ci][7], R_pc[P_TOK + 1]], [R_ps[bH]])
                        for ci, (hh, blk) in enumerate(chains):
                            pb = 64 * hh
                            OP('dve', 'scalar_tensor_tensor', [R_ps[bH], R_c32, R_wcbj[j]], [R_sm[ci][9]], out=smv[ci][pb:pb + 64, 1280 + c * 64:1280 + (c + 1) * 64],
                               in0=c32[pb:pb + 64, CI + pb:CI + pb + 64], scalar=wcb[pb:pb + 64, j, blk * 2 + c:blk * 2 + c + 1],
                               in1=ps[bH][pb:pb + 64, blk * 64:(blk + 1) * 64], op0=ALU.mult, op1=ALU.add)
                        bN = nb()
                        for ci, (hh, blk) in enumerate(chains):
                            pb = 64 * hh
                            MM(ps[bN][pb:pb + 64, blk * 64:(blk + 1) * 64], Bhtok[cs, blk, hh * 64:(hh + 1) * 64], smv[ci][cs, 1216:1280], True, False,
                               [R_pc[P_TOK + 1], R_sm[ci][7]], [R_ps[bN]])
                            MM(ps[bN][pb:pb + 64, blk * 64:(blk + 1) * 64], Khtok[cs, blk, hh * 64:(hh + 1) * 64], Vtok[cs, blk, hh * 64:(hh + 1) * 64], False, True,
                               [R_pc[P_TOK + 1], R_pc[P_TOK]], [R_ps[bN]])
                        for hh in range(2):
                            pb = 64 * hh
                            OP('act', 'activation', [R_ps[bN]], RS(2, range(4 * hh, 4 * hh + 4)), out=SM[pb:pb + 64, 4 * hh:4 * hh + 4, 640 + c * 64:640 + (c + 1) * 64],
                               in_=ps[bN][pb:pb + 64, 0:256].rearrange('p (c n) -> p c n', n=64), func=AF.Copy)
                    rpump()
                    CK(3.4)
                    for ci, (hh, blk), c in [(hh_ * 4 + blk_, (hh_, blk_), c_) for blk_ in range(4) for c_ in range(2) for hh_ in range(2)]:
                        pb = 64 * hh
                        s = smv[ci]
                        if True:
                            par = st_par[j][hh]
                            st_cur, r_cur = ST[pb:pb + 64, par, j, :], R_ST[par][j][hh]
                            st_new, r_new = ST[pb:pb + 64, 1 - par, j, :], R_ST[1 - par][j][hh]
                            st_par[j][hh] = 1 - par
                            bS = nb()
                            so = ps[bS][pb:pb + 64, 0:64]
                            MM(so, s[pb:pb + 64, 1280 + c * 64:1280 + (c + 1) * 64], st_cur, True, False, [R_sm[ci][9], r_cur], [R_ps[bS]])
                            MM(so, identb[pb:pb + 64, pb:pb + 64], s[pb:pb + 64, 640 + c * 64:640 + (c + 1) * 64], False, True, [R_cb, R_sm[ci][2]], [R_ps[bS]])
                            OP('act', 'activation', [R_ps[bS]], [r_new], out=st_new, in_=so, func=AF.Copy)
                            yo = ps[bY][pb:pb + 64, blk * 128 + c * 64:blk * 128 + (c + 1) * 64]
                            MM(yo, st_cur, s[pb:pb + 64, 384 + c * 64:384 + (c + 1) * 64], True, False, [r_cur, R_sm[ci][1]], [R_ps[bY]])
                            MM(yo, s[:, 1216:1280], s[:, 256 + c * 64:256 + (c + 1) * 64], False, False, [R_sm[ci][7], R_sm[ci][0]], [R_ps[bY]])
                            MM(yo, Vtok[:, blk, hh * 64:(hh + 1) * 64], s[:, 512 + c * 64:512 + (c + 1) * 64], False, True, [R_pc[P_TOK], R_sm[ci][12]], [R_ps[bY]])
                        if (blk * 2 + c) % 2 == 1 and hh == 1:
                            rpump(0.5, 1)

                    rpump()
                while next(gA, 'done') != 'done':
                    pump(WP)
                CK(4)
                auto_pump['w'] = 1.6 * WP
                ysb, Ry = pc(P_PRW + j), R_pc[P_PRW + j]
                tA, RtA = pc(P_PRW + 4 + j), R_pc[P_PRW + 4 + j]
                OP('act', 'activation', [R_ps[bY]], [Ry], out=ysb, in_=ps[bY][:, :], func=AF.Copy)
                ybf_ = pcb(P_B + 4)[:, 512:1024]
                OP('dve', 'tensor_copy', [Ry], [R_B[4]], out=ybf_, in_=ysb)
                OP('act', 'activation', [Ry], [R_kk2], out=kk2, in_=ysb, func=AF.Square)
                bm, bv_ = nb(), nb()
                MM(ps[bm][:, :], blk64, ybf_, True, True, [R_cb, R_B[4]], [R_ps[bm]])
                MM(ps[bv_][:, :], blk64, kk2, True, True, [R_cb, R_kk2], [R_ps[bv_]])
                bb_ = nb()
                MM(ps[bb_][:, :], blk1, rk, True, True, [R_cb, R_rk], [R_ps[bb_]])
                OP('act', 'activation', [R_ps[bm]], [RtA], out=tA, in_=ps[bm][:, :], func=AF.Square)
                OP('dve', 'tensor_tensor', [R_ps[bv_], RtA], [RtA], out=tA, in0=ps[bv_][:, :], in1=tA, op=ALU.subtract)
                OP('act', 'activation', [RtA, R_dv], [RtA], out=tA, in_=tA, func=AF.Ln, bias=eps_gn, scale=1.0)
                OP('dve', 'tensor_tensor', [Ry, R_ps[bm]], [Ry], out=ysb, in0=ysb, in1=ps[bm][:, :], op=ALU.subtract)
                OP('act', 'activation', [RtA], [RtA], out=tA, in_=tA, func=AF.Exp, scale=-0.5)
                OP('dve', 'tensor_tensor', [R_ps[bb_], RVr], [RVr], out=Vr, in0=ps[bb_][:, :], in1=Vr, op=ALU.mult)
                OP('dve', 'scalar_tensor_tensor', [Ry, RtA, R_vec], [Ry], out=ysb, in0=ysb, scalar=vec[:, 42 + j:43 + j], in1=tA, op0=ALU.mult, op1=ALU.mult)
                OP('dve', 'scalar_tensor_tensor', [Ry, RVr, R_vec], [Ry], out=ysb, in0=ysb, scalar=vec[:, 46 + j:47 + j], in1=Vr, op0=ALU.add, op1=ALU.add)
                bg = nb()
                MM(ps[bg][:, :], gupb[:, j * 128:(j + 1) * 128], sgd, True, True, [R_gup, R_pc[P_B]], [R_ps[bg]])
                OP('dve', 'tensor_tensor', [R_ps[bg], Ry], [R_abf[j]], out=abf[:, j, :], in0=ps[bg][:, :], in1=ysb, op=ALU.mult)

            CK(5)
            auto_pump['w'] = 0.0
            auto_pump['fn'] = None
            for _ in ag:
                pass
            bank_pool[:] = list(range(7))

            CK(6)
            def proj8(seq_base, src_reads_k, rhs_k, evac):
                for c in range(8):
                    slot = use_panel(seq_base + c // 4)
                    pos = c % 4
                    b = nb()
                    for k in range(8):
                        MM(ps[b][:, :], ring[:, slot, k * 512 + pos * 128:k * 512 + pos * 128 + 128], rhs_k(k), k == 0, k == 7,
                           [R_ring[slot]] + src_reads_k(k), [R_ps[b]])
                    evac(c, b)

            h = [pc(i) for i in range(8)]
            Rh = [R_pc[i] for i in range(8)]
            for c in range(8):
                DMA('sp', f'xh{c}', h[c], xT[c * 128:(c + 1) * 128, tsl], [], [Rh[c]])

            def ev_res(c, b):
                OP('dve', 'tensor_tensor', [R_ps[b], Rh[c]], [Rh[c]], out=h[c], in0=ps[b][:, :], in1=h[c], op=ALU.add)
            proj8(seq0 + 7, lambda k: [R_abf[k]], lambda k: abf[:, k, :], ev_res)

            CK(7)
            P_SQ2, P_RS2 = 30, 31

            def norm_cast(gcol):
                rs = rms_stats(lambda k: (h[k], [Rh[k]]), None, P_SQ2, P_RS2, nb())
                for k in range(8):
                    if k % 2 == 0:
                        OP('dve', 'tensor_scalar', [Rh[k], R_vec], [R_abf[k]], out=abf[:, k, :], in0=h[k],
                           scalar1=vec[:, gcol + k:gcol + k + 1], scalar2=None, op0=ALU.mult)
                    else:
                        OP('act', 'activation', [Rh[k], R_vec], [R_abf[k]], out=abf[:, k, :], in_=h[k], func=AF.Copy, scale=vec[:, gcol + k:gcol + k + 1])
                return rs
            rs2 = norm_cast(51)
            qm = pcb(24, 4).rearrange('p (c t) -> p c t', t=512)
            Rqm = Rp(24, 4)

            def ev_qm(c, b):
                OP('dve', 'scalar_tensor_tensor', [R_ps[b], R_pc[P_RS2]], Rqm, out=qm[:, c, :], in0=ps[b][:, :], scalar=1.0 / 16, in1=rs2,
                   op0=ALU.mult, op1=ALU.mult)
            proj8(seq0 + 9, lambda k: [R_abf[k]], lambda k: abf[:, k, :], ev_qm)

            CK(7.3)
            pm = pcb(28, 2).rearrange('p (m t) -> p m t', t=512)
            for hd in range(4):
                for mb in range(2):
                    b = nb()
                    for dc in range(2):
                        MM(ps[b][:, :], KmT[:, hd * 2 + dc, mb * 128:(mb + 1) * 128], qm[:, hd * 2 + dc, :], dc == 0, dc == 1, [R_KmT] + Rqm, [R_ps[b]])
                    OP('act', 'activation', [R_ps[b]], [R_pc[28 + mb // 2]], out=pm[:, mb, :], in_=ps[b][:, :], func=AF.Exp)
                bl = nb()
                for mb in range(2):
                    MM(ps[bl][:, :], ones1, pm[:, mb, :], mb == 0, mb == 1, [R_cb, R_pc[28]], [R_ps[bl]])
                rlm = pc(29)
                OP('act', 'activation', [R_ps[bl]], [R_pc[29]], out=rlm, in_=ps[bl][:, :], func=AF.Ln)
                OP('act', 'activation', [R_pc[29]], [R_pc[29]], out=rlm, in_=rlm, func=AF.Exp, scale=-1.0)
                for dc in range(2):
                    b = nb()
                    for mb in range(2):
                        MM(ps[b][:, :], Vm[:, mb, (hd * 2 + dc) * 128:(hd * 2 + dc + 1) * 128], pm[:, mb, :], mb == 0, mb == 1, [R_Vm, R_pc[28]], [R_ps[b]])
                    OP('dve', 'tensor_tensor', [R_ps[b], R_pc[29]], [R_abf[hd * 2 + dc]], out=abf[:, hd * 2 + dc, :], in0=ps[b][:, :], in1=rlm, op=ALU.mult)

            CK(7.6)
            proj8(seq0 + 11, lambda k: [R_abf[k]], lambda k: abf[:, k, :], ev_res)

            CK(8)
            rs3 = norm_cast(67)
            ubf = pcb(8, 16).rearrange('p (f t) -> p f t', t=512)
            for fo in range(32):
                slot = use_panel(seq0 + 13 + fo // 4)
                pos = fo % 4
                b = nb()
                for k in range(8):
                    MM(ps[b][:, :], ring[:, slot, k * 512 + pos * 128:k * 512 + pos * 128 + 128], abf[:, k, :], k == 0, k == 7,
                       [R_ring[slot], R_abf[k]], [R_ps[b]])
                tr = pc(28 + fo % 2)
                OP('dve', 'scalar_tensor_tensor', [R_ps[b], R_pc[P_RS2]], [R_pc[28 + fo % 2]], out=tr, in0=ps[b][:, :], scalar=0.0, in1=rs3,
                   op0=ALU.max, op1=ALU.mult)
                OP('act', 'activation', [R_pc[28 + fo % 2]], [R_pc[8 + fo // 2]], out=ubf[:, fo, :], in_=tr, func=AF.Square)
            if tau + 1 < ntiles:
                load_x_tile(tau + 1)
            CK(8.5)
            for c in range(8):
                slot = use_panel(seq0 + 21 + c)
                b = nb()
                for fo in range(32):
                    MM(ps[b][:, :], ring[:, slot, fo * 128:(fo + 1) * 128], ubf[:, fo, :], fo == 0, fo == 31, [R_ring[slot], R_pc[8 + fo // 2]], [R_ps[b]])
                ev_res(c, b)
            CK(9)
            rs4 = rms_stats(lambda k: (h[k], [Rh[k]]), None, P_SQ2, P_RS2, nb())
            for c in range(8):
                OP('dve', 'scalar_tensor_tensor', [Rh[c], R_vec, R_pc[P_RS2]], [Rh[c]], out=h[c], in0=h[c],
                   scalar=vec[:, 75 + c:76 + c], in1=rs4, op0=ALU.mult, op1=ALU.mult)
                DMA('sp', f'st{c}', outT[c * 128:(c + 1) * 128, tsl], h[c], [Rh[c]], [R_out])

        try:
            CK(0)
            load_x_tile(0)
            for tau in range(ntiles):
                tile_body(tau)
        except _Stop:
            pass

        for key in [f'st{c}' for c in range(8)]:
            if key in T.dma_cnt:
                T.streams['sp'].append(('wait', 'dma:' + key, T.dma_cnt[key] * 16))
        block = es.enter_context(nc.Block())
        T.emit(nc, block, es)
    return nc


def _panels(W, pw):
    K, N = W.shape
    kc = K // 128
    out = []
    for c0 in range(0, N, pw):
        blk = W[:, c0:c0 + pw].reshape(kc, 128, pw).transpose(1, 0, 2).reshape(128, kc * pw)
        out.append(blk)
    return out


def _prep_shared(inp):
    f = lambda k: np.asarray(inp[k], dtype=np.float32)
    w_in = f('w_in')[0]
    zpad = np.zeros((1024, 256), np.float32)
    w_in_perm = np.concatenate([w_in[:, :2560], w_in[:, 2560:2816], zpad, w_in[:, 2816:3328]], axis=1)
    pans = _panels(w_in_perm, 512)
    for k in ('w_out', 'w_mq', 'w_mo'):
        pans += _panels(f(k)[0], 512)
    pans += _panels(f('w_up')[0], 512)
    pans += _panels(f('w_down')[0], 128)
    wpan = np.ascontiguousarray(np.stack(pans, 0))
    assert wpan.shape == (NPAN, 128, 4096)
    wmkv = np.stack([_panels(f('w_mk')[0], 1024)[0], _panels(f('w_mv')[0], 1024)[0]], 0)
    wlora = np.zeros((128, 1024), np.float32)
    wlora[0:64, 0:512] = f('w_decay_up')[0]
    wlora[64:128, 0:512] = f('a_up')[0]
    wlora[:, 512:1024] = f('g_up')[0]
    vec = np.zeros((128, NV), np.float32)
    col = lambda v: np.asarray(v, np.float32).reshape(-1, 128).T
    vec[:, 0:8] = col(f('norm_mix_w')[0])
    vec[:, 8:22] = col(f('mu_shift')[0])
    vec[:, 22:26] = col(f('w_decay0')[0])
    vec[:, 26:30] = col(f('a0')[0])
    vec[:, 30:34] = col(f('k_k')[0])
    vec[:, 34:38] = col(f('k_a')[0])
    vec[:, 38:42] = col(f('r_k')[0].reshape(-1))
    vec[:, 42:46] = col(f('lnx_w')[0])
    vec[:, 46:50] = col(f('lnx_b')[0])
    vec[:, 50] = f('subln_w')[0]
    vec[:, 51:59] = col(f('norm_mem_w')[0])
    vec[:, 59:67] = col(f('norm_src_w')[0])
    vec[:, 67:75] = col(f('norm_mlp_w')[0])
    vec[:, 75:83] = col(f('norm_final_w'))
    for i, k in enumerate(('lam_q1', 'lam_k1', 'lam_q2', 'lam_k2')):
        vec[0:64, 83 + i] = f(k)[0]
    cst = np.zeros((128, NCST), np.float32)
    idx = np.arange(128)
    same = (idx[:, None] // 64) == (idx[None, :] // 64)
    mLT = ((idx[:, None] < idx[None, :]) & same).astype(np.float32)
    mInc = ((idx[:, None] <= idx[None, :]) & same).astype(np.float32)
    cst[:, CI:CI + 128] = np.eye(128, dtype=np.float32)
    cst[:, CM5:CM5 + 640] = np.concatenate([mLT.T, mLT, mInc, mLT, mInc], axis=1)
    rs = np.ones(512, np.float32)
    rs[::64] = 0.0
    cst[:, CRS:CRS + 512] = rs[None, :]
    cst[:, CBO:CBO + 128] = same.astype(np.float32)
    return dict(wpan=wpan, wmkv=np.ascontiguousarray(wmkv), wlora=wlora, vecs=vec, cst=cst)


_NC_CACHE = {}
MARKS = []


def kernel(**inputs):
    x = np.asarray(inputs['x'], dtype=np.float32)
    mem = np.asarray(inputs['mem'], dtype=np.float32)
    shared = _prep_shared(inputs)
    B = x.shape[0]
    in_maps = []
    for b in range(B):
        m = dict(shared)
        m['xT'] = np.ascontiguousarray(x[b].T)
        m['memT'] = np.ascontiguousarray(mem[b].T)
        in_maps.append(m)
    if 'nc' not in _NC_CACHE:
        _NC_CACHE['nc'] = build_nc()
    nc = _NC_CACHE['nc']
    res = run_bass_kernel_spmd(nc, in_maps, core_ids=list(range(B)))
    out = np.stack([np.ascontiguousarray(res.results[b]['outT'].T) for b in range(B)], 0)
    return out.astype(np.float32)
```
